# Optimizing a Trainium2 kernel written in Bass

```python
import jax, jax.numpy as jnp
from jax import lax
import numpy as np

D_MODEL = 1024
BATCH = 8
SEQ = 2048
DEPTH = 2

N_MIXERS = 2
GMLP_CHUNK = 128
GMLP_WIDTH = 2 * D_MODEL
GMLP_GROUPS = 8
GMLP_GROUP_DIM = GMLP_WIDTH // GMLP_GROUPS
N_HEADS = 8
HEAD_DIM = D_MODEL // N_HEADS
MOBA_BLOCK = 256
MOBA_TOPK = 3
MOBA_QCHUNK = 16
ROPE_THETA = 10000.0
D_FF = 2816
CONV_WIDTH = 3

EPS = 1e-6
NEG_INF = -1e30
N_A_LAYERS = (DEPTH + 1) // 2
N_B_LAYERS = DEPTH // 2

kernel_name = "hybrid_gmlp_moba_convffn"


def rms_norm(x, g):
    xf = x.astype(jnp.float32)
    y = xf * lax.rsqrt(jnp.mean(xf * xf, axis=-1, keepdims=True) + EPS)
    return (y * g.astype(jnp.float32)).astype(x.dtype)


def rope(x, positions):
    half = HEAD_DIM // 2
    inv = ROPE_THETA ** (-jnp.arange(half, dtype=jnp.float32) / half)
    ang = positions.astype(jnp.float32)[:, None] * inv[None, :]
    cos = jnp.cos(ang)[None, :, None, :]
    sin = jnp.sin(ang)[None, :, None, :]
    xf = x.astype(jnp.float32)
    x1, x2 = xf[..., :half], xf[..., half:]
    return jnp.concatenate([x1 * cos - x2 * sin, x2 * cos + x1 * sin], axis=-1).astype(x.dtype)


def gmlp_mixer(h, w_in, v_gain, w_s, b_s, w_out):
    B, S, _ = h.shape
    z = jax.nn.gelu(h @ w_in)
    u, v = jnp.split(z, 2, axis=-1)
    v = rms_norm(v, v_gain)
    v = v.reshape(B, S // GMLP_CHUNK, GMLP_CHUNK, GMLP_GROUPS, GMLP_GROUP_DIM)
    causal = jnp.tril(jnp.ones((GMLP_CHUNK, GMLP_CHUNK), dtype=bool))
    ws = jnp.where(causal[None], w_s, jnp.zeros_like(w_s)).astype(v.dtype)
    s = jnp.einsum('gts,bcsgd->bctgd', ws, v) + b_s.T[None, None, :, :, None].astype(v.dtype)
    return (u * s.reshape(B, S, GMLP_WIDTH)) @ w_out


def moba_mixer(h, w_qkv, w_o):
    B, S, _ = h.shape
    qkv = (h @ w_qkv).reshape(B, S, 3, N_HEADS, HEAD_DIM)
    pos = jnp.arange(S)
    q = rope(qkv[:, :, 0], pos)
    k = rope(qkv[:, :, 1], pos)
    v = qkv[:, :, 2]
    pad = (-S) % MOBA_BLOCK
    Sp = S + pad
    NB = Sp // MOBA_BLOCK
    q, k, v = [jnp.pad(t, ((0, 0), (0, pad), (0, 0), (0, 0))).transpose(0, 2, 1, 3) for t in (q, k, v)]
    kb = k.reshape(B, N_HEADS, NB, MOBA_BLOCK, HEAD_DIM)
    vb = v.reshape(B, N_HEADS, NB, MOBA_BLOCK, HEAD_DIM)
    q_blk = jnp.arange(Sp) // MOBA_BLOCK
    n_sel = min(MOBA_TOPK, NB - 1)
    NQ = Sp // MOBA_QCHUNK
    scale = HEAD_DIM ** -0.5
    q_c = q.reshape(B, N_HEADS, NQ, MOBA_QCHUNK, HEAD_DIM).transpose(2, 0, 1, 3, 4)
    chunk_ids = jnp.arange(NQ)

    if n_sel > 0:
        k_mean = jnp.mean(kb.astype(jnp.float32), axis=3)
        gate = jnp.einsum('bhsd,bhnd->bhsn', q.astype(jnp.float32), k_mean)
        past = jnp.arange(NB)[None, :] < q_blk[:, None]
        gate = jnp.where(past[None, None], gate, NEG_INF)
        _, sel = lax.top_k(gate, n_sel)
        sel_valid = jnp.arange(n_sel)[None, :] < q_blk[:, None]
        sel_c = sel.reshape(B, N_HEADS, NQ, MOBA_QCHUNK, n_sel).transpose(2, 0, 1, 3, 4)
        valid_c = sel_valid.reshape(NQ, MOBA_QCHUNK, n_sel)
    else:
        sel_c = jnp.zeros((NQ, B, N_HEADS, MOBA_QCHUNK, 0), dtype=jnp.int32)
        valid_c = jnp.zeros((NQ, MOBA_QCHUNK, 0), dtype=bool)

    bi = jnp.arange(B)[:, None, None, None]
    hi = jnp.arange(N_HEADS)[None, :, None, None]

    def attend(args):
        qc, selc, validc, c = args
        blk = (c * MOBA_QCHUNK) // MOBA_BLOCK
        qpos = c * MOBA_QCHUNK + jnp.arange(MOBA_QCHUNK)
        kpos = blk * MOBA_BLOCK + jnp.arange(MOBA_BLOCK)
        k_own = lax.dynamic_index_in_dim(kb, blk, axis=2, keepdims=False)
        v_own = lax.dynamic_index_in_dim(vb, blk, axis=2, keepdims=False)
        s_own = jnp.einsum('bhqd,bhkd->bhqk', qc, k_own).astype(jnp.float32) * scale
        s_own = jnp.where((kpos[None, :] <= qpos[:, None])[None, None], s_own, NEG_INF)
        if n_sel > 0:
            k_sel = kb[bi, hi, selc]
            v_sel = vb[bi, hi, selc]
            s_sel = jnp.einsum('bhqd,bhqrkd->bhqrk', qc, k_sel).astype(jnp.float32) * scale
            s_sel = jnp.where(validc[None, None, :, :, None], s_sel, NEG_INF)
            s_sel = s_sel.reshape(B, N_HEADS, MOBA_QCHUNK, n_sel * MOBA_BLOCK)
            p = jax.nn.softmax(jnp.concatenate([s_sel, s_own], axis=-1), axis=-1).astype(qc.dtype)
            p_sel = p[..., :n_sel * MOBA_BLOCK].reshape(B, N_HEADS, MOBA_QCHUNK, n_sel, MOBA_BLOCK)
            p_own = p[..., n_sel * MOBA_BLOCK:]
            return (jnp.einsum('bhqrk,bhqrkd->bhqd', p_sel, v_sel)
                    + jnp.einsum('bhqk,bhkd->bhqd', p_own, v_own))
        p_own = jax.nn.softmax(s_own, axis=-1).astype(qc.dtype)
        return jnp.einsum('bhqk,bhkd->bhqd', p_own, v_own)

    out = lax.map(attend, (q_c, sel_c, valid_c, chunk_ids))
    out = out.transpose(1, 0, 3, 2, 4).reshape(B, Sp, D_MODEL)[:, :S]
    return out @ w_o


def conv_ffn(h, w_up, conv_w, conv_b, w_down):
    a = h @ w_up
    a = lax.conv_general_dilated(
        a, conv_w[:, None, :].astype(a.dtype), window_strides=(1,),
        padding=[(CONV_WIDTH - 1, 0)], dimension_numbers=('NWC', 'WIO', 'NWC'),
        feature_group_count=2 * D_FF) + conv_b
    g, up = jnp.split(a, 2, axis=-1)
    return (jax.nn.gelu(g) * up) @ w_down


def setup_inputs(seed: int = 0) -> dict:
    key = jax.random.key(seed)
    ks = jax.random.split(key, 16)
    f32 = jnp.float32
    nrm = lambda k, shape, s: jax.random.normal(k, shape, f32) * s
    return {
        "x": nrm(ks[0], (BATCH, SEQ, D_MODEL), 1.0),
        "mix_norm": 1.0 + nrm(ks[1], (DEPTH, D_MODEL), 0.05),
        "a_w_in": nrm(ks[2], (N_A_LAYERS, D_MODEL, 2 * GMLP_WIDTH), D_MODEL ** -0.5),
        "a_v_gain": 1.0 + nrm(ks[3], (N_A_LAYERS, GMLP_WIDTH), 0.05),
        "a_w_s": nrm(ks[4], (N_A_LAYERS, GMLP_GROUPS, GMLP_CHUNK, GMLP_CHUNK), GMLP_CHUNK ** -0.5),
        "a_b_s": 1.0 + nrm(ks[5], (N_A_LAYERS, GMLP_GROUPS, GMLP_CHUNK), 0.1),
        "a_w_out": nrm(ks[6], (N_A_LAYERS, GMLP_WIDTH, D_MODEL), GMLP_WIDTH ** -0.5),
        "b_w_qkv": nrm(ks[7], (N_B_LAYERS, D_MODEL, 3 * D_MODEL), D_MODEL ** -0.5),
        "b_w_o": nrm(ks[8], (N_B_LAYERS, D_MODEL, D_MODEL), D_MODEL ** -0.5),
        "ffn_norm": 1.0 + nrm(ks[9], (DEPTH, D_MODEL), 0.05),
        "ffn_w_up": nrm(ks[10], (DEPTH, D_MODEL, 2 * D_FF), D_MODEL ** -0.5),
        "ffn_conv_w": nrm(ks[11], (DEPTH, CONV_WIDTH, 2 * D_FF), CONV_WIDTH ** -0.5),
        "ffn_conv_b": nrm(ks[12], (DEPTH, 2 * D_FF), 0.02),
        "ffn_w_down": nrm(ks[13], (DEPTH, D_FF, D_MODEL), D_FF ** -0.5),
        "final_norm": 1.0 + nrm(ks[14], (D_MODEL,), 0.05),
    }


def reference(x, mix_norm, a_w_in, a_v_gain, a_w_s, a_b_s, a_w_out, b_w_qkv, b_w_o,
              ffn_norm, ffn_w_up, ffn_conv_w, ffn_conv_b, ffn_w_down, final_norm):
    h = x
    for i in range(DEPTH):
        y = rms_norm(h, mix_norm[i])
        j = i // N_MIXERS
        if i % N_MIXERS == 0:
            h = h + gmlp_mixer(y, a_w_in[j], a_v_gain[j], a_w_s[j], a_b_s[j], a_w_out[j])
        else:
            h = h + moba_mixer(y, b_w_qkv[j], b_w_o[j])
        h = h + conv_ffn(rms_norm(h, ffn_norm[i]), ffn_w_up[i], ffn_conv_w[i], ffn_conv_b[i], ffn_w_down[i])
    return rms_norm(h, final_norm)
```

```python
import os
from contextlib import ExitStack

import numpy as np
import concourse.bass as bass
import concourse.mybir as mybir
from concourse.bass_utils import run_bass_kernel_spmd

F32 = mybir.dt.float32
BF16 = mybir.dt.bfloat16
I32 = mybir.dt.int32
AF = mybir.ActivationFunctionType
ALU = mybir.AluOpType
AX = mybir.AxisListType

D = 1024
SEQ = 2048
NB = 8
DFF = 2816
EPS = 1e-6
ROPE_THETA = 10000.0
NEGM = 30000.0

C_GAIN = 0
C_VG = 40
C_CW = 56
C_CB = C_CW + 264
C_WS = C_CB + 88
C_BB = C_WS + 1024
NCONST = C_BB + 1024


class Buf:
    __slots__ = ("name", "w", "r", "ndma", "sem")

    def __init__(self, name):
        self.name = name
        self.w = None
        self.r = {}
        self.ndma = 0
        self.sem = None


class Op:
    __slots__ = ("eng", "fn", "deps", "dmadeps", "marked", "seq", "dma_buf", "dma_cnt", "pos")

    def __init__(self, eng, fn):
        self.eng = eng
        self.fn = fn
        self.deps = []
        self.dmadeps = {}
        self.marked = False
        self.seq = 0
        self.dma_buf = None
        self.dma_cnt = 0


class Sched:
    ENGS = ("pe", "act", "dve", "pool", "sp")

    def __init__(self):
        self.q = {e: [] for e in self.ENGS}
        self.dma_bufs = []
        self.bar = None
        self.bar_done = set()

    def _dep(self, op, prev):
        if prev is None or prev is op:
            return
        if prev.dma_buf is not None:
            b = prev.dma_buf
            op.dmadeps[b] = max(op.dmadeps.get(b, 0), b.ndma)
            return
        if prev.eng == "pe" and op.eng == "pe":
            return
        op.deps.append(prev)
        prev.marked = True

    def barrier(self):
        self.bar = [self.q[e][-1] for e in self.ENGS if self.q[e]]
        self.bar_done = set()

    def op(self, eng, fn, reads=(), writes=(), dma_to=None, nobarrier=False):
        o = Op(eng, fn)
        if self.bar is not None and not nobarrier and eng not in self.bar_done:
            for p in self.bar:
                self._dep(o, p)
            self.bar_done.add(eng)
        for b in reads:
            self._dep(o, b.w)
        for b in writes:
            self._dep(o, b.w)
            for r in b.r.values():
                self._dep(o, r)
        key = eng if dma_to is None else ("dma", dma_to.name, len(self.q[eng]))
        for b in reads:
            b.r[key] = o
        for b in writes:
            b.w = o
            b.r = {}
        if dma_to is not None:
            dma_to.ndma += 1
            o.dma_buf = dma_to
            o.dma_cnt = dma_to.ndma
            if dma_to not in self.dma_bufs:
                self.dma_bufs.append(dma_to)
        o.pos = len(self.q[eng])
        self.q[eng].append(o)
        return o

    def emit(self, nc, final_waits=()):
        with ExitStack() as es:
            sems = {e: es.enter_context(nc.semaphore("s_" + e)) for e in ("pe", "act", "dve", "pool")}
            for b in self.dma_bufs:
                b.sem = es.enter_context(nc.semaphore("d_" + b.name))
            for e in self.ENGS:
                c = 0
                for o in self.q[e]:
                    if o.marked and o.dma_buf is None:
                        c += 1
                        o.seq = c
            block = es.enter_context(nc.Block())
            sched = self

            def run(ename, eng):
                seen = {}
                for o in sched.q[ename]:
                    need = {}
                    for d in o.deps:
                        if need.get(d.eng, 0) < d.seq:
                            need[d.eng] = d.seq
                    for k, v in need.items():
                        if seen.get(k, 0) < v:
                            eng.wait_ge(sems[k], v)
                            seen[k] = v
                    for b, cnt in o.dmadeps.items():
                        if seen.get(b, 0) < cnt:
                            eng.wait_ge(b.sem, 16 * cnt)
                            seen[b] = cnt
                    ins = o.fn(eng)
                    if o.dma_buf is not None:
                        ins.then_inc(o.dma_buf.sem, 16)
                    elif o.marked:
                        ins.then_inc(sems[ename], 1)
                if ename == "sp":
                    for b in final_waits:
                        eng.wait_ge(b.sem, 16 * b.ndma)

            @block.tensor
            def _(eng):
                run("pe", eng)

            @block.scalar
            def _(eng):
                run("act", eng)

            @block.vector
            def _(eng):
                run("dve", eng)

            @block.gpsimd
            def _(eng):
                run("pool", eng)

            @block.sync
            def _(eng):
                run("sp", eng)


def slab_defs():
    d = {}
    for vb in range(4):
        d[f"Wv{vb}"] = 8 * 512
    for fc in range(16):
        d[f"Wu{fc}"] = 8 * 128
    for dc in range(8):
        d[f"Wo{dc}"] = 16 * 128
    for l in range(2):
        for jj in range(22):
            d[f"Wup{l}_{jj}"] = 8 * 2 * 128
        for G in range(2):
            for dc in range(8):
                d[f"Wd{l}_{G}_{dc}"] = 11 * 128
    for h in range(8):
        d[f"Wqk{h}"] = 8 * 2 * 128
        d[f"Wvh{h}"] = 8 * 128
    for hg in range(2):
        for dc in range(8):
            d[f"Wob{hg}_{dc}"] = 4 * 128
    return d


def slab_offsets():
    offs = {}
    o = 0
    for k, n in slab_defs().items():
        offs[k] = (o, n)
        o += n
    return offs, o


def attn_order():
    return ["p0", "p1", "a0", "p2", "a1", "p3", "a2", "p4", "a3", "w0", "p5", "a4", "p6", "a5", "p7", "a6", "a7", "w1"]


def stream_order(stop_after):
    seq = []
    for half in range(2):
        seq += [f"Wv{vb}" for vb in range(4)]
        seq += [f"Wu{fc}" for fc in range(16)]
        seq += [f"Wo{dc}" for dc in range(8)]
    if stop_after == "gmlp":
        return seq

    def ffn(l):
        s = []
        for G in range(2):
            s += [f"Wup{l}_{G * 11 + j}" for j in range(11)]
            s += [f"Wd{l}_{G}_{dc}" for dc in range(8)]
        return s

    seq += ffn(0)
    if stop_after == "ffn0":
        return seq
    for it in attn_order():
        k = int(it[1])
        if it[0] == "p":
            seq += [f"Wqk{k}", f"Wvh{k}"]
        elif it[0] == "w":
            seq += [f"Wob{k}_{dc}" for dc in range(8)]
    if stop_after == "attn":
        return seq
    seq += ffn(1)
    return seq


def build_nc(stop_after="final"):
    nc = bass.Bass("TRN2", target_bir_lowering=False)
    offs, WTOT = slab_offsets()
    xT_d = nc.dram_tensor("xT", [8, 128, SEQ], F32, kind="ExternalInput").ap()
    consts_d = nc.dram_tensor("consts", [128, NCONST], F32, kind="ExternalInput").ap()
    wts_d = nc.dram_tensor("wts", [128, WTOT], F32, kind="ExternalInput").ap()
    out_d = nc.dram_tensor("outT", [8, 128, SEQ], F32, kind="ExternalOutput").ap()

    S = Sched()
    sb_off = [16512]
    SB_LIMIT = 227328

    def alloc(name, shape, dt, at=None):
        esz = 2 if dt == BF16 else 4
        n = esz
        for s_ in shape[1:]:
            n *= s_
        off = sb_off[0] if at is None else at
        off = (off + 63) // 64 * 64
        assert off + n <= SB_LIMIT, (name, off, n)
        t = nc.alloc_sbuf_tensor_at(name, list(shape), dt, offset=off)
        if at is None:
            sb_off[0] = off + n
        return t

    hT = alloc("hT", [128, 8, SEQ], F32)
    yT = alloc("yT", [128, 8, SEQ], BF16)
    cst = alloc("cst", [128, NCONST], F32)
    ident = alloc("ident", [128, 128], BF16)
    ones = alloc("ones", [128, 128], BF16)
    epst = alloc("epst", [128, 1], F32)
    permT = alloc("permT", [128, 128], BF16)
    wsb = alloc("wsb", [128, 8, 128], BF16)
    wslots = [alloc(f"wslot{i}", [128, 4096], BF16) for i in range(3)]
    ARENA = sb_off[0]

    hB = [[Buf(f"h{c}_{t}") for t in range(4)] for c in range(8)]
    yB = [[Buf(f"y{c}_{t}") for t in range(4)] for c in range(8)]
    cstB, identB, onesB, epsB, wsbB = Buf("cst"), Buf("ident"), Buf("ones"), Buf("eps"), Buf("wsb")
    wslotB = [Buf(f"wslot{i}") for i in range(3)]
    outB = Buf("out")
    xinB = [Buf(f"xin{c}") for c in range(8)]
    aliasB = Buf("alias")

    pball = nc.alloc_psum_tensor("pball", [128, 4096], F32)
    pbank = [pball[:, i * 512:(i + 1) * 512] for i in range(8)]
    pbB = [Buf(f"pb{i}") for i in range(8)]
    rot = {"all": [0, list(range(8))], "proj": [0, [0, 1]], "proj4": [0, [0, 1, 2, 3]], "s": [0, [2, 3, 0]], "pv": [0, [4, 6]], "dn": [0, [5, 7]]}

    def bank(kind="all"):
        r = rot[kind]
        i = r[1][r[0] % len(r[1])]
        r[0] += 1
        return pbank[i], pbB[i]

    order = stream_order(stop_after)
    wstate = {"issued": 0, "taken": 0}

    def w_issue_upto(n):
        while wstate["issued"] < min(n, len(order)):
            i = wstate["issued"]
            name = order[i]
            o, ne = offs[name]
            t, B = wslots[i % 3], wslotB[i % 3]
            S.op("pool", lambda e, t=t, o=o, ne=ne: e.dma_start(out=t[:, 0:ne], in_=wts_d[:, o:o + ne]),
                 writes=[B], dma_to=B, nobarrier=True)
            wstate["issued"] += 1

    def wnext(name):
        i = wstate["taken"]
        assert order[i] == name, (order[i], name)
        w_issue_upto(i + 3)
        wstate["taken"] += 1
        return wslots[i % 3], wslotB[i % 3]

    def mm(out_ap, lhsT, rhs, start, stop, reads, writes, skip=False):
        S.op("pe", lambda e: e.matmul(out_ap, lhsT=lhsT, rhs=rhs, start=start, stop=stop, skip_group_check=skip),
             reads=reads, writes=writes)

    def tsl(tt):
        return slice(tt * 512, (tt + 1) * 512)

    def emit_tables(sinT, cosT, sinB, cosB, ktmp, kf, tpos, pcol, scr):
        S.op("pool", lambda e: e.iota(pcol[:, 0:1], pattern=[[0, 1]], base=0, channel_multiplier=1,
                                      allow_small_or_imprecise_dtypes=True), writes=[scr])
        S.op("dve", lambda e: e.tensor_single_scalar(out=pcol[:, 1:2], in_=pcol[:, 0:1], scalar=64.0, op=ALU.is_ge), reads=[scr], writes=[scr])
        S.op("dve", lambda e: e.scalar_tensor_tensor(out=pcol[:, 2:3], in0=pcol[:, 1:2], scalar=-64.0, in1=pcol[:, 0:1],
                                                     op0=ALU.mult, op1=ALU.add), reads=[scr], writes=[scr])
        S.op("dve", lambda e: e.tensor_scalar(out=pcol[:, 3:4], in0=pcol[:, 1:2], scalar1=2.0, scalar2=-1.0, op0=ALU.mult, op1=ALU.add),
             reads=[scr], writes=[scr])
        S.op("act", lambda e: e.activation(out=pcol[:, 2:3], in_=pcol[:, 2:3], func=AF.Exp, scale=-float(np.log(ROPE_THETA)) / 64.0),
             reads=[scr], writes=[scr])
        S.op("pool", lambda e: e.iota(tpos[:, :], pattern=[[1, SEQ]], base=0, channel_multiplier=0,
                                      allow_small_or_imprecise_dtypes=True), writes=[scr])
        C1 = 6.28125
        C2 = float(2 * np.pi - C1)

        def table(dst, dstB, shift, signed):
            S.op("dve", lambda e: e.tensor_scalar(out=dst[:, :], in0=tpos[:, :], scalar1=pcol[:, 2:3], scalar2=float(shift),
                                                  op0=ALU.mult, op1=ALU.add), reads=[scr], writes=[dstB])
            S.op("dve", lambda e: e.tensor_scalar(out=ktmp[:, :], in0=dst[:, :], scalar1=float(1.0 / (2 * np.pi)), scalar2=None,
                                                  op0=ALU.mult), reads=[dstB], writes=[scr])
            S.op("dve", lambda e: e.tensor_copy(out=kf[:, :], in_=ktmp[:, :]), reads=[scr], writes=[scr])
            S.op("dve", lambda e: e.scalar_tensor_tensor(out=dst[:, :], in0=kf[:, :], scalar=-C1, in1=dst[:, :],
                                                         op0=ALU.mult, op1=ALU.add), reads=[scr, dstB], writes=[dstB])
            S.op("dve", lambda e: e.scalar_tensor_tensor(out=dst[:, :], in0=kf[:, :], scalar=-C2, in1=dst[:, :],
                                                         op0=ALU.mult, op1=ALU.add), reads=[scr, dstB], writes=[dstB])
            S.op("dve", lambda e: e.tensor_scalar(out=dst[:, :], in0=dst[:, :], scalar1=-3.1415925, scalar2=3.1415925,
                                                  op0=ALU.max, op1=ALU.min), reads=[dstB], writes=[dstB])
            S.op("act", lambda e: e.activation(out=dst[:, :], in_=dst[:, :], func=AF.Sin), reads=[dstB], writes=[dstB])
            if signed:
                S.op("dve", lambda e: e.tensor_scalar(out=dst[:, :], in0=dst[:, :], scalar1=pcol[:, 3:4], scalar2=None, op0=ALU.mult),
                     reads=[dstB, scr], writes=[dstB])

        table(sinT, sinB, 0.0, True)
        table(cosT, cosB, float(np.pi / 2), False)

    S.op("sp", lambda e: e.dma_start(out=cst[:, :], in_=consts_d), writes=[cstB], dma_to=cstB)
    for tt in range(4):
        for cg in range(2):
            S.op("sp", lambda e, tt=tt, cg=cg: e.dma_start(
                out=hT[:, cg * 4:(cg + 1) * 4, tsl(tt)],
                in_=xT_d.rearrange("c p t -> p c t")[:, cg * 4:(cg + 1) * 4, tsl(tt)]),
                writes=[hB[c][tt] for c in range(cg * 4, (cg + 1) * 4)], dma_to=xinB[tt * 2 + cg])
    w_issue_upto(3)

    sb_off[0] = ARENA + 61440
    io_f = alloc("io_f", [128, 256], F32)
    ioB = Buf("io")
    S.op("pool", lambda e: e.iota(io_f[:, :], pattern=[[1, 256]], base=0, channel_multiplier=-1,
                                  allow_small_or_imprecise_dtypes=True), writes=[ioB])
    S.op("dve", lambda e: e.tensor_single_scalar(out=ident[:, :], in_=io_f[:, 0:128], scalar=0.0, op=ALU.is_equal),
         reads=[ioB], writes=[identB])
    S.op("dve", lambda e: e.memset(ones[:, :], 1.0), writes=[onesB])
    iosq = alloc("iosq", [128, 128], F32)
    iosqB, permB = Buf("iosq"), Buf("perm")
    S.op("dve", lambda e: e.tensor_tensor(out=iosq[:, :], in0=io_f[:, 0:128], in1=io_f[:, 0:128], op=ALU.mult), reads=[ioB], writes=[iosqB])
    S.op("dve", lambda e: e.tensor_single_scalar(out=permT[:, :], in_=iosq[:, :], scalar=4096.0, op=ALU.is_equal),
         reads=[iosqB], writes=[permB])
    S.op("dve", lambda e: e.memset(epst[:, :], EPS), writes=[epsB])
    msk = alloc("msk", [128, 128], F32)
    mskB = Buf("msk")
    S.op("dve", lambda e: e.tensor_single_scalar(out=msk[:, :], in_=io_f[:, 0:128], scalar=0.0, op=ALU.is_ge),
         reads=[ioB], writes=[mskB])
    S.op("dve", lambda e: e.tensor_tensor(
        out=wsb[:, :, :], in0=cst[:, C_WS:C_WS + 1024].rearrange("p (g t) -> p g t", g=8),
        in1=msk[:, :].unsqueeze(1).to_broadcast([128, 8, 128]), op=ALU.mult),
        reads=[cstB, mskB], writes=[wsbB])

    tab_d = nc.dram_tensor("ropetab", [2, 128, SEQ], F32, kind="Internal").ap()
    tabdB = Buf("tabd")
    if stop_after not in ("gmlp", "ffn0"):
        sb_off[0] = ARENA + 16384
        s_sin = alloc("s_sin", [128, SEQ], F32)
        s_cos = alloc("s_cos", [128, SEQ], F32)
        s_ktmp = alloc("s_ktmp", [128, SEQ], I32)
        s_kf = alloc("s_kf", [128, SEQ], F32)
        s_tpos = alloc("s_tpos", [128, SEQ], F32)
        s_pcol = alloc("s_pcol", [128, 4], F32)
        s_sinB, s_cosB, s_scr = Buf("s_sin"), Buf("s_cos"), Buf("s_scr")
        emit_tables(s_sin, s_cos, s_sinB, s_cosB, s_ktmp, s_kf, s_tpos, s_pcol, s_scr)
        S.op("sp", lambda e: e.dma_start(out=tab_d[0], in_=s_sin[:, :]), reads=[s_sinB], writes=[tabdB], dma_to=tabdB)
        S.op("sp", lambda e: e.dma_start(out=tab_d[1], in_=s_cos[:, :]), reads=[s_cosB], writes=[tabdB], dma_to=tabdB)

    def norm_phase(gi, final=False, pre=None, nobar=False):
        if not nobar:
            S.barrier()
        if pre is not None:
            pre()
        sb_off[0] = ARENA
        rs2 = alloc(f"rs{gi}", [128, 2, 512], F32)
        sqb = alloc(f"sqb{gi}", [128, 12, 512], BF16)
        rsB2 = [Buf(f"rs{gi}_{t}") for t in range(2)]
        rsB = [rsB2[t % 2] for t in range(4)]
        sqB = [Buf(f"sq{gi}_{i}") for i in range(12)]

        def rs_(tt):
            return rs2[:, tt % 2, :]
        sqi = [0, 0]
        def y_ops(tt):
            for c in range(8):
                gcol = cst[:, C_GAIN + gi * 8 + c:C_GAIN + gi * 8 + c + 1]
                if final:
                    S.op("dve", lambda e, c=c, gcol=gcol: e.scalar_tensor_tensor(
                        out=hT[:, c, tsl(tt)], in0=hT[:, c, tsl(tt)], scalar=gcol, in1=rs_(tt),
                        op0=ALU.mult, op1=ALU.mult), reads=[hB[c][tt], rsB[tt], cstB], writes=[hB[c][tt]])
                else:
                    S.op("dve", lambda e, c=c, gcol=gcol: e.scalar_tensor_tensor(
                        out=yT[:, c, tsl(tt)], in0=hT[:, c, tsl(tt)], scalar=gcol, in1=rs_(tt),
                        op0=ALU.mult, op1=ALU.mult), reads=[hB[c][tt], rsB[tt], cstB, aliasB], writes=[yB[c][tt]])
                if final and c % 4 == 3:
                    c0 = c - 3
                    S.op("sp", lambda e, c0=c0: e.dma_start(out=out_d.rearrange("c p t -> p c t")[:, c0:c0 + 4, tsl(tt)],
                                                            in_=hT[:, c0:c0 + 4, tsl(tt)]),
                         reads=[hB[cc][tt] for cc in range(c0, c0 + 4)], dma_to=outB)

        for tt in range(4):
            for c in range(8):
                al = [aliasB] if (tt == 0 and c in (0, 3)) else []
                if c != 3:
                    sl = sqi[0] % 10
                    sqi[0] += 1
                    S.op("act", lambda e, c=c, tt=tt, sl=sl: e.activation(out=sqb[:, sl, :], in_=hT[:, c, tsl(tt)], func=AF.Square),
                         reads=[hB[c][tt]], writes=[sqB[sl]] + al)
                else:
                    sl = 10 + sqi[1] % 2
                    sqi[1] += 1
                    S.op("dve", lambda e, c=c, tt=tt, sl=sl: e.tensor_tensor(out=sqb[:, sl, :], in0=hT[:, c, tsl(tt)], in1=hT[:, c, tsl(tt)],
                                                                            op=ALU.mult),
                         reads=[hB[c][tt]], writes=[sqB[sl]] + al)
                mm(pbank[tt][:, :], ones[:, :], sqb[:, sl, :], c == 0, c == 7, [sqB[sl], onesB], [pbB[tt]])
            S.op("act", lambda e, tt=tt: e.activation(out=rs_(tt), in_=pbank[tt][:, :], func=AF.Ln,
                                                     scale=1.0 / D, bias=epst[:, :]),
                 reads=[pbB[tt], epsB], writes=[rsB[tt]] + ([aliasB] if tt == 0 else []))
            S.op("act", lambda e, tt=tt: e.activation(out=rs_(tt), in_=rs_(tt), func=AF.Exp, scale=-0.5),
                 reads=[rsB[tt]], writes=[rsB[tt]])
            if tt >= 1:
                y_ops(tt - 1)
        y_ops(3)

    def gmlp_phase():
        sb_off[0] = ARENA
        uT = alloc("uT", [128, 16, 1024], BF16)
        vtm = alloc("vtm", [128, 8, 2048], BF16)
        junk = alloc("junk", [128, 512], BF16)
        t1 = alloc("t1", [128, 2, 512], F32)
        ssq = alloc("ssq", [128, 8, 4], F32)
        ss = alloc("ss", [128, 8], F32)
        rv = alloc("rv", [128, 8], F32)
        uB = [[Buf(f"u{c}_{t}") for t in range(2)] for c in range(16)]
        vB = [[Buf(f"v{c}_{b}") for b in range(4)] for c in range(8)]
        junkB, ssB, rvB = Buf("junk"), Buf("ss"), Buf("rv")
        t1B = [Buf("t1_0"), Buf("t1_1")]
        ssqB = [Buf(f"ssq{c}") for c in range(8)]
        t1i = [0]
        for half in range(2):
            T0 = half * 1024
            for ch in range(8):
                S.op("dve", lambda e, ch=ch: e.memset(ssq[:, ch, :], 0.0), writes=[ssqB[ch]])
            for vb in range(4):
                wt, wB = wnext(f"Wv{vb}")
                wv = wt[:, 0:4096].rearrange("p (k c) -> p k c", k=8)
                for ch in range(8):
                    t0 = T0 + ch * 128
                    bk, bB = bank()
                    for k in range(8):
                        mm(bk[:, :], yT[:, k, t0:t0 + 128], wv[:, k, :], k == 0, k == 7, [yB[k][t0 // 512], wB], [bB])
                    S.op("act", lambda e, bk=bk, ch=ch, vb=vb: e.activation(
                        out=vtm[:, ch, vb * 512:(vb + 1) * 512], in_=bk[:, :], func=AF.Gelu_apprx_tanh),
                        reads=[bB], writes=[vB[ch][vb]])
                    S.op("act", lambda e, ch=ch, vb=vb: e.activation(
                        out=junk[:, :], in_=vtm[:, ch, vb * 512:(vb + 1) * 512], func=AF.Square,
                        accum_out=ssq[:, ch, vb:vb + 1]),
                        reads=[vB[ch][vb]], writes=[junkB, ssqB[ch]])
            S.op("dve", lambda e: e.reduce_sum(out=ss[:, :], in_=ssq[:, :, :], axis=AX.X), reads=ssqB, writes=[ssB])
            S.op("act", lambda e: e.activation(out=rv[:, :], in_=ss[:, :], func=AF.Sqrt, scale=1.0 / 2048, bias=epst[:, :]),
                 reads=[ssB, epsB], writes=[rvB])
            S.op("dve", lambda e: e.reciprocal(out=rv[:, :], in_=rv[:, :]), reads=[rvB], writes=[rvB])
            for vb in range(4):
                S.op("dve", lambda e, vb=vb: e.tensor_tensor(
                    out=vtm[:, :, vb * 512:(vb + 1) * 512], in0=vtm[:, :, vb * 512:(vb + 1) * 512],
                    in1=rv[:, :].unsqueeze(2).to_broadcast([128, 8, 512]), op=ALU.mult),
                    reads=[vB[ch][vb] for ch in range(8)] + [rvB], writes=[vB[ch][vb] for ch in range(8)])
            steps = [(fc, tt) for fc in range(16) for tt in range(2)]
            wu_c = {}

            def U(fc, tt):
                if tt == 0:
                    wt, wB = wnext(f"Wu{fc}")
                    wu_c[fc] = (wt[:, 0:1024].rearrange("p (k c) -> p k c", k=8), wB)
                wu, wB = wu_c[fc]
                gs = slice(T0 + tt * 512, T0 + (tt + 1) * 512)
                ls = slice(tt * 512, (tt + 1) * 512)
                bk, bB = bank()
                for k in range(8):
                    mm(bk[:, :], wu[:, k, :], yT[:, k, gs], k == 0, k == 7, [yB[k][gs.start // 512], wB], [bB])
                S.op("act", lambda e, bk=bk, fc=fc, ls=ls: e.activation(out=uT[:, fc, ls], in_=bk[:, :], func=AF.Gelu_apprx_tanh),
                     reads=[bB] + ([tabdB] if fc >= 8 else []), writes=[uB[fc][tt]] + ([aliasB] if fc < 8 else []))

            def SG(fc, tt):
                g = fc // 2
                ls = slice(tt * 512, (tt + 1) * 512)
                bk2, bB2 = bank()
                for c4 in range(4):
                    ch = tt * 4 + c4
                    mm(bk2[:, c4 * 128:(c4 + 1) * 128], vtm[:, ch, fc * 128:(fc + 1) * 128], wsb[:, g, :], True, True,
                       [vB[ch][fc // 4], wsbB], [bB2])
                i = t1i[0] % 2
                t1i[0] += 1
                S.op("dve", lambda e, bk2=bk2, fc=fc, g=g, i=i: e.scalar_tensor_tensor(
                    out=t1[:, i, :].rearrange("p (a t) -> p a t", a=4),
                    in0=bk2[:, :].rearrange("p (a t) -> p a t", a=4),
                    scalar=cst[:, C_VG + fc:C_VG + fc + 1],
                    in1=cst[:, C_BB + g * 128:C_BB + (g + 1) * 128].unsqueeze(1).to_broadcast([128, 4, 128]),
                    op0=ALU.mult, op1=ALU.add), reads=[bB2, cstB], writes=[t1B[i]])
                S.op("dve", lambda e, fc=fc, ls=ls, i=i: e.tensor_tensor(out=uT[:, fc, ls], in0=t1[:, i, :], in1=uT[:, fc, ls],
                                                                          op=ALU.mult),
                     reads=[t1B[i], uB[fc][tt]], writes=[uB[fc][tt]])

            LA = 5
            for i_ in range(LA):
                U(*steps[i_])
            for i_ in range(32):
                SG(*steps[i_])
                if i_ + LA < 32:
                    U(*steps[i_ + LA])
            for dc in range(8):
                wt, wB = wnext(f"Wo{dc}")
                wo = wt[:, 0:2048].rearrange("p (k c) -> p k c", k=16)
                for tt in range(2):
                    gs = slice(T0 + tt * 512, T0 + (tt + 1) * 512)
                    ls = slice(tt * 512, (tt + 1) * 512)
                    bk, bB = bank()
                    for k in range(16):
                        mm(bk[:, :], wo[:, k, :], uT[:, k, ls], k == 0, k == 15, [uB[k][tt], wB] + ([aliasB] if k < 8 else []), [bB])
                    hb = hB[dc][gs.start // 512]
                    S.op("dve", lambda e, bk=bk, dc=dc, gs=gs: e.tensor_tensor(out=hT[:, dc, gs], in0=bk[:, :], in1=hT[:, dc, gs], op=ALU.add),
                         reads=[bB, hb], writes=[hb])

    def ffn_phase(l):
        sb_off[0] = ARENA
        R = 3
        actT = alloc(f"actT{l}", [128, 11, SEQ], BF16)
        ag = alloc(f"ag{l}", [128, 2 + SEQ], F32)
        au = alloc(f"au{l}", [128, 2 + SEQ], F32)
        tg = alloc(f"tg{l}", [128, R, 512], F32)
        tu = alloc(f"tu{l}", [128, R, 512], F32)
        actB = [[Buf(f"act{j}_{t}") for t in range(4)] for j in range(11)]
        agB = [Buf(f"ag{i}") for i in range(4)]
        auB = [Buf(f"au{i}") for i in range(4)]
        zB = Buf("azero")
        tgB = [Buf(f"tg{i}") for i in range(R)]
        tuB = [Buf(f"tu{i}") for i in range(R)]
        S.op("dve", lambda e: e.memset(ag[:, 0:2], 0.0), writes=[zB])
        S.op("dve", lambda e: e.memset(au[:, 0:2], 0.0), writes=[zB])

        def cw(chunk, tap):
            o = C_CW + (l * 44 + chunk) * 3 + tap
            return cst[:, o:o + 1]

        def cb(chunk):
            o = C_CB + l * 44 + chunk
            return cst[:, o:o + 1]

        def evac(bk, bkB, a, aB, t, tB, i, tt, chunk):
            c0 = 2 + 512 * tt
            S.op("act", lambda e: e.activation(out=a[:, c0:c0 + 512], in_=bk[:, :], func=AF.Copy), reads=[bkB], writes=[aB[tt]])
            S.op("act", lambda e: e.activation(out=t[:, i, :], in_=bk[:, :], func=AF.Identity, scale=cw(chunk, 2), bias=cb(chunk)),
                 reads=[bkB, cstB], writes=[tB[i]])

        def tap(a, aB, t, tB, i, tt, chunk, tp):
            c0 = 2 + 512 * tt
            lo = c0 - (2 - tp)
            rd = [aB[tt], tB[i], cstB] + ([aB[tt - 1]] if tt > 0 else [zB])
            S.op("dve", lambda e: e.scalar_tensor_tensor(out=t[:, i, :], in0=a[:, lo:lo + 512], scalar=cw(chunk, tp), in1=t[:, i, :],
                                                         op0=ALU.mult, op1=ALU.add), reads=rd, writes=[tB[i]])

        cnt = 0
        for G in range(2):
            pend = None
            for j in range(11):
                jj = G * 11 + j
                wt, wB = wnext(f"Wup{l}_{jj}")
                wu = wt[:, 0:2048].rearrange("p (k b c) -> p k b c", k=8, b=2)
                for tt in range(4):
                    i = cnt % R
                    cnt += 1
                    bg, bgB = bank()
                    bu, buB = bank()
                    for k in range(8):
                        mm(bg[:, :], wu[:, k, 0, :], yT[:, k, tsl(tt)], k == 0, k == 7, [yB[k][tt], wB], [bgB])
                    for k in range(8):
                        mm(bu[:, :], wu[:, k, 1, :], yT[:, k, tsl(tt)], k == 0, k == 7, [yB[k][tt], wB], [buB])
                    evac(bg, bgB, ag, agB, tg, tgB, i, tt, jj)
                    evac(bu, buB, au, auB, tu, tuB, i, tt, 22 + jj)
                    if pend is not None:
                        pi_, pj, ptt = pend
                        S.op("act", lambda e, pi_=pi_: e.activation(out=tg[:, pi_, :], in_=tg[:, pi_, :], func=AF.Gelu_apprx_tanh),
                             reads=[tgB[pi_]], writes=[tgB[pi_]])
                    tap(ag, agB, tg, tgB, i, tt, jj, 1)
                    tap(au, auB, tu, tuB, i, tt, 22 + jj, 1)
                    tap(ag, agB, tg, tgB, i, tt, jj, 0)
                    tap(au, auB, tu, tuB, i, tt, 22 + jj, 0)
                    if pend is not None:
                        pi_, pj, ptt = pend
                        S.op("dve", lambda e, pi_=pi_, pj=pj, ptt=ptt: e.tensor_tensor(out=actT[:, pj, tsl(ptt)], in0=tg[:, pi_, :],
                                                                                      in1=tu[:, pi_, :], op=ALU.mult),
                             reads=[tgB[pi_], tuB[pi_]], writes=[actB[pj][ptt]] + ([aliasB] if pj < 4 else []))
                    pend = (i, j, tt)
            pi_, pj, ptt = pend
            S.op("act", lambda e, pi_=pi_: e.activation(out=tg[:, pi_, :], in_=tg[:, pi_, :], func=AF.Gelu_apprx_tanh),
                 reads=[tgB[pi_]], writes=[tgB[pi_]])
            S.op("dve", lambda e, pi_=pi_, pj=pj, ptt=ptt: e.tensor_tensor(out=actT[:, pj, tsl(ptt)], in0=tg[:, pi_, :], in1=tu[:, pi_, :],
                                                                          op=ALU.mult),
                 reads=[tgB[pi_], tuB[pi_]], writes=[actB[pj][ptt]])
            for dc in range(8):
                wt, wB = wnext(f"Wd{l}_{G}_{dc}")
                wd = wt[:, 0:1408].rearrange("p (k c) -> p k c", k=11)
                for tt in range(4):
                    bk, bB = bank()
                    for k in range(11):
                        mm(bk[:, :], wd[:, k, :], actT[:, k, tsl(tt)], k == 0, k == 10, [actB[k][tt], wB] + ([aliasB] if k < 4 else []), [bB])
                    S.op("dve", lambda e, bk=bk, dc=dc, tt=tt: e.tensor_tensor(out=hT[:, dc, tsl(tt)], in0=bk[:, :], in1=hT[:, dc, tsl(tt)],
                                                                              op=ALU.add),
                         reads=[bB, hB[dc][tt]], writes=[hB[dc][tt]])

    def attn_phase():
        sb_off[0] = ARENA
        attnT = alloc("attnT", [128, 4, SEQ], BF16)
        cosT = alloc("cosT", [128, SEQ], F32)
        sinT = alloc("sinT", [128, SEQ], F32)
        qT = [alloc(f"qT{s}", [128, SEQ], BF16) for s in range(2)]
        kT = [alloc(f"kT{s}", [128, SEQ], BF16) for s in range(2)]
        vh = [alloc(f"vh{s}", [128, 16, 128], BF16) for s in range(2)]
        nm = [alloc(f"nm{s}", [128, 16, 8], BF16) for s in range(2)]
        pT = [alloc(f"pT{i}", [128, 2, 512], BF16) for i in range(2)]
        r1 = alloc("r1", [128, 2, 512], F32)
        r2 = alloc("r2", [128, 1, 512], F32)
        qb16 = alloc("qb16", [128, 2, 512], BF16)
        qb16B = [Buf("qb16_0"), Buf("qb16_1")]
        gm = alloc("gm", [128, 16, 8], F32)
        cmpt = alloc("cmpt", [128, 8, 8, 8], BF16)
        rank = alloc("rank", [128, 16, 8], F32)
        elig = alloc("elig", [128, 16, 8], BF16)
        negb = alloc("negb", [128, 16, 8], F32)
        km32 = alloc("km32", [128, 2, 8], F32)
        kmb = alloc("kmb", [128, 2, 8], BF16)
        rec = r2[:, 0, :]
        cm = alloc("cm", [128, 2, 256], BF16)
        negcm = alloc("negcm", [128, 2, 256], BF16)
        negcmB = Buf("negcm")
        ktmp = nc.alloc_sbuf_tensor_at("ktmp", [128, SEQ], I32, offset=(ARENA + 16384 + 16384 + 63) // 64 * 64)
        kf = nc.alloc_sbuf_tensor_at("kf", [128, SEQ], F32, offset=(ARENA + 16384 + 16384 + 8192 + 63) // 64 * 64)
        tpos = nc.alloc_sbuf_tensor_at("tpos", [128, SEQ], F32, offset=(ARENA + 16384 + 16384 + 16384 + 63) // 64 * 64)
        assert ARENA + 16384 * 3 + 8192 <= sb_off[0], "overlay scratch must stay inside arena"

        cosB, sinB, cmB, eligB, negbB = Buf("cos"), Buf("sin"), Buf("cm"), Buf("elig"), Buf("negb")
        attnB = [[Buf(f"at{k}_{q}") for q in range(4)] for k in range(4)]
        qB = [[Buf(f"q{s}_{t}") for t in range(4)] for s in range(2)]
        kB = [[Buf(f"k{s}_{t}") for t in range(4)] for s in range(2)]
        vhB = [[Buf(f"vh{s}_{t}") for t in range(4)] for s in range(2)]
        nmB = [Buf("nm0"), Buf("nm1")]
        pTB = [Buf(f"pT{i}") for i in range(2)]
        r1B = [Buf("r1_0"), Buf("r1_1")]
        r2B = [Buf("r2_0"), Buf("r2_0")]
        r2B[1] = r2B[0]
        gmB, cmpB, rankB = Buf("gm"), Buf("cmp"), Buf("rank")
        km32B = [Buf("km32_0"), Buf("km32_1")]
        kmbB = [Buf("kmb_0"), Buf("kmb_1")]
        recB = r2B[0]
        scr = Buf("scr")

        S.op("sp", lambda e: e.dma_start(out=sinT[:, :], in_=tab_d[0]), reads=[tabdB] + hB[7], writes=[sinB], dma_to=sinB)
        S.op("sp", lambda e: e.dma_start(out=cosT[:, :], in_=tab_d[1]), reads=[tabdB] + hB[7], writes=[cosB], dma_to=cosB)
        S.op("pool", lambda e: e.iota(kf[:, 0:512].rearrange("p (j i) -> p j i", j=2), pattern=[[-128, 2], [1, 256]], base=0,
                                      channel_multiplier=-1, allow_small_or_imprecise_dtypes=True), reads=[scr] + hB[7], writes=[scr])
        S.op("dve", lambda e: e.tensor_single_scalar(out=cm[:, :, :], in_=kf[:, 0:512].rearrange("p (j i) -> p j i", j=2),
                                                     scalar=0.0, op=ALU.is_ge), reads=[scr], writes=[cmB])
        S.op("dve", lambda e: e.tensor_scalar(out=negcm[:, :, :], in0=cm[:, :, :], scalar1=-1.0, scalar2=NEGM, op0=ALU.add, op1=ALU.mult),
             reads=[cmB], writes=[negcmB])
        S.op("pool", lambda e: e.iota(kf[:, 512:640].rearrange("p (a b) -> p a b", a=16), pattern=[[1, 16], [-2, 8]], base=-2,
                                      channel_multiplier=0, allow_small_or_imprecise_dtypes=True), reads=[scr] + hB[7], writes=[scr])
        S.op("dve", lambda e: e.tensor_single_scalar(out=elig[:, :, :], in_=kf[:, 512:640].rearrange("p (a b) -> p a b", a=16),
                                                     scalar=0.0, op=ALU.is_ge), reads=[scr], writes=[eligB])
        S.op("dve", lambda e: e.tensor_scalar(out=negb[:, :, :], in0=elig[:, :, :], scalar1=-1.0, scalar2=1e30, op0=ALU.add, op1=ALU.mult),
             reads=[eligB], writes=[negbB])
        fenceB = Buf("fence")
        S.op("dve", lambda e: e.memset(km32[:, 0, 0:1], 0.0), reads=[scr, cosB, sinB, cmB, negcmB, eligB, negbB], writes=[fenceB, scr])
        yield

        def proj(h):
            s = h % 2
            wt, wB = wnext(f"Wqk{h}")
            wqk = wt[:, 0:2048].rearrange("p (k b c) -> p k b c", k=8, b=2)
            tiles = [(w_, dst, dstB, tt) for (w_, dst, dstB) in ((0, qT[s], qB[s]), (1, kT[s], kB[s])) for tt in range(4)]
            pend = None

            def finish_tile(p):
                n, ba, baB, dst, dstB, tt = p
                i = n % 2
                bs, bsB = bank("proj4")
                mm(bs[:, :], permT[:, :], qb16[:, i, :], True, True, [permB, qb16B[i]], [bsB])
                S.op("dve", lambda e: e.tensor_tensor(out=r1[:, i, :], in0=ba[:, :], in1=cosT[:, tsl(tt)], op=ALU.mult),
                     reads=[baB, cosB, qb16B[i]], writes=[r1B[i]])
                S.op("dve", lambda e: e.tensor_tensor(out=r2[:, 0, :], in0=bs[:, :], in1=sinT[:, tsl(tt)], op=ALU.mult),
                     reads=[bsB, sinB], writes=[r2B[0]])
                S.op("pool", lambda e: e.tensor_tensor(out=dst[:, tsl(tt)], in0=r1[:, i, :], in1=r2[:, 0, :], op=ALU.add),
                     reads=[r1B[i], r2B[0], fenceB], writes=[dstB[tt]])

            for n, (w_, dst, dstB, tt) in enumerate(tiles):
                i = n % 2
                ba, baB = bank("proj4")
                for k in range(8):
                    mm(ba[:, :], wqk[:, k, w_, :], yT[:, k, tsl(tt)], k == 0, k == 7, [yB[k][tt], wB], [baB])
                S.op("act", lambda e, ba=ba, i=i: e.activation(out=qb16[:, i, :], in_=ba[:, :], func=AF.Copy),
                     reads=[baB], writes=[qb16B[i]])
                if pend is not None:
                    finish_tile(pend)
                pend = (n, ba, baB, dst, dstB, tt)
            finish_tile(pend)
            wt2, wB2 = wnext(f"Wvh{h}")
            wvh = wt2[:, 0:1024].rearrange("p (k c) -> p k c", k=8)
            for tq in range(4):
                bk, bB = bank("proj")
                for c4 in range(4):
                    tch = tq * 4 + c4
                    for k in range(8):
                        mm(bk[:, c4 * 128:(c4 + 1) * 128], yT[:, k, tch * 128:(tch + 1) * 128], wvh[:, k, :], k == 0, k == 7,
                           [yB[k][tq], wB2], [bB])
                S.op("act", lambda e, bk=bk, tq=tq, s=s: e.activation(
                    out=vh[s][:, tq * 4:(tq + 1) * 4, :].rearrange("p a c -> p (a c)"), in_=bk[:, :], func=AF.Copy),
                    reads=[bB, fenceB], writes=[vhB[s][tq]])
            S.op("dve", lambda e, s=s: e.reduce_sum(out=km32[:, s, :], in_=kT[s][:, :].rearrange("p (n t) -> p n t", n=8), axis=AX.X),
                 reads=kB[s], writes=[km32B[s]])
            S.op("dve", lambda e, s=s: e.tensor_scalar(out=kmb[:, s, :], in0=km32[:, s, :], scalar1=1.0 / 256, scalar2=None, op0=ALU.mult),
                 reads=[km32B[s]], writes=[kmbB[s]])
            pending_gates.append(h)

        def gate(h):
            s = h % 2
            b0 = spair()
            bk, bB = pbank[b0], pbB[b0]
            for qt in range(16):
                mm(bk[:, qt * 8:(qt + 1) * 8], qT[s][:, qt * 128:(qt + 1) * 128], kmb[:, s, :], True, True, [qB[s][qt // 4], kmbB[s]], [bB])
            S.op("dve", lambda e, bk=bk: e.tensor_tensor(out=gm[:, :, :], in0=bk[:, 0:128].rearrange("p (a b) -> p a b", a=16),
                                                        in1=negb[:, :, :], op=ALU.add), reads=[bB, negbB], writes=[gmB])
            for hf in range(2):
                hsl = slice(hf * 8, (hf + 1) * 8)
                S.op("dve", lambda e, hsl=hsl: e.tensor_tensor(out=cmpt[:, :, :, :], in0=gm[:, hsl, :].unsqueeze(2).to_broadcast([128, 8, 8, 8]),
                                                               in1=gm[:, hsl, :].unsqueeze(3).to_broadcast([128, 8, 8, 8]), op=ALU.is_gt),
                     reads=[gmB], writes=[cmpB])
                S.op("dve", lambda e, hsl=hsl: e.reduce_sum(out=rank[:, hsl, :], in_=cmpt[:, :, :, :], axis=AX.X), reads=[cmpB], writes=[rankB])
            S.op("dve", lambda e: e.scalar_tensor_tensor(out=rank[:, :, :], in0=rank[:, :, :], scalar=3.0, in1=elig[:, :, :],
                                                         op0=ALU.is_lt, op1=ALU.mult), reads=[rankB, eligB], writes=[rankB])
            S.op("dve", lambda e, s=s: e.tensor_scalar(out=nm[s][:, :, :], in0=rank[:, :, :], scalar1=-1.0, scalar2=NEGM,
                                                      op0=ALU.add, op1=ALU.mult), reads=[rankB], writes=[nmB[s]])

        pti = [0]
        pending_gates = []
        scale = float(128 ** -0.5)

        spi = [0]

        def spair():
            b0 = (0, 2)[spi[0] % 2]
            spi[0] += 1
            return b0

        def attend(h):
            s = h % 2
            hh = h % 4
            items = []
            for t in range(4):
                nblk = 2 * t + 2
                for n in range(nblk):
                    items.append(dict(t=t, n=n, first=(n == 0), last=(n == nblk - 1)))
            banks = {}

            def front(it):
                t, n = it["t"], it["n"]
                c0 = 256 if n == 2 * t + 1 else 0
                W = 512 - c0
                b0 = spair()
                masks = []
                for qh in range(c0 // 128, 4):
                    qb_q = 2 * t + qh // 2
                    if n < qb_q and qb_q >= 4:
                        masks.append(qh)
                own = n >= 2 * t
                hs = slice((n - 2 * t) * 256, (n - 2 * t) * 256 + 256) if own else None
                for j in range(2):
                    kt = 2 * n + j
                    sp_, spB = pbank[b0 + j], pbB[b0 + j]
                    extra = []
                    for qh in masks:
                        qt = 4 * t + qh
                        extra.append((sp_[:, qh * 128:(qh + 1) * 128], nm[s][:, qt, n:n + 1].to_broadcast([128, 128]), ident[:, :],
                                      [nmB[s], identB]))
                    if own:
                        extra.append((sp_[:, hs], ident[:, :], negcm[:, j, :], [identB, negcmB]))
                    mm(sp_[:, c0:512], kT[s][:, kt * 128:(kt + 1) * 128], qT[s][:, t * 512 + c0:(t + 1) * 512], True, len(extra) == 0,
                       [kB[s][kt // 4], qB[s][t]], [spB])
                    for mi, (o_, l_, r_, rd_) in enumerate(extra):
                        mm(o_, l_, r_, False, mi == len(extra) - 1, rd_, [spB])
                pi = pti[0] % 2
                pti[0] += 1
                src = pball[:, b0 * 512:(b0 + 2) * 512].rearrange("p (j c) -> p j c", j=2)
                S.op("act", lambda e: e.activation(out=pT[pi][:, :, c0:512], in_=src[:, :, c0:512], func=AF.Exp, scale=scale),
                     reads=[pbB[b0], pbB[b0 + 1]], writes=[pTB[pi]])
                it["pi"] = pi
                it["c0"] = c0

            def back(it):
                t, n, pi, c0 = it["t"], it["n"], it["pi"], it["c0"]
                if it["first"]:
                    banks[t] = (bank("pv"), bank("dn"))
                (pv, pvB), (dn, dnB) = banks[t]
                for j in range(2):
                    kt = 2 * n + j
                    fst = it["first"] and j == 0
                    lst = it["last"] and j == 1
                    mm(pv[:, c0:512], vh[s][:, kt, :], pT[pi][:, j, c0:512], fst, lst, [vhB[s][kt // 4], pTB[pi]], [pvB], skip=True)
                    mm(dn[:, c0:512], ones[:, :], pT[pi][:, j, c0:512], fst, lst, [onesB, pTB[pi]], [dnB], skip=True)
                if it["last"]:
                    qs = slice(t * 512, (t + 1) * 512)
                    S.op("dve", lambda e: e.reciprocal(out=rec, in_=dn[:, :]), reads=[dnB], writes=[recB])
                    S.op("dve", lambda e: e.tensor_tensor(out=attnT[:, hh, qs], in0=pv[:, :], in1=rec, op=ALU.mult),
                         reads=[pvB, recB], writes=[attnB[hh][t], aliasB])

            front(items[0])
            for i, it in enumerate(items):
                if i + 1 < len(items):
                    front(items[i + 1])
                back(it)
                if it["last"] and it["t"] == 0:
                    for g in [g for g in pending_gates if g <= h]:
                        pending_gates.remove(g)
                        gate(g)
                if it["last"] and it["t"] == 2:
                    while pending_gates:
                        gate(pending_gates.pop(0))

        def wo(hg):
            for dc in range(8):
                wt, wB = wnext(f"Wob{hg}_{dc}")
                wo_ = wt[:, 0:512].rearrange("p (k c) -> p k c", k=4)
                for tt in range(4):
                    bk, bB = bank("proj")
                    for k in range(4):
                        mm(bk[:, :], wo_[:, k, :], attnT[:, k, tsl(tt)], k == 0, k == 3, [attnB[k][tt], wB, aliasB], [bB])
                    S.op("dve", lambda e, bk=bk, dc=dc, tt=tt: e.tensor_tensor(out=hT[:, dc, tsl(tt)], in0=bk[:, :], in1=hT[:, dc, tsl(tt)],
                                                                              op=ALU.add),
                         reads=[bB, hB[dc][tt]], writes=[hB[dc][tt]])

        for it in attn_order():
            k = int(it[1])
            if it[0] == "p":
                proj(k)
            elif it[0] == "a":
                attend(k)
            else:
                wo(k)

    def finish():
        if stop_after != "final":
            for c in range(8):
                S.op("sp", lambda e, c=c: e.dma_start(out=out_d[c], in_=hT[:, c, :]), reads=hB[c], dma_to=outB)
        S.emit(nc, final_waits=[outB])
        return nc

    norm_phase(0, nobar=True)
    gmlp_phase()
    if stop_after == "gmlp":
        return finish()
    norm_phase(2, nobar=True)
    ffn_phase(0)
    if stop_after == "ffn0":
        return finish()
    attn_gen = attn_phase()
    norm_phase(1, pre=lambda: next(attn_gen), nobar=True)
    next(attn_gen, None)
    if stop_after == "attn":
        return finish()
    norm_phase(3, nobar=True)
    ffn_phase(1)
    if stop_after == "ffn1":
        return finish()
    norm_phase(4, final=True, nobar=True)
    return finish()


def _kpc(w, nk):
    C = w.shape[1]
    return np.ascontiguousarray(w.reshape(nk, 128, C).transpose(1, 0, 2))


def pack_weights(a_w_in, a_w_out, b_w_qkv, b_w_o, ffn_w_up, ffn_w_down):
    offs, WTOT = slab_offsets()
    wts = np.empty((128, WTOT), np.float32)

    def put(name, arr):
        o, n = offs[name]
        wts[:, o:o + n] = arr.reshape(128, n)

    w_in = a_w_in[0]
    for vb in range(4):
        put(f"Wv{vb}", _kpc(w_in[:, 2048 + vb * 512:2048 + (vb + 1) * 512], 8))
    for fc in range(16):
        put(f"Wu{fc}", _kpc(w_in[:, fc * 128:(fc + 1) * 128], 8))
    w_out = a_w_out[0]
    for dc in range(8):
        put(f"Wo{dc}", _kpc(w_out[:, dc * 128:(dc + 1) * 128], 16))
    for l in range(2):
        wu = ffn_w_up[l]
        for jj in range(22):
            g = _kpc(wu[:, jj * 128:(jj + 1) * 128], 8)
            u = _kpc(wu[:, DFF + jj * 128:DFF + (jj + 1) * 128], 8)
            put(f"Wup{l}_{jj}", np.stack([g, u], axis=2))
        wd = ffn_w_down[l]
        for G in range(2):
            for dc in range(8):
                put(f"Wd{l}_{G}_{dc}", _kpc(wd[G * 1408:(G + 1) * 1408, dc * 128:(dc + 1) * 128], 11))
    wqkv = b_w_qkv[0]
    perm = (np.arange(128) + 64) % 128
    for h in range(8):
        q = wqkv[:, h * 128:(h + 1) * 128]
        k = wqkv[:, 1024 + h * 128:1024 + (h + 1) * 128]
        v = wqkv[:, 2048 + h * 128:2048 + (h + 1) * 128]
        put(f"Wqk{h}", np.stack([_kpc(q, 8), _kpc(k, 8)], axis=2))
        put(f"Wvh{h}", _kpc(v, 8))
    w_o = b_w_o[0]
    for hg in range(2):
        for dc in range(8):
            put(f"Wob{hg}_{dc}", _kpc(w_o[hg * 512:(hg + 1) * 512, dc * 128:(dc + 1) * 128], 4))
    return wts


def pack_consts(mix_norm, ffn_norm, final_norm, a_v_gain, a_w_s, a_b_s, ffn_conv_w, ffn_conv_b):
    c = np.zeros((128, NCONST), np.float32)
    gains = [mix_norm[0], mix_norm[1], ffn_norm[0], ffn_norm[1], final_norm]
    for i, g in enumerate(gains):
        c[:, C_GAIN + i * 8:C_GAIN + (i + 1) * 8] = g.reshape(8, 128).T
    c[:, C_VG:C_VG + 16] = a_v_gain[0].reshape(16, 128).T
    for l in range(2):
        cw = ffn_conv_w[l].reshape(3, 44, 128)
        c[:, C_CW + l * 132:C_CW + (l + 1) * 132] = cw.transpose(2, 1, 0).reshape(128, 132)
        c[:, C_CB + l * 44:C_CB + (l + 1) * 44] = ffn_conv_b[l].reshape(44, 128).T
    ws = a_w_s[0]
    c[:, C_WS:C_WS + 1024] = ws.transpose(2, 0, 1).reshape(128, 1024)
    c[:, C_BB:C_BB + 1024] = np.broadcast_to(a_b_s[0].reshape(1, 1024), (128, 1024))
    return c


_NC_CACHE = {}


def kernel(x, mix_norm, a_w_in, a_v_gain, a_w_s, a_b_s, a_w_out, b_w_qkv, b_w_o,
           ffn_norm, ffn_w_up, ffn_conv_w, ffn_conv_b, ffn_w_down, final_norm, _stop_after="final", _cores=8):
    f = lambda a: np.asarray(a, dtype=np.float32)
    x = f(x)
    wts = pack_weights(f(a_w_in), f(a_w_out), f(b_w_qkv), f(b_w_o), f(ffn_w_up), f(ffn_w_down))
    consts = pack_consts(f(mix_norm), f(ffn_norm), f(final_norm), f(a_v_gain), f(a_w_s), f(a_b_s), f(ffn_conv_w), f(ffn_conv_b))
    nc = build_nc(_stop_after)
    in_maps = []
    for b in range(_cores):
        xT = np.ascontiguousarray(x[b].T).reshape(8, 128, SEQ)
        in_maps.append({"xT": xT, "consts": consts, "wts": wts})
    res = run_bass_kernel_spmd(nc, in_maps, core_ids=list(range(_cores)))
    outs = []
    for b in range(_cores):
        oT = res.results[b]["outT"].reshape(D, SEQ)
        outs.append(np.ascontiguousarray(oT.T))
    return np.stack(outs, axis=0).astype(np.float32)
```

```python
import os
from contextlib import ExitStack

import numpy as np
import concourse.bass as bass
import concourse.mybir as mybir
from concourse.bass_utils import run_bass_kernel_spmd

F32 = mybir.dt.float32
BF16 = mybir.dt.bfloat16
I32 = mybir.dt.int32
AF = mybir.ActivationFunctionType
ALU = mybir.AluOpType
AX = mybir.AxisListType

D = 1024
SEQ = 2048
NB = 8
DFF = 2816
EPS = 1e-6
ROPE_THETA = 10000.0
NEGM = 30000.0

C_GAIN = 0
C_VG = 40
C_CW = 56
C_CB = C_CW + 264
C_WS = C_CB + 88
C_BB = C_WS + 1024
NCONST = C_BB + 1024


class Buf:
    __slots__ = ("name", "w", "r", "ndma", "sem")

    def __init__(self, name):
        self.name = name
        self.w = None
        self.r = {}
        self.ndma = 0
        self.sem = None


class Op:
    __slots__ = ("eng", "fn", "deps", "dmadeps", "marked", "seq", "dma_buf", "dma_cnt", "pos")

    def __init__(self, eng, fn):
        self.eng = eng
        self.fn = fn
        self.deps = []
        self.dmadeps = {}
        self.marked = False
        self.seq = 0
        self.dma_buf = None
        self.dma_cnt = 0


class Sched:
    ENGS = ("pe", "act", "dve", "pool", "sp")

    def __init__(self):
        self.q = {e: [] for e in self.ENGS}
        self.dma_bufs = []
        self.bar = None
        self.bar_done = set()

    def _dep(self, op, prev):
        if prev is None or prev is op:
            return
        if prev.dma_buf is not None:
            b = prev.dma_buf
            op.dmadeps[b] = max(op.dmadeps.get(b, 0), b.ndma)
            return
        if prev.eng == "pe" and op.eng == "pe":
            return
        op.deps.append(prev)
        prev.marked = True

    def barrier(self):
        self.bar = [self.q[e][-1] for e in self.ENGS if self.q[e]]
        self.bar_done = set()

    def op(self, eng, fn, reads=(), writes=(), dma_to=None, nobarrier=False):
        o = Op(eng, fn)
        if self.bar is not None and not nobarrier and eng not in self.bar_done:
            for p in self.bar:
                self._dep(o, p)
            self.bar_done.add(eng)
        for b in reads:
            self._dep(o, b.w)
        for b in writes:
            self._dep(o, b.w)
            for r in b.r.values():
                self._dep(o, r)
        key = eng if dma_to is None else ("dma", dma_to.name, len(self.q[eng]))
        for b in reads:
            b.r[key] = o
        for b in writes:
            b.w = o
            b.r = {}
        if dma_to is not None:
            dma_to.ndma += 1
            o.dma_buf = dma_to
            o.dma_cnt = dma_to.ndma
            if dma_to not in self.dma_bufs:
                self.dma_bufs.append(dma_to)
        o.pos = len(self.q[eng])
        self.q[eng].append(o)
        return o

    def emit(self, nc, final_waits=()):
        with ExitStack() as es:
            sems = {e: es.enter_context(nc.semaphore("s_" + e)) for e in ("pe", "act", "dve", "pool")}
            for b in self.dma_bufs:
                b.sem = es.enter_context(nc.semaphore("d_" + b.name))
            for e in self.ENGS:
                c = 0
                for o in self.q[e]:
                    if o.marked and o.dma_buf is None:
                        c += 1
                        o.seq = c
            block = es.enter_context(nc.Block())
            sched = self

            def run(ename, eng):
                seen = {}
                for o in sched.q[ename]:
                    need = {}
                    for d in o.deps:
                        if need.get(d.eng, 0) < d.seq:
                            need[d.eng] = d.seq
                    for k, v in need.items():
                        if seen.get(k, 0) < v:
                            eng.wait_ge(sems[k], v)
                            seen[k] = v
                    for b, cnt in o.dmadeps.items():
                        if seen.get(b, 0) < cnt:
                            eng.wait_ge(b.sem, 16 * cnt)
                            seen[b] = cnt
                    ins = o.fn(eng)
                    if o.dma_buf is not None:
                        ins.then_inc(o.dma_buf.sem, 16)
                    elif o.marked:
                        ins.then_inc(sems[ename], 1)
                if ename == "sp":
                    for b in final_waits:
                        eng.wait_ge(b.sem, 16 * b.ndma)

            @block.tensor
            def _(eng):
                run("pe", eng)

            @block.scalar
            def _(eng):
                run("act", eng)

            @block.vector
            def _(eng):
                run("dve", eng)

            @block.gpsimd
            def _(eng):
                run("pool", eng)

            @block.sync
            def _(eng):
                run("sp", eng)


def slab_defs():
    d = {}
    for vb in range(4):
        d[f"Wv{vb}"] = 8 * 512
    for fc in range(16):
        d[f"Wu{fc}"] = 8 * 128
    for dc in range(8):
        d[f"Wo{dc}"] = 16 * 128
    for l in range(2):
        for jj in range(22):
            d[f"Wup{l}_{jj}"] = 8 * 2 * 128
        for G in range(2):
            for dc in range(8):
                d[f"Wd{l}_{G}_{dc}"] = 11 * 128
    for h in range(8):
        d[f"Wqk{h}"] = 8 * 2 * 128
        d[f"Wvh{h}"] = 8 * 128
    for hg in range(2):
        for dc in range(8):
            d[f"Wob{hg}_{dc}"] = 4 * 128
    return d


def slab_offsets():
    offs = {}
    o = 0
    for k, n in slab_defs().items():
        offs[k] = (o, n)
        o += n
    return offs, o


def attn_order():
    return ["p0", "p1", "a0", "p2", "a1", "p3", "a2", "p4", "a3", "w0", "p5", "a4", "p6", "a5", "p7", "a6", "a7", "w1"]


def stream_order(stop_after):
    seq = []
    for half in range(2):
        seq += [f"Wv{vb}" for vb in range(4)]
        seq += [f"Wu{fc}" for fc in range(16)]
        seq += [f"Wo{dc}" for dc in range(8)]
    if stop_after == "gmlp":
        return seq

    def ffn(l):
        s = []
        for G in range(2):
            s += [f"Wup{l}_{G * 11 + j}" for j in range(11)]
            s += [f"Wd{l}_{G}_{dc}" for dc in range(8)]
        return s

    seq += ffn(0)
    if stop_after == "ffn0":
        return seq
    for it in attn_order():
        k = int(it[1])
        if it[0] == "p":
            seq += [f"Wqk{k}", f"Wvh{k}"]
        elif it[0] == "w":
            seq += [f"Wob{k}_{dc}" for dc in range(8)]
    if stop_after == "attn":
        return seq
    seq += ffn(1)
    return seq


def build_nc(stop_after="final"):
    nc = bass.Bass("TRN2", target_bir_lowering=False)
    offs, WTOT = slab_offsets()
    xT_d = nc.dram_tensor("xT", [8, 128, SEQ], F32, kind="ExternalInput").ap()
    consts_d = nc.dram_tensor("consts", [128, NCONST], F32, kind="ExternalInput").ap()
    wts_d = nc.dram_tensor("wts", [128, WTOT], F32, kind="ExternalInput").ap()
    out_d = nc.dram_tensor("outT", [8, 128, SEQ], F32, kind="ExternalOutput").ap()

    S = Sched()
    sb_off = [16512]
    SB_LIMIT = 227328

    def alloc(name, shape, dt, at=None):
        esz = 2 if dt == BF16 else 4
        n = esz
        for s_ in shape[1:]:
            n *= s_
        off = sb_off[0] if at is None else at
        off = (off + 63) // 64 * 64
        assert off + n <= SB_LIMIT, (name, off, n)
        t = nc.alloc_sbuf_tensor_at(name, list(shape), dt, offset=off)
        if at is None:
            sb_off[0] = off + n
        return t

    hT = alloc("hT", [128, 8, SEQ], F32)
    yT = alloc("yT", [128, 8, SEQ], BF16)
    cst = alloc("cst", [128, NCONST], F32)
    ident = alloc("ident", [128, 128], BF16)
    ones = alloc("ones", [128, 128], BF16)
    epst = alloc("epst", [128, 1], F32)
    permT = alloc("permT", [128, 128], BF16)
    wsb = alloc("wsb", [128, 8, 128], BF16)
    wslots = [alloc(f"wslot{i}", [128, 4096], BF16) for i in range(3)]
    ARENA = sb_off[0]

    hB = [[Buf(f"h{c}_{t}") for t in range(4)] for c in range(8)]
    yB = [[Buf(f"y{c}_{t}") for t in range(4)] for c in range(8)]
    cstB, identB, onesB, epsB, wsbB = Buf("cst"), Buf("ident"), Buf("ones"), Buf("eps"), Buf("wsb")
    wslotB = [Buf(f"wslot{i}") for i in range(3)]
    outB = Buf("out")
    xinB = [Buf(f"xin{c}") for c in range(8)]
    aliasB = Buf("alias")

    pball = nc.alloc_psum_tensor("pball", [128, 4096], F32)
    pbank = [pball[:, i * 512:(i + 1) * 512] for i in range(8)]
    pbB = [Buf(f"pb{i}") for i in range(8)]
    rot = {"all": [0, list(range(8))], "proj": [0, [0, 1, 2, 3, 4, 5, 6, 7]], "proj4": [0, [0, 1, 2, 3, 4, 5, 6, 7]], "s": [0, [2, 3, 0]], "pv": [0, [4, 6]], "dn": [0, [5, 7]]}

    def bank(kind="all"):
        r = rot[kind]
        i = r[1][r[0] % len(r[1])]
        r[0] += 1
        return pbank[i], pbB[i]

    order = stream_order(stop_after)
    wstate = {"issued": 0, "taken": 0}

    def w_issue_upto(n):
        while wstate["issued"] < min(n, len(order)):
            i = wstate["issued"]
            name = order[i]
            o, ne = offs[name]
            t, B = wslots[i % 3], wslotB[i % 3]
            S.op("pool", lambda e, t=t, o=o, ne=ne: e.dma_start(out=t[:, 0:ne], in_=wts_d[:, o:o + ne]),
                 writes=[B], dma_to=B, nobarrier=True)
            wstate["issued"] += 1

    def wnext(name):
        i = wstate["taken"]
        assert order[i] == name, (order[i], name)
        w_issue_upto(i + 3)
        wstate["taken"] += 1
        return wslots[i % 3], wslotB[i % 3]

    def mm(out_ap, lhsT, rhs, start, stop, reads, writes, skip=False):
        S.op("pe", lambda e: e.matmul(out_ap, lhsT=lhsT, rhs=rhs, start=start, stop=stop, skip_group_check=skip),
             reads=reads, writes=writes)

    def tsl(tt):
        return slice(tt * 512, (tt + 1) * 512)

    def emit_tables(sinT, cosT, sinB, cosB, ktmp, kf, tpos, pcol, scr):
        S.op("pool", lambda e: e.iota(pcol[:, 0:1], pattern=[[0, 1]], base=0, channel_multiplier=1,
                                      allow_small_or_imprecise_dtypes=True), writes=[scr])
        S.op("dve", lambda e: e.tensor_single_scalar(out=pcol[:, 1:2], in_=pcol[:, 0:1], scalar=64.0, op=ALU.is_ge), reads=[scr], writes=[scr])
        S.op("dve", lambda e: e.scalar_tensor_tensor(out=pcol[:, 2:3], in0=pcol[:, 1:2], scalar=-64.0, in1=pcol[:, 0:1],
                                                     op0=ALU.mult, op1=ALU.add), reads=[scr], writes=[scr])
        S.op("dve", lambda e: e.tensor_scalar(out=pcol[:, 3:4], in0=pcol[:, 1:2], scalar1=2.0, scalar2=-1.0, op0=ALU.mult, op1=ALU.add),
             reads=[scr], writes=[scr])
        S.op("act", lambda e: e.activation(out=pcol[:, 2:3], in_=pcol[:, 2:3], func=AF.Exp, scale=-float(np.log(ROPE_THETA)) / 64.0),
             reads=[scr], writes=[scr])
        S.op("pool", lambda e: e.iota(tpos[:, :], pattern=[[1, SEQ]], base=0, channel_multiplier=0,
                                      allow_small_or_imprecise_dtypes=True), writes=[scr])
        C1 = 6.28125
        C2 = float(2 * np.pi - C1)

        def table(dst, dstB, shift, signed):
            S.op("dve", lambda e: e.tensor_scalar(out=dst[:, :], in0=tpos[:, :], scalar1=pcol[:, 2:3], scalar2=float(shift),
                                                  op0=ALU.mult, op1=ALU.add), reads=[scr], writes=[dstB])
            S.op("dve", lambda e: e.tensor_scalar(out=ktmp[:, :], in0=dst[:, :], scalar1=float(1.0 / (2 * np.pi)), scalar2=None,
                                                  op0=ALU.mult), reads=[dstB], writes=[scr])
            S.op("dve", lambda e: e.tensor_copy(out=kf[:, :], in_=ktmp[:, :]), reads=[scr], writes=[scr])
            S.op("dve", lambda e: e.scalar_tensor_tensor(out=dst[:, :], in0=kf[:, :], scalar=-C1, in1=dst[:, :],
                                                         op0=ALU.mult, op1=ALU.add), reads=[scr, dstB], writes=[dstB])
            S.op("dve", lambda e: e.scalar_tensor_tensor(out=dst[:, :], in0=kf[:, :], scalar=-C2, in1=dst[:, :],
                                                         op0=ALU.mult, op1=ALU.add), reads=[scr, dstB], writes=[dstB])
            S.op("dve", lambda e: e.tensor_scalar(out=dst[:, :], in0=dst[:, :], scalar1=-3.1415925, scalar2=3.1415925,
                                                  op0=ALU.max, op1=ALU.min), reads=[dstB], writes=[dstB])
            S.op("act", lambda e: e.activation(out=dst[:, :], in_=dst[:, :], func=AF.Sin), reads=[dstB], writes=[dstB])
            if signed:
                S.op("dve", lambda e: e.tensor_scalar(out=dst[:, :], in0=dst[:, :], scalar1=pcol[:, 3:4], scalar2=None, op0=ALU.mult),
                     reads=[dstB, scr], writes=[dstB])

        table(sinT, sinB, 0.0, True)
        table(cosT, cosB, float(np.pi / 2), False)

    S.op("sp", lambda e: e.dma_start(out=cst[:, :], in_=consts_d), writes=[cstB], dma_to=cstB)
    for tt in range(4):
        for cg in range(2):
            S.op("sp", lambda e, tt=tt, cg=cg: e.dma_start(
                out=hT[:, cg * 4:(cg + 1) * 4, tsl(tt)],
                in_=xT_d.rearrange("c p t -> p c t")[:, cg * 4:(cg + 1) * 4, tsl(tt)]),
                writes=[hB[c][tt] for c in range(cg * 4, (cg + 1) * 4)], dma_to=xinB[tt * 2 + cg])
    w_issue_upto(3)

    sb_off[0] = ARENA + 61440
    io_f = alloc("io_f", [128, 256], F32)
    ioB = Buf("io")
    S.op("pool", lambda e: e.iota(io_f[:, :], pattern=[[1, 256]], base=0, channel_multiplier=-1,
                                  allow_small_or_imprecise_dtypes=True), writes=[ioB])
    S.op("dve", lambda e: e.tensor_single_scalar(out=ident[:, :], in_=io_f[:, 0:128], scalar=0.0, op=ALU.is_equal),
         reads=[ioB], writes=[identB])
    S.op("dve", lambda e: e.memset(ones[:, :], 1.0), writes=[onesB])
    iosq = alloc("iosq", [128, 128], F32)
    iosqB, permB = Buf("iosq"), Buf("perm")
    S.op("dve", lambda e: e.tensor_tensor(out=iosq[:, :], in0=io_f[:, 0:128], in1=io_f[:, 0:128], op=ALU.mult), reads=[ioB], writes=[iosqB])
    S.op("dve", lambda e: e.tensor_single_scalar(out=permT[:, :], in_=iosq[:, :], scalar=4096.0, op=ALU.is_equal),
         reads=[iosqB], writes=[permB])
    S.op("dve", lambda e: e.memset(epst[:, :], EPS), writes=[epsB])
    msk = alloc("msk", [128, 128], F32)
    mskB = Buf("msk")
    S.op("dve", lambda e: e.tensor_single_scalar(out=msk[:, :], in_=io_f[:, 0:128], scalar=0.0, op=ALU.is_ge),
         reads=[ioB], writes=[mskB])
    S.op("dve", lambda e: e.tensor_tensor(
        out=wsb[:, :, :], in0=cst[:, C_WS:C_WS + 1024].rearrange("p (g t) -> p g t", g=8),
        in1=msk[:, :].unsqueeze(1).to_broadcast([128, 8, 128]), op=ALU.mult),
        reads=[cstB, mskB], writes=[wsbB])

    tab_d = nc.dram_tensor("ropetab", [2, 128, SEQ], F32, kind="Internal").ap()
    tabdB = Buf("tabd")
    if stop_after not in ("gmlp", "ffn0"):
        sb_off[0] = ARENA + 16384
        s_sin = alloc("s_sin", [128, SEQ], F32)
        s_cos = alloc("s_cos", [128, SEQ], F32)
        s_ktmp = alloc("s_ktmp", [128, SEQ], I32)
        s_kf = alloc("s_kf", [128, SEQ], F32)
        s_tpos = alloc("s_tpos", [128, SEQ], F32)
        s_pcol = alloc("s_pcol", [128, 4], F32)
        s_sinB, s_cosB, s_scr = Buf("s_sin"), Buf("s_cos"), Buf("s_scr")
        emit_tables(s_sin, s_cos, s_sinB, s_cosB, s_ktmp, s_kf, s_tpos, s_pcol, s_scr)
        S.op("sp", lambda e: e.dma_start(out=tab_d[0], in_=s_sin[:, :]), reads=[s_sinB], writes=[tabdB], dma_to=tabdB)
        S.op("sp", lambda e: e.dma_start(out=tab_d[1], in_=s_cos[:, :]), reads=[s_cosB], writes=[tabdB], dma_to=tabdB)

    def norm_phase(gi, final=False, pre=None, nobar=False):
        if not nobar:
            S.barrier()
        if pre is not None:
            pre()
        sb_off[0] = ARENA
        rs2 = alloc(f"rs{gi}", [128, 2, 512], F32)
        sqb = alloc(f"sqb{gi}", [128, 12, 512], BF16)
        rsB2 = [Buf(f"rs{gi}_{t}") for t in range(2)]
        rsB = [rsB2[t % 2] for t in range(4)]
        sqB = [Buf(f"sq{gi}_{i}") for i in range(12)]

        def rs_(tt):
            return rs2[:, tt % 2, :]
        sqi = [0, 0]
        def y_ops(tt):
            for c in range(8):
                gcol = cst[:, C_GAIN + gi * 8 + c:C_GAIN + gi * 8 + c + 1]
                if final:
                    S.op("dve", lambda e, c=c, gcol=gcol: e.scalar_tensor_tensor(
                        out=hT[:, c, tsl(tt)], in0=hT[:, c, tsl(tt)], scalar=gcol, in1=rs_(tt),
                        op0=ALU.mult, op1=ALU.mult), reads=[hB[c][tt], rsB[tt], cstB], writes=[hB[c][tt]])
                else:
                    S.op("dve", lambda e, c=c, gcol=gcol: e.scalar_tensor_tensor(
                        out=yT[:, c, tsl(tt)], in0=hT[:, c, tsl(tt)], scalar=gcol, in1=rs_(tt),
                        op0=ALU.mult, op1=ALU.mult), reads=[hB[c][tt], rsB[tt], cstB, aliasB], writes=[yB[c][tt]])
                if final and c % 4 == 3:
                    c0 = c - 3
                    S.op("sp", lambda e, c0=c0: e.dma_start(out=out_d.rearrange("c p t -> p c t")[:, c0:c0 + 4, tsl(tt)],
                                                            in_=hT[:, c0:c0 + 4, tsl(tt)]),
                         reads=[hB[cc][tt] for cc in range(c0, c0 + 4)], dma_to=outB)

        for tt in range(4):
            for c in range(8):
                al = [aliasB] if (tt == 0 and c in (0, 3)) else []
                if c != 3:
                    sl = sqi[0] % 10
                    sqi[0] += 1
                    S.op("act", lambda e, c=c, tt=tt, sl=sl: e.activation(out=sqb[:, sl, :], in_=hT[:, c, tsl(tt)], func=AF.Square),
                         reads=[hB[c][tt]], writes=[sqB[sl]] + al)
                else:
                    sl = 10 + sqi[1] % 2
                    sqi[1] += 1
                    S.op("dve", lambda e, c=c, tt=tt, sl=sl: e.tensor_tensor(out=sqb[:, sl, :], in0=hT[:, c, tsl(tt)], in1=hT[:, c, tsl(tt)],
                                                                            op=ALU.mult),
                         reads=[hB[c][tt]], writes=[sqB[sl]] + al)
                mm(pbank[tt][:, :], ones[:, :], sqb[:, sl, :], c == 0, c == 7, [sqB[sl], onesB], [pbB[tt]])
            S.op("act", lambda e, tt=tt: e.activation(out=rs_(tt), in_=pbank[tt][:, :], func=AF.Ln,
                                                     scale=1.0 / D, bias=epst[:, :]),
                 reads=[pbB[tt], epsB], writes=[rsB[tt]] + ([aliasB] if tt == 0 else []))
            S.op("act", lambda e, tt=tt: e.activation(out=rs_(tt), in_=rs_(tt), func=AF.Exp, scale=-0.5),
                 reads=[rsB[tt]], writes=[rsB[tt]])
            if tt >= 1:
                y_ops(tt - 1)
        y_ops(3)

    def gmlp_phase():
        sb_off[0] = ARENA
        uT = alloc("uT", [128, 16, 1024], BF16)
        vtm = alloc("vtm", [128, 8, 2048], BF16)
        junk = alloc("junk", [128, 512], BF16)
        t1 = alloc("t1", [128, 2, 512], F32)
        ssq = alloc("ssq", [128, 8, 4], F32)
        ss = alloc("ss", [128, 8], F32)
        rv = alloc("rv", [128, 8], F32)
        uB = [[Buf(f"u{c}_{t}") for t in range(2)] for c in range(16)]
        vB = [[Buf(f"v{c}_{b}") for b in range(4)] for c in range(8)]
        junkB, ssB, rvB = Buf("junk"), Buf("ss"), Buf("rv")
        t1B = [Buf("t1_0"), Buf("t1_1")]
        ssqB = [Buf(f"ssq{c}") for c in range(8)]
        t1i = [0]
        for half in range(2):
            T0 = half * 1024
            for ch in range(8):
                S.op("dve", lambda e, ch=ch: e.memset(ssq[:, ch, :], 0.0), writes=[ssqB[ch]])
            for vb in range(4):
                wt, wB = wnext(f"Wv{vb}")
                wv = wt[:, 0:4096].rearrange("p (k c) -> p k c", k=8)
                for ch in range(8):
                    t0 = T0 + ch * 128
                    bk, bB = bank()
                    for k in range(8):
                        mm(bk[:, :], yT[:, k, t0:t0 + 128], wv[:, k, :], k == 0, k == 7, [yB[k][t0 // 512], wB], [bB])
                    S.op("act", lambda e, bk=bk, ch=ch, vb=vb: e.activation(
                        out=vtm[:, ch, vb * 512:(vb + 1) * 512], in_=bk[:, :], func=AF.Gelu_apprx_tanh),
                        reads=[bB], writes=[vB[ch][vb]])
                    S.op("act", lambda e, ch=ch, vb=vb: e.activation(
                        out=junk[:, :], in_=vtm[:, ch, vb * 512:(vb + 1) * 512], func=AF.Square,
                        accum_out=ssq[:, ch, vb:vb + 1]),
                        reads=[vB[ch][vb]], writes=[junkB, ssqB[ch]])
            S.op("dve", lambda e: e.reduce_sum(out=ss[:, :], in_=ssq[:, :, :], axis=AX.X), reads=ssqB, writes=[ssB])
            S.op("act", lambda e: e.activation(out=rv[:, :], in_=ss[:, :], func=AF.Sqrt, scale=1.0 / 2048, bias=epst[:, :]),
                 reads=[ssB, epsB], writes=[rvB])
            S.op("dve", lambda e: e.reciprocal(out=rv[:, :], in_=rv[:, :]), reads=[rvB], writes=[rvB])
            for ch in range(8):
                S.op("dve", lambda e, ch=ch: e.tensor_scalar(out=vtm[:, ch, :], in0=vtm[:, ch, :], scalar1=rv[:, ch:ch + 1],
                                                            scalar2=None, op0=ALU.mult),
                     reads=vB[ch] + [rvB], writes=vB[ch])
            steps = [(fc, tt) for fc in range(16) for tt in range(2)]
            wu_c = {}

            def U(fc, tt):
                if tt == 0:
                    wt, wB = wnext(f"Wu{fc}")
                    wu_c[fc] = (wt[:, 0:1024].rearrange("p (k c) -> p k c", k=8), wB)
                wu, wB = wu_c[fc]
                gs = slice(T0 + tt * 512, T0 + (tt + 1) * 512)
                ls = slice(tt * 512, (tt + 1) * 512)
                bk, bB = bank()
                for k in range(8):
                    mm(bk[:, :], wu[:, k, :], yT[:, k, gs], k == 0, k == 7, [yB[k][gs.start // 512], wB], [bB])
                S.op("act", lambda e, bk=bk, fc=fc, ls=ls: e.activation(out=uT[:, fc, ls], in_=bk[:, :], func=AF.Gelu_apprx_tanh),
                     reads=[bB] + ([tabdB] if fc >= 8 else []), writes=[uB[fc][tt]] + ([aliasB] if fc < 8 else []))

            def SG(fc, tt):
                g = fc // 2
                ls = slice(tt * 512, (tt + 1) * 512)
                bk2, bB2 = bank()
                for c4 in range(4):
                    ch = tt * 4 + c4
                    mm(bk2[:, c4 * 128:(c4 + 1) * 128], vtm[:, ch, fc * 128:(fc + 1) * 128], wsb[:, g, :], True, True,
                       [vB[ch][fc // 4], wsbB], [bB2])
                i = t1i[0] % 2
                t1i[0] += 1
                S.op("dve", lambda e, bk2=bk2, fc=fc, g=g, i=i: e.scalar_tensor_tensor(
                    out=t1[:, i, :].rearrange("p (a t) -> p a t", a=4),
                    in0=bk2[:, :].rearrange("p (a t) -> p a t", a=4),
                    scalar=cst[:, C_VG + fc:C_VG + fc + 1],
                    in1=cst[:, C_BB + g * 128:C_BB + (g + 1) * 128].unsqueeze(1).to_broadcast([128, 4, 128]),
                    op0=ALU.mult, op1=ALU.add), reads=[bB2, cstB], writes=[t1B[i]])
                S.op("dve", lambda e, fc=fc, ls=ls, i=i: e.tensor_tensor(out=uT[:, fc, ls], in0=t1[:, i, :], in1=uT[:, fc, ls],
                                                                          op=ALU.mult),
                     reads=[t1B[i], uB[fc][tt]], writes=[uB[fc][tt]])

            LA = 5
            for i_ in range(LA):
                U(*steps[i_])
            for i_ in range(32):
                SG(*steps[i_])
                if i_ + LA < 32:
                    U(*steps[i_ + LA])
            for dc in range(8):
                wt, wB = wnext(f"Wo{dc}")
                wo = wt[:, 0:2048].rearrange("p (k c) -> p k c", k=16)
                for tt in range(2):
                    gs = slice(T0 + tt * 512, T0 + (tt + 1) * 512)
                    ls = slice(tt * 512, (tt + 1) * 512)
                    bk, bB = bank()
                    for k in range(16):
                        mm(bk[:, :], wo[:, k, :], uT[:, k, ls], k == 0, k == 15, [uB[k][tt], wB] + ([aliasB] if k < 8 else []), [bB])
                    hb = hB[dc][gs.start // 512]
                    S.op("dve", lambda e, bk=bk, dc=dc, gs=gs: e.tensor_tensor(out=hT[:, dc, gs], in0=bk[:, :], in1=hT[:, dc, gs], op=ALU.add),
                         reads=[bB, hb], writes=[hb])

    def ffn_phase(l):
        sb_off[0] = ARENA
        R = 3
        actT = alloc(f"actT{l}", [128, 11, SEQ], BF16)
        ag = alloc(f"ag{l}", [128, 2 + SEQ], F32)
        au = alloc(f"au{l}", [128, 2 + SEQ], F32)
        tg = alloc(f"tg{l}", [128, R, 512], F32)
        tu = alloc(f"tu{l}", [128, R, 512], F32)
        actB = [[Buf(f"act{j}_{t}") for t in range(4)] for j in range(11)]
        agB = [Buf(f"ag{i}") for i in range(4)]
        auB = [Buf(f"au{i}") for i in range(4)]
        zB = Buf("azero")
        tgB = [Buf(f"tg{i}") for i in range(R)]
        tuB = [Buf(f"tu{i}") for i in range(R)]
        S.op("dve", lambda e: e.memset(ag[:, 0:2], 0.0), writes=[zB])
        S.op("dve", lambda e: e.memset(au[:, 0:2], 0.0), writes=[zB])

        def cw(chunk, tap):
            o = C_CW + (l * 44 + chunk) * 3 + tap
            return cst[:, o:o + 1]

        def cb(chunk):
            o = C_CB + l * 44 + chunk
            return cst[:, o:o + 1]

        def evac(bk, bkB, a, aB, t, tB, i, tt, chunk):
            c0 = 2 + 512 * tt
            S.op("act", lambda e: e.activation(out=a[:, c0:c0 + 512], in_=bk[:, :], func=AF.Copy), reads=[bkB], writes=[aB[tt]])
            S.op("act", lambda e: e.activation(out=t[:, i, :], in_=bk[:, :], func=AF.Identity, scale=cw(chunk, 2), bias=cb(chunk)),
                 reads=[bkB, cstB], writes=[tB[i]])

        def tap(a, aB, t, tB, i, tt, chunk, tp):
            c0 = 2 + 512 * tt
            lo = c0 - (2 - tp)
            rd = [aB[tt], tB[i], cstB] + ([aB[tt - 1]] if tt > 0 else [zB])
            S.op("dve", lambda e: e.scalar_tensor_tensor(out=t[:, i, :], in0=a[:, lo:lo + 512], scalar=cw(chunk, tp), in1=t[:, i, :],
                                                         op0=ALU.mult, op1=ALU.add), reads=rd, writes=[tB[i]])

        cnt = 0
        for G in range(2):
            pend = None
            for j in range(11):
                jj = G * 11 + j
                wt, wB = wnext(f"Wup{l}_{jj}")
                wu = wt[:, 0:2048].rearrange("p (k b c) -> p k b c", k=8, b=2)
                for tt in range(4):
                    i = cnt % R
                    cnt += 1
                    bg, bgB = bank()
                    bu, buB = bank()
                    for k in range(8):
                        mm(bg[:, :], wu[:, k, 0, :], yT[:, k, tsl(tt)], k == 0, k == 7, [yB[k][tt], wB], [bgB])
                    for k in range(8):
                        mm(bu[:, :], wu[:, k, 1, :], yT[:, k, tsl(tt)], k == 0, k == 7, [yB[k][tt], wB], [buB])
                    evac(bg, bgB, ag, agB, tg, tgB, i, tt, jj)
                    evac(bu, buB, au, auB, tu, tuB, i, tt, 22 + jj)
                    if pend is not None:
                        pi_, pj, ptt = pend
                        S.op("act", lambda e, pi_=pi_: e.activation(out=tg[:, pi_, :], in_=tg[:, pi_, :], func=AF.Gelu_apprx_tanh),
                             reads=[tgB[pi_]], writes=[tgB[pi_]])
                    tap(ag, agB, tg, tgB, i, tt, jj, 1)
                    tap(au, auB, tu, tuB, i, tt, 22 + jj, 1)
                    tap(ag, agB, tg, tgB, i, tt, jj, 0)
                    tap(au, auB, tu, tuB, i, tt, 22 + jj, 0)
                    if pend is not None:
                        pi_, pj, ptt = pend
                        S.op("dve", lambda e, pi_=pi_, pj=pj, ptt=ptt: e.tensor_tensor(out=actT[:, pj, tsl(ptt)], in0=tg[:, pi_, :],
                                                                                      in1=tu[:, pi_, :], op=ALU.mult),
                             reads=[tgB[pi_], tuB[pi_]], writes=[actB[pj][ptt]] + ([aliasB] if pj < 4 else []))
                    pend = (i, j, tt)
            pi_, pj, ptt = pend
            S.op("act", lambda e, pi_=pi_: e.activation(out=tg[:, pi_, :], in_=tg[:, pi_, :], func=AF.Gelu_apprx_tanh),
                 reads=[tgB[pi_]], writes=[tgB[pi_]])
            S.op("dve", lambda e, pi_=pi_, pj=pj, ptt=ptt: e.tensor_tensor(out=actT[:, pj, tsl(ptt)], in0=tg[:, pi_, :], in1=tu[:, pi_, :],
                                                                          op=ALU.mult),
                 reads=[tgB[pi_], tuB[pi_]], writes=[actB[pj][ptt]])
            for dc in range(8):
                wt, wB = wnext(f"Wd{l}_{G}_{dc}")
                wd = wt[:, 0:1408].rearrange("p (k c) -> p k c", k=11)
                for tt in range(4):
                    bk, bB = bank()
                    for k in range(11):
                        mm(bk[:, :], wd[:, k, :], actT[:, k, tsl(tt)], k == 0, k == 10, [actB[k][tt], wB] + ([aliasB] if k < 4 else []), [bB])
                    S.op("dve", lambda e, bk=bk, dc=dc, tt=tt: e.tensor_tensor(out=hT[:, dc, tsl(tt)], in0=bk[:, :], in1=hT[:, dc, tsl(tt)],
                                                                              op=ALU.add),
                         reads=[bB, hB[dc][tt]], writes=[hB[dc][tt]])

    def attn_phase():
        sb_off[0] = ARENA
        attnT = alloc("attnT", [128, 4, SEQ], BF16)
        cosT = alloc("cosT", [128, SEQ], F32)
        sinT = alloc("sinT", [128, SEQ], F32)
        qT = [alloc(f"qT{s}", [128, SEQ], BF16) for s in range(2)]
        kT = [alloc(f"kT{s}", [128, SEQ], BF16) for s in range(2)]
        vh = [alloc(f"vh{s}", [128, 16, 128], BF16) for s in range(2)]
        nm = [alloc(f"nm{s}", [128, 16, 8], BF16) for s in range(2)]
        pT = [alloc(f"pT{i}", [128, 2, 512], BF16) for i in range(2)]
        r1 = alloc("r1", [128, 2, 512], F32)
        r2 = alloc("r2", [128, 1, 512], F32)
        qb16 = alloc("qb16", [128, 2, 512], BF16)
        qb16B = [Buf("qb16_0"), Buf("qb16_1")]
        gm = alloc("gm", [128, 16, 8], F32)
        cmpt = alloc("cmpt", [128, 8, 8, 8], BF16)
        rank = alloc("rank", [128, 16, 8], F32)
        elig = alloc("elig", [128, 16, 8], BF16)
        negb = alloc("negb", [128, 16, 8], F32)
        km32 = alloc("km32", [128, 2, 8], F32)
        kmb = alloc("kmb", [128, 2, 8], BF16)
        rec = r2[:, 0, :]
        cm = alloc("cm", [128, 2, 256], BF16)
        negcm = alloc("negcm", [128, 2, 256], BF16)
        negcmB = Buf("negcm")
        ktmp = nc.alloc_sbuf_tensor_at("ktmp", [128, SEQ], I32, offset=(ARENA + 16384 + 16384 + 63) // 64 * 64)
        kf = nc.alloc_sbuf_tensor_at("kf", [128, SEQ], F32, offset=(ARENA + 16384 + 16384 + 8192 + 63) // 64 * 64)
        tpos = nc.alloc_sbuf_tensor_at("tpos", [128, SEQ], F32, offset=(ARENA + 16384 + 16384 + 16384 + 63) // 64 * 64)
        assert ARENA + 16384 * 3 + 8192 <= sb_off[0], "overlay scratch must stay inside arena"

        cosB, sinB, cmB, eligB, negbB = Buf("cos"), Buf("sin"), Buf("cm"), Buf("elig"), Buf("negb")
        attnB = [[Buf(f"at{k}_{q}") for q in range(4)] for k in range(4)]
        qB = [[Buf(f"q{s}_{t}") for t in range(4)] for s in range(2)]
        kB = [[Buf(f"k{s}_{t}") for t in range(4)] for s in range(2)]
        vhB = [[Buf(f"vh{s}_{t}") for t in range(4)] for s in range(2)]
        nmB = [Buf("nm0"), Buf("nm1")]
        pTB = [Buf(f"pT{i}") for i in range(2)]
        r1B = [Buf("r1_0"), Buf("r1_1")]
        r2B = [Buf("r2_0"), Buf("r2_0")]
        r2B[1] = r2B[0]
        gmB, cmpB, rankB = Buf("gm"), Buf("cmp"), Buf("rank")
        km32B = [Buf("km32_0"), Buf("km32_1")]
        kmbB = [Buf("kmb_0"), Buf("kmb_1")]
        recB = r2B[0]
        scr = Buf("scr")

        S.op("sp", lambda e: e.dma_start(out=sinT[:, :], in_=tab_d[0]), reads=[tabdB] + hB[7], writes=[sinB], dma_to=sinB)
        S.op("sp", lambda e: e.dma_start(out=cosT[:, :], in_=tab_d[1]), reads=[tabdB] + hB[7], writes=[cosB], dma_to=cosB)
        S.op("pool", lambda e: e.iota(kf[:, 0:512].rearrange("p (j i) -> p j i", j=2), pattern=[[-128, 2], [1, 256]], base=0,
                                      channel_multiplier=-1, allow_small_or_imprecise_dtypes=True), reads=[scr] + hB[7], writes=[scr])
        S.op("dve", lambda e: e.tensor_single_scalar(out=cm[:, :, :], in_=kf[:, 0:512].rearrange("p (j i) -> p j i", j=2),
                                                     scalar=0.0, op=ALU.is_ge), reads=[scr], writes=[cmB])
        S.op("dve", lambda e: e.tensor_scalar(out=negcm[:, :, :], in0=cm[:, :, :], scalar1=-1.0, scalar2=NEGM, op0=ALU.add, op1=ALU.mult),
             reads=[cmB], writes=[negcmB])
        S.op("pool", lambda e: e.iota(kf[:, 512:640].rearrange("p (a b) -> p a b", a=16), pattern=[[1, 16], [-2, 8]], base=-2,
                                      channel_multiplier=0, allow_small_or_imprecise_dtypes=True), reads=[scr] + hB[7], writes=[scr])
        S.op("dve", lambda e: e.tensor_single_scalar(out=elig[:, :, :], in_=kf[:, 512:640].rearrange("p (a b) -> p a b", a=16),
                                                     scalar=0.0, op=ALU.is_ge), reads=[scr], writes=[eligB])
        S.op("dve", lambda e: e.tensor_scalar(out=negb[:, :, :], in0=elig[:, :, :], scalar1=-1.0, scalar2=1e30, op0=ALU.add, op1=ALU.mult),
             reads=[eligB], writes=[negbB])
        fenceB = Buf("fence")
        S.op("dve", lambda e: e.memset(km32[:, 0, 0:1], 0.0), reads=[scr, cosB, sinB, cmB, negcmB, eligB, negbB], writes=[fenceB, scr])
        yield

        def proj(h):
            s = h % 2
            wt, wB = wnext(f"Wqk{h}")
            wqk = wt[:, 0:2048].rearrange("p (k b c) -> p k b c", k=8, b=2)
            tiles = [(w_, dst, dstB, tt) for (w_, dst, dstB) in ((0, qT[s], qB[s]), (1, kT[s], kB[s])) for tt in range(4)]
            pend = None

            def finish_tile(p):
                n, ba, baB, dst, dstB, tt = p
                i = n % 2
                bs, bsB = bank("proj4")
                mm(bs[:, :], permT[:, :], qb16[:, i, :], True, True, [permB, qb16B[i]], [bsB])
                S.op("dve", lambda e: e.tensor_tensor(out=r1[:, i, :], in0=ba[:, :], in1=cosT[:, tsl(tt)], op=ALU.mult),
                     reads=[baB, cosB, qb16B[i]], writes=[r1B[i]])
                S.op("dve", lambda e: e.tensor_tensor(out=r2[:, 0, :], in0=bs[:, :], in1=sinT[:, tsl(tt)], op=ALU.mult),
                     reads=[bsB, sinB], writes=[r2B[0]])
                S.op("pool", lambda e: e.tensor_tensor(out=dst[:, tsl(tt)], in0=r1[:, i, :], in1=r2[:, 0, :], op=ALU.add),
                     reads=[r1B[i], r2B[0], fenceB], writes=[dstB[tt]])

            for n, (w_, dst, dstB, tt) in enumerate(tiles):
                i = n % 2
                ba, baB = bank("proj4")
                for k in range(8):
                    mm(ba[:, :], wqk[:, k, w_, :], yT[:, k, tsl(tt)], k == 0, k == 7, [yB[k][tt], wB], [baB])
                S.op("act", lambda e, ba=ba, i=i: e.activation(out=qb16[:, i, :], in_=ba[:, :], func=AF.Copy),
                     reads=[baB], writes=[qb16B[i]])
                if pend is not None:
                    finish_tile(pend)
                pend = (n, ba, baB, dst, dstB, tt)
            finish_tile(pend)
            wt2, wB2 = wnext(f"Wvh{h}")
            wvh = wt2[:, 0:1024].rearrange("p (k c) -> p k c", k=8)
            for tq in range(4):
                bk, bB = bank("proj")
                for c4 in range(4):
                    tch = tq * 4 + c4
                    for k in range(8):
                        mm(bk[:, c4 * 128:(c4 + 1) * 128], yT[:, k, tch * 128:(tch + 1) * 128], wvh[:, k, :], k == 0, k == 7,
                           [yB[k][tq], wB2], [bB])
                S.op("act", lambda e, bk=bk, tq=tq, s=s: e.activation(
                    out=vh[s][:, tq * 4:(tq + 1) * 4, :].rearrange("p a c -> p (a c)"), in_=bk[:, :], func=AF.Copy),
                    reads=[bB, fenceB], writes=[vhB[s][tq]])
            S.op("dve", lambda e, s=s: e.reduce_sum(out=km32[:, s, :], in_=kT[s][:, :].rearrange("p (n t) -> p n t", n=8), axis=AX.X),
                 reads=kB[s], writes=[km32B[s]])
            S.op("dve", lambda e, s=s: e.tensor_scalar(out=kmb[:, s, :], in0=km32[:, s, :], scalar1=1.0 / 256, scalar2=None, op0=ALU.mult),
                 reads=[km32B[s]], writes=[kmbB[s]])
            pending_gates.append(h)

        def gate(h):
            s = h % 2
            b0 = spair()
            bk, bB = pbank[b0], pbB[b0]
            for qt in range(16):
                mm(bk[:, qt * 8:(qt + 1) * 8], qT[s][:, qt * 128:(qt + 1) * 128], kmb[:, s, :], True, True, [qB[s][qt // 4], kmbB[s]], [bB])
            S.op("dve", lambda e, bk=bk: e.tensor_tensor(out=gm[:, :, :], in0=bk[:, 0:128].rearrange("p (a b) -> p a b", a=16),
                                                        in1=negb[:, :, :], op=ALU.add), reads=[bB, negbB], writes=[gmB])
            for hf in range(2):
                hsl = slice(hf * 8, (hf + 1) * 8)
                S.op("dve", lambda e, hsl=hsl: e.tensor_tensor(out=cmpt[:, :, :, :], in0=gm[:, hsl, :].unsqueeze(2).to_broadcast([128, 8, 8, 8]),
                                                               in1=gm[:, hsl, :].unsqueeze(3).to_broadcast([128, 8, 8, 8]), op=ALU.is_gt),
                     reads=[gmB], writes=[cmpB])
                S.op("dve", lambda e, hsl=hsl: e.reduce_sum(out=rank[:, hsl, :], in_=cmpt[:, :, :, :], axis=AX.X), reads=[cmpB], writes=[rankB])
            S.op("dve", lambda e: e.scalar_tensor_tensor(out=rank[:, :, :], in0=rank[:, :, :], scalar=3.0, in1=elig[:, :, :],
                                                         op0=ALU.is_lt, op1=ALU.mult), reads=[rankB, eligB], writes=[rankB])
            S.op("dve", lambda e, s=s: e.tensor_scalar(out=nm[s][:, :, :], in0=rank[:, :, :], scalar1=-1.0, scalar2=NEGM,
                                                      op0=ALU.add, op1=ALU.mult), reads=[rankB], writes=[nmB[s]])

        pti = [0]
        pending_gates = []
        scale = float(128 ** -0.5)

        spi = [0]

        def spair():
            b0 = (0, 2)[spi[0] % 2]
            spi[0] += 1
            return b0

        def attend(h):
            s = h % 2
            hh = h % 4
            items = []
            for t in range(4):
                nblk = 2 * t + 2
                for n in range(nblk):
                    items.append(dict(t=t, n=n, first=(n == 0), last=(n == nblk - 1)))
            banks = {}

            def front(it):
                t, n = it["t"], it["n"]
                c0 = 256 if n == 2 * t + 1 else 0
                W = 512 - c0
                b0 = spair()
                masks = []
                for qh in range(c0 // 128, 4):
                    qb_q = 2 * t + qh // 2
                    if n < qb_q and qb_q >= 4:
                        masks.append(qh)
                own = n >= 2 * t
                hs = slice((n - 2 * t) * 256, (n - 2 * t) * 256 + 256) if own else None
                for j in range(2):
                    kt = 2 * n + j
                    sp_, spB = pbank[b0 + j], pbB[b0 + j]
                    extra = []
                    for qh in masks:
                        qt = 4 * t + qh
                        extra.append((sp_[:, qh * 128:(qh + 1) * 128], nm[s][:, qt, n:n + 1].to_broadcast([128, 128]), ident[:, :],
                                      [nmB[s], identB]))
                    if own:
                        extra.append((sp_[:, hs], ident[:, :], negcm[:, j, :], [identB, negcmB]))
                    mm(sp_[:, c0:512], kT[s][:, kt * 128:(kt + 1) * 128], qT[s][:, t * 512 + c0:(t + 1) * 512], True, len(extra) == 0,
                       [kB[s][kt // 4], qB[s][t]], [spB])
                    for mi, (o_, l_, r_, rd_) in enumerate(extra):
                        mm(o_, l_, r_, False, mi == len(extra) - 1, rd_, [spB])
                pi = pti[0] % 2
                pti[0] += 1
                src = pball[:, b0 * 512:(b0 + 2) * 512].rearrange("p (j c) -> p j c", j=2)
                S.op("act", lambda e: e.activation(out=pT[pi][:, :, c0:512], in_=src[:, :, c0:512], func=AF.Exp, scale=scale),
                     reads=[pbB[b0], pbB[b0 + 1]], writes=[pTB[pi]])
                it["pi"] = pi
                it["c0"] = c0

            def back(it):
                t, n, pi, c0 = it["t"], it["n"], it["pi"], it["c0"]
                if it["first"]:
                    banks[t] = (bank("pv"), bank("dn"))
                (pv, pvB), (dn, dnB) = banks[t]
                for j in range(2):
                    kt = 2 * n + j
                    fst = it["first"] and j == 0
                    lst = it["last"] and j == 1
                    mm(pv[:, c0:512], vh[s][:, kt, :], pT[pi][:, j, c0:512], fst, lst, [vhB[s][kt // 4], pTB[pi]], [pvB], skip=True)
                    mm(dn[:, c0:512], ones[:, :], pT[pi][:, j, c0:512], fst, lst, [onesB, pTB[pi]], [dnB], skip=True)
                if it["last"]:
                    qs = slice(t * 512, (t + 1) * 512)
                    S.op("dve", lambda e: e.reciprocal(out=rec, in_=dn[:, :]), reads=[dnB], writes=[recB])
                    S.op("dve", lambda e: e.tensor_tensor(out=attnT[:, hh, qs], in0=pv[:, :], in1=rec, op=ALU.mult),
                         reads=[pvB, recB], writes=[attnB[hh][t], aliasB])

            front(items[0])
            for i, it in enumerate(items):
                if i + 1 < len(items):
                    front(items[i + 1])
                back(it)
                if it["last"] and it["t"] == 0:
                    for g in [g for g in pending_gates if g <= h]:
                        pending_gates.remove(g)
                        gate(g)
                if it["last"] and it["t"] == 2:
                    while pending_gates:
                        gate(pending_gates.pop(0))

        def wo(hg):
            for dc in range(8):
                wt, wB = wnext(f"Wob{hg}_{dc}")
                wo_ = wt[:, 0:512].rearrange("p (k c) -> p k c", k=4)
                for tt in range(4):
                    bk, bB = bank("proj")
                    for k in range(4):
                        mm(bk[:, :], wo_[:, k, :], attnT[:, k, tsl(tt)], k == 0, k == 3, [attnB[k][tt], wB, aliasB], [bB])
                    S.op("dve", lambda e, bk=bk, dc=dc, tt=tt: e.tensor_tensor(out=hT[:, dc, tsl(tt)], in0=bk[:, :], in1=hT[:, dc, tsl(tt)],
                                                                              op=ALU.add),
                         reads=[bB, hB[dc][tt]], writes=[hB[dc][tt]])

        for it in attn_order():
            k = int(it[1])
            if it[0] == "p":
                proj(k)
            elif it[0] == "a":
                attend(k)
            else:
                wo(k)

    def finish():
        if stop_after != "final":
            for c in range(8):
                S.op("sp", lambda e, c=c: e.dma_start(out=out_d[c], in_=hT[:, c, :]), reads=hB[c], dma_to=outB)
        S.emit(nc, final_waits=[outB])
        return nc

    norm_phase(0, nobar=True)
    gmlp_phase()
    if stop_after == "gmlp":
        return finish()
    norm_phase(2, nobar=True)
    ffn_phase(0)
    if stop_after == "ffn0":
        return finish()
    attn_gen = attn_phase()
    norm_phase(1, pre=lambda: next(attn_gen), nobar=True)
    next(attn_gen, None)
    if stop_after == "attn":
        return finish()
    norm_phase(3, nobar=True)
    ffn_phase(1)
    if stop_after == "ffn1":
        return finish()
    norm_phase(4, final=True, nobar=True)
    return finish()


def _kpc(w, nk):
    C = w.shape[1]
    return np.ascontiguousarray(w.reshape(nk, 128, C).transpose(1, 0, 2))


def pack_weights(a_w_in, a_w_out, b_w_qkv, b_w_o, ffn_w_up, ffn_w_down):
    offs, WTOT = slab_offsets()
    wts = np.empty((128, WTOT), np.float32)

    def put(name, arr):
        o, n = offs[name]
        wts[:, o:o + n] = arr.reshape(128, n)

    w_in = a_w_in[0]
    for vb in range(4):
        put(f"Wv{vb}", _kpc(w_in[:, 2048 + vb * 512:2048 + (vb + 1) * 512], 8))
    for fc in range(16):
        put(f"Wu{fc}", _kpc(w_in[:, fc * 128:(fc + 1) * 128], 8))
    w_out = a_w_out[0]
    for dc in range(8):
        put(f"Wo{dc}", _kpc(w_out[:, dc * 128:(dc + 1) * 128], 16))
    for l in range(2):
        wu = ffn_w_up[l]
        for jj in range(22):
            g = _kpc(wu[:, jj * 128:(jj + 1) * 128], 8)
            u = _kpc(wu[:, DFF + jj * 128:DFF + (jj + 1) * 128], 8)
            put(f"Wup{l}_{jj}", np.stack([g, u], axis=2))
        wd = ffn_w_down[l]
        for G in range(2):
            for dc in range(8):
                put(f"Wd{l}_{G}_{dc}", _kpc(wd[G * 1408:(G + 1) * 1408, dc * 128:(dc + 1) * 128], 11))
    wqkv = b_w_qkv[0]
    perm = (np.arange(128) + 64) % 128
    for h in range(8):
        q = wqkv[:, h * 128:(h + 1) * 128]
        k = wqkv[:, 1024 + h * 128:1024 + (h + 1) * 128]
        v = wqkv[:, 2048 + h * 128:2048 + (h + 1) * 128]
        put(f"Wqk{h}", np.stack([_kpc(q, 8), _kpc(k, 8)], axis=2))
        put(f"Wvh{h}", _kpc(v, 8))
    w_o = b_w_o[0]
    for hg in range(2):
        for dc in range(8):
            put(f"Wob{hg}_{dc}", _kpc(w_o[hg * 512:(hg + 1) * 512, dc * 128:(dc + 1) * 128], 4))
    return wts


def pack_consts(mix_norm, ffn_norm, final_norm, a_v_gain, a_w_s, a_b_s, ffn_conv_w, ffn_conv_b):
    c = np.zeros((128, NCONST), np.float32)
    gains = [mix_norm[0], mix_norm[1], ffn_norm[0], ffn_norm[1], final_norm]
    for i, g in enumerate(gains):
        c[:, C_GAIN + i * 8:C_GAIN + (i + 1) * 8] = g.reshape(8, 128).T
    c[:, C_VG:C_VG + 16] = a_v_gain[0].reshape(16, 128).T
    for l in range(2):
        cw = ffn_conv_w[l].reshape(3, 44, 128)
        c[:, C_CW + l * 132:C_CW + (l + 1) * 132] = cw.transpose(2, 1, 0).reshape(128, 132)
        c[:, C_CB + l * 44:C_CB + (l + 1) * 44] = ffn_conv_b[l].reshape(44, 128).T
    ws = a_w_s[0]
    c[:, C_WS:C_WS + 1024] = ws.transpose(2, 0, 1).reshape(128, 1024)
    c[:, C_BB:C_BB + 1024] = np.broadcast_to(a_b_s[0].reshape(1, 1024), (128, 1024))
    return c


_NC_CACHE = {}


def kernel(x, mix_norm, a_w_in, a_v_gain, a_w_s, a_b_s, a_w_out, b_w_qkv, b_w_o,
           ffn_norm, ffn_w_up, ffn_conv_w, ffn_conv_b, ffn_w_down, final_norm, _stop_after="final", _cores=8):
    f = lambda a: np.asarray(a, dtype=np.float32)
    x = f(x)
    wts = pack_weights(f(a_w_in), f(a_w_out), f(b_w_qkv), f(b_w_o), f(ffn_w_up), f(ffn_w_down))
    consts = pack_consts(f(mix_norm), f(ffn_norm), f(final_norm), f(a_v_gain), f(a_w_s), f(a_b_s), f(ffn_conv_w), f(ffn_conv_b))
    nc = build_nc(_stop_after)
    in_maps = []
    for b in range(_cores):
        xT = np.ascontiguousarray(x[b].T).reshape(8, 128, SEQ)
        in_maps.append({"xT": xT, "consts": consts, "wts": wts})
    res = run_bass_kernel_spmd(nc, in_maps, core_ids=list(range(_cores)))
    outs = []
    for b in range(_cores):
        oT = res.results[b]["outT"].reshape(D, SEQ)
        outs.append(np.ascontiguousarray(oT.T))
    return np.stack(outs, axis=0).astype(np.float32)
```

```python
import os
from contextlib import ExitStack

import numpy as np
import concourse.bass as bass
import concourse.mybir as mybir
from concourse.bass_utils import run_bass_kernel_spmd

F32 = mybir.dt.float32
BF16 = mybir.dt.bfloat16
I32 = mybir.dt.int32
AF = mybir.ActivationFunctionType
ALU = mybir.AluOpType
AX = mybir.AxisListType

D = 1024
SEQ = 2048
NB = 8
DFF = 2816
EPS = 1e-6
ROPE_THETA = 10000.0
NEGM = 30000.0

C_GAIN = 0
C_VG = 40
C_CW = 56
C_CB = C_CW + 264
C_WS = C_CB + 88
C_BB = C_WS + 1024
NCONST = C_BB + 1024


class Buf:
    __slots__ = ("name", "w", "r", "ndma", "sem")

    def __init__(self, name):
        self.name = name
        self.w = None
        self.r = {}
        self.ndma = 0
        self.sem = None


class Op:
    __slots__ = ("eng", "fn", "deps", "dmadeps", "marked", "seq", "dma_buf", "dma_cnt", "pos")

    def __init__(self, eng, fn):
        self.eng = eng
        self.fn = fn
        self.deps = []
        self.dmadeps = {}
        self.marked = False
        self.seq = 0
        self.dma_buf = None
        self.dma_cnt = 0


class Sched:
    ENGS = ("pe", "act", "dve", "pool", "sp")

    def __init__(self):
        self.q = {e: [] for e in self.ENGS}
        self.dma_bufs = []
        self.bar = None
        self.bar_done = set()

    def _dep(self, op, prev):
        if prev is None or prev is op:
            return
        if prev.dma_buf is not None:
            b = prev.dma_buf
            op.dmadeps[b] = max(op.dmadeps.get(b, 0), b.ndma)
            return
        if prev.eng == "pe" and op.eng == "pe":
            return
        op.deps.append(prev)
        prev.marked = True

    def barrier(self):
        self.bar = [self.q[e][-1] for e in self.ENGS if self.q[e]]
        self.bar_done = set()

    def op(self, eng, fn, reads=(), writes=(), dma_to=None, nobarrier=False):
        o = Op(eng, fn)
        if self.bar is not None and not nobarrier and eng not in self.bar_done:
            for p in self.bar:
                self._dep(o, p)
            self.bar_done.add(eng)
        for b in reads:
            self._dep(o, b.w)
        for b in writes:
            self._dep(o, b.w)
            for r in b.r.values():
                self._dep(o, r)
        key = eng if dma_to is None else ("dma", dma_to.name, len(self.q[eng]))
        for b in reads:
            b.r[key] = o
        for b in writes:
            b.w = o
            b.r = {}
        if dma_to is not None:
            dma_to.ndma += 1
            o.dma_buf = dma_to
            o.dma_cnt = dma_to.ndma
            if dma_to not in self.dma_bufs:
                self.dma_bufs.append(dma_to)
        o.pos = len(self.q[eng])
        self.q[eng].append(o)
        return o

    def emit(self, nc, final_waits=()):
        with ExitStack() as es:
            sems = {e: es.enter_context(nc.semaphore("s_" + e)) for e in ("pe", "act", "dve", "pool")}
            for b in self.dma_bufs:
                b.sem = es.enter_context(nc.semaphore("d_" + b.name))
            for e in self.ENGS:
                c = 0
                for o in self.q[e]:
                    if o.marked and o.dma_buf is None:
                        c += 1
                        o.seq = c
            block = es.enter_context(nc.Block())
            sched = self

            def run(ename, eng):
                seen = {}
                for o in sched.q[ename]:
                    need = {}
                    for d in o.deps:
                        if need.get(d.eng, 0) < d.seq:
                            need[d.eng] = d.seq
                    for k, v in need.items():
                        if seen.get(k, 0) < v:
                            eng.wait_ge(sems[k], v)
                            seen[k] = v
                    for b, cnt in o.dmadeps.items():
                        if seen.get(b, 0) < cnt:
                            eng.wait_ge(b.sem, 16 * cnt)
                            seen[b] = cnt
                    ins = o.fn(eng)
                    if o.dma_buf is not None:
                        ins.then_inc(o.dma_buf.sem, 16)
                    elif o.marked:
                        ins.then_inc(sems[ename], 1)
                if ename == "sp":
                    for b in final_waits:
                        eng.wait_ge(b.sem, 16 * b.ndma)

            @block.tensor
            def _(eng):
                run("pe", eng)

            @block.scalar
            def _(eng):
                run("act", eng)

            @block.vector
            def _(eng):
                run("dve", eng)

            @block.gpsimd
            def _(eng):
                run("pool", eng)

            @block.sync
            def _(eng):
                run("sp", eng)


def slab_defs():
    d = {}
    for vb in range(4):
        d[f"Wv{vb}"] = 8 * 512
    for fc in range(16):
        d[f"Wu{fc}"] = 8 * 128
    for dc in range(8):
        d[f"Wo{dc}"] = 16 * 128
    for l in range(2):
        for jj in range(22):
            d[f"Wup{l}_{jj}"] = 8 * 2 * 128
        for G in range(2):
            for dc in range(8):
                d[f"Wd{l}_{G}_{dc}"] = 11 * 128
    for h in range(8):
        d[f"Wqk{h}"] = 8 * 2 * 128
        d[f"Wvh{h}"] = 8 * 128
    for hg in range(2):
        for dc in range(8):
            d[f"Wob{hg}_{dc}"] = 4 * 128
    return d


def slab_offsets():
    offs = {}
    o = 0
    for k, n in slab_defs().items():
        offs[k] = (o, n)
        o += n
    return offs, o


def attn_order():
    return ["p0", "p1", "a0", "p2", "a1", "p3", "a2", "p4", "a3", "w0", "p5", "a4", "p6", "a5", "p7", "a6", "a7", "w1"]


def stream_order(stop_after):
    seq = []
    for half in range(2):
        seq += [f"Wv{vb}" for vb in range(4)]
        seq += [f"Wu{fc}" for fc in range(16)]
        seq += [f"Wo{dc}" for dc in range(8)]
    if stop_after == "gmlp":
        return seq

    def ffn(l):
        s = []
        for G in range(2):
            s += [f"Wup{l}_{G * 11 + j}" for j in range(11)]
            s += [f"Wd{l}_{G}_{dc}" for dc in range(8)]
        return s

    seq += ffn(0)
    if stop_after == "ffn0":
        return seq
    for it in attn_order():
        k = int(it[1])
        if it[0] == "p":
            seq += [f"Wqk{k}", f"Wvh{k}"]
        elif it[0] == "w":
            seq += [f"Wob{k}_{dc}" for dc in range(8)]
    if stop_after == "attn":
        return seq
    seq += ffn(1)
    return seq


def build_nc(stop_after="final"):
    nc = bass.Bass("TRN2", target_bir_lowering=False)
    offs, WTOT = slab_offsets()
    xT_d = nc.dram_tensor("xT", [8, 128, SEQ], F32, kind="ExternalInput").ap()
    consts_d = nc.dram_tensor("consts", [128, NCONST], F32, kind="ExternalInput").ap()
    wts_d = nc.dram_tensor("wts", [128, WTOT], F32, kind="ExternalInput").ap()
    out_d = nc.dram_tensor("outT", [8, 128, SEQ], F32, kind="ExternalOutput").ap()

    S = Sched()
    sb_off = [16512]
    SB_LIMIT = 227328

    def alloc(name, shape, dt, at=None):
        esz = 2 if dt == BF16 else 4
        n = esz
        for s_ in shape[1:]:
            n *= s_
        off = sb_off[0] if at is None else at
        off = (off + 63) // 64 * 64
        assert off + n <= SB_LIMIT, (name, off, n)
        t = nc.alloc_sbuf_tensor_at(name, list(shape), dt, offset=off)
        if at is None:
            sb_off[0] = off + n
        return t

    hT = alloc("hT", [128, 8, SEQ], F32)
    yT = alloc("yT", [128, 8, SEQ], BF16)
    cst = alloc("cst", [128, NCONST], F32)
    ident = alloc("ident", [128, 128], BF16)
    ones = alloc("ones", [128, 128], BF16)
    epst = alloc("epst", [128, 1], F32)
    permT = alloc("permT", [128, 128], BF16)
    wsb = alloc("wsb", [128, 8, 128], BF16)
    wslots = [alloc(f"wslot{i}", [128, 4096], BF16) for i in range(3)]
    ARENA = sb_off[0]

    hB = [[Buf(f"h{c}_{t}") for t in range(4)] for c in range(8)]
    yB = [[Buf(f"y{c}_{t}") for t in range(4)] for c in range(8)]
    cstB, identB, onesB, epsB, wsbB = Buf("cst"), Buf("ident"), Buf("ones"), Buf("eps"), Buf("wsb")
    wslotB = [Buf(f"wslot{i}") for i in range(3)]
    outB = Buf("out")
    xinB = [Buf(f"xin{c}") for c in range(8)]
    aliasB = Buf("alias")

    pball = nc.alloc_psum_tensor("pball", [128, 4096], F32)
    pbank = [pball[:, i * 512:(i + 1) * 512] for i in range(8)]
    pbB = [Buf(f"pb{i}") for i in range(8)]
    rot = {"all": [0, list(range(8))], "proj": [0, [0, 1, 2, 3, 4, 5, 6, 7]], "proj4": [0, [0, 1, 2, 3, 4, 5, 6, 7]], "s": [0, [2, 3, 0]], "pv": [0, [4, 6]], "dn": [0, [5, 7]]}

    def bank(kind="all"):
        r = rot[kind]
        i = r[1][r[0] % len(r[1])]
        r[0] += 1
        return pbank[i], pbB[i]

    order = stream_order(stop_after)
    wstate = {"issued": 0, "taken": 0}

    def w_issue_upto(n):
        while wstate["issued"] < min(n, len(order)):
            i = wstate["issued"]
            name = order[i]
            o, ne = offs[name]
            t, B = wslots[i % 3], wslotB[i % 3]
            S.op("pool", lambda e, t=t, o=o, ne=ne: e.dma_start(out=t[:, 0:ne], in_=wts_d[:, o:o + ne]),
                 writes=[B], dma_to=B, nobarrier=True)
            wstate["issued"] += 1

    def wnext(name):
        i = wstate["taken"]
        assert order[i] == name, (order[i], name)
        w_issue_upto(i + 3)
        wstate["taken"] += 1
        return wslots[i % 3], wslotB[i % 3]

    def mm(out_ap, lhsT, rhs, start, stop, reads, writes, skip=False):
        S.op("pe", lambda e: e.matmul(out_ap, lhsT=lhsT, rhs=rhs, start=start, stop=stop, skip_group_check=skip),
             reads=reads, writes=writes)

    def tsl(tt):
        return slice(tt * 512, (tt + 1) * 512)

    def emit_tables(sinT, cosT, sinB, cosB, ktmp, kf, tpos, pcol, scr):
        S.op("pool", lambda e: e.iota(pcol[:, 0:1], pattern=[[0, 1]], base=0, channel_multiplier=1,
                                      allow_small_or_imprecise_dtypes=True), writes=[scr])
        S.op("dve", lambda e: e.tensor_single_scalar(out=pcol[:, 1:2], in_=pcol[:, 0:1], scalar=64.0, op=ALU.is_ge), reads=[scr], writes=[scr])
        S.op("dve", lambda e: e.scalar_tensor_tensor(out=pcol[:, 2:3], in0=pcol[:, 1:2], scalar=-64.0, in1=pcol[:, 0:1],
                                                     op0=ALU.mult, op1=ALU.add), reads=[scr], writes=[scr])
        S.op("dve", lambda e: e.tensor_scalar(out=pcol[:, 3:4], in0=pcol[:, 1:2], scalar1=2.0, scalar2=-1.0, op0=ALU.mult, op1=ALU.add),
             reads=[scr], writes=[scr])
        S.op("act", lambda e: e.activation(out=pcol[:, 2:3], in_=pcol[:, 2:3], func=AF.Exp, scale=-float(np.log(ROPE_THETA)) / 64.0),
             reads=[scr], writes=[scr])
        S.op("pool", lambda e: e.iota(tpos[:, :], pattern=[[1, SEQ]], base=0, channel_multiplier=0,
                                      allow_small_or_imprecise_dtypes=True), writes=[scr])
        C1 = 6.28125
        C2 = float(2 * np.pi - C1)

        def table(dst, dstB, shift, signed):
            S.op("dve", lambda e: e.tensor_scalar(out=dst[:, :], in0=tpos[:, :], scalar1=pcol[:, 2:3], scalar2=float(shift),
                                                  op0=ALU.mult, op1=ALU.add), reads=[scr], writes=[dstB])
            S.op("dve", lambda e: e.tensor_scalar(out=ktmp[:, :], in0=dst[:, :], scalar1=float(1.0 / (2 * np.pi)), scalar2=None,
                                                  op0=ALU.mult), reads=[dstB], writes=[scr])
            S.op("dve", lambda e: e.tensor_copy(out=kf[:, :], in_=ktmp[:, :]), reads=[scr], writes=[scr])
            S.op("dve", lambda e: e.scalar_tensor_tensor(out=dst[:, :], in0=kf[:, :], scalar=-C1, in1=dst[:, :],
                                                         op0=ALU.mult, op1=ALU.add), reads=[scr, dstB], writes=[dstB])
            S.op("dve", lambda e: e.scalar_tensor_tensor(out=dst[:, :], in0=kf[:, :], scalar=-C2, in1=dst[:, :],
                                                         op0=ALU.mult, op1=ALU.add), reads=[scr, dstB], writes=[dstB])
            S.op("dve", lambda e: e.tensor_scalar(out=dst[:, :], in0=dst[:, :], scalar1=-3.1415925, scalar2=3.1415925,
                                                  op0=ALU.max, op1=ALU.min), reads=[dstB], writes=[dstB])
            S.op("act", lambda e: e.activation(out=dst[:, :], in_=dst[:, :], func=AF.Sin), reads=[dstB], writes=[dstB])
            if signed:
                S.op("dve", lambda e: e.tensor_scalar(out=dst[:, :], in0=dst[:, :], scalar1=pcol[:, 3:4], scalar2=None, op0=ALU.mult),
                     reads=[dstB, scr], writes=[dstB])

        table(sinT, sinB, 0.0, True)
        table(cosT, cosB, float(np.pi / 2), False)

    S.op("sp", lambda e: e.dma_start(out=cst[:, :], in_=consts_d), writes=[cstB], dma_to=cstB)
    for tt in range(4):
        for cg in range(2):
            S.op("sp", lambda e, tt=tt, cg=cg: e.dma_start(
                out=hT[:, cg * 4:(cg + 1) * 4, tsl(tt)],
                in_=xT_d.rearrange("c p t -> p c t")[:, cg * 4:(cg + 1) * 4, tsl(tt)]),
                writes=[hB[c][tt] for c in range(cg * 4, (cg + 1) * 4)], dma_to=xinB[tt * 2 + cg])
    w_issue_upto(3)

    sb_off[0] = ARENA + 61440
    io_f = alloc("io_f", [128, 256], F32)
    ioB = Buf("io")
    S.op("pool", lambda e: e.iota(io_f[:, :], pattern=[[1, 256]], base=0, channel_multiplier=-1,
                                  allow_small_or_imprecise_dtypes=True), writes=[ioB])
    S.op("dve", lambda e: e.tensor_single_scalar(out=ident[:, :], in_=io_f[:, 0:128], scalar=0.0, op=ALU.is_equal),
         reads=[ioB], writes=[identB])
    S.op("dve", lambda e: e.memset(ones[:, :], 1.0), writes=[onesB])
    iosq = alloc("iosq", [128, 128], F32)
    iosqB, permB = Buf("iosq"), Buf("perm")
    S.op("dve", lambda e: e.tensor_tensor(out=iosq[:, :], in0=io_f[:, 0:128], in1=io_f[:, 0:128], op=ALU.mult), reads=[ioB], writes=[iosqB])
    S.op("dve", lambda e: e.tensor_single_scalar(out=permT[:, :], in_=iosq[:, :], scalar=4096.0, op=ALU.is_equal),
         reads=[iosqB], writes=[permB])
    S.op("dve", lambda e: e.memset(epst[:, :], EPS), writes=[epsB])
    msk = alloc("msk", [128, 128], F32)
    mskB = Buf("msk")
    S.op("dve", lambda e: e.tensor_single_scalar(out=msk[:, :], in_=io_f[:, 0:128], scalar=0.0, op=ALU.is_ge),
         reads=[ioB], writes=[mskB])
    S.op("dve", lambda e: e.tensor_tensor(
        out=wsb[:, :, :], in0=cst[:, C_WS:C_WS + 1024].rearrange("p (g t) -> p g t", g=8),
        in1=msk[:, :].unsqueeze(1).to_broadcast([128, 8, 128]), op=ALU.mult),
        reads=[cstB, mskB], writes=[wsbB])

    tab_d = nc.dram_tensor("ropetab", [2, 128, SEQ], F32, kind="Internal").ap()
    tabdB = Buf("tabd")
    if stop_after not in ("gmlp", "ffn0"):
        sb_off[0] = ARENA + 16384
        s_sin = alloc("s_sin", [128, SEQ], F32)
        s_cos = alloc("s_cos", [128, SEQ], F32)
        s_ktmp = alloc("s_ktmp", [128, SEQ], I32)
        s_kf = alloc("s_kf", [128, SEQ], F32)
        s_tpos = alloc("s_tpos", [128, SEQ], F32)
        s_pcol = alloc("s_pcol", [128, 4], F32)
        s_sinB, s_cosB, s_scr = Buf("s_sin"), Buf("s_cos"), Buf("s_scr")
        emit_tables(s_sin, s_cos, s_sinB, s_cosB, s_ktmp, s_kf, s_tpos, s_pcol, s_scr)
        S.op("sp", lambda e: e.dma_start(out=tab_d[0], in_=s_sin[:, :]), reads=[s_sinB], writes=[tabdB], dma_to=tabdB)
        S.op("sp", lambda e: e.dma_start(out=tab_d[1], in_=s_cos[:, :]), reads=[s_cosB], writes=[tabdB], dma_to=tabdB)

    def norm_phase(gi, final=False, pre=None, nobar=False):
        if not nobar:
            S.barrier()
        if pre is not None:
            pre()
        sb_off[0] = ARENA
        rs2 = alloc(f"rs{gi}", [128, 2, 512], F32)
        sqb = alloc(f"sqb{gi}", [128, 12, 512], BF16)
        rsB2 = [Buf(f"rs{gi}_{t}") for t in range(2)]
        rsB = [rsB2[t % 2] for t in range(4)]
        sqB = [Buf(f"sq{gi}_{i}") for i in range(12)]

        def rs_(tt):
            return rs2[:, tt % 2, :]
        sqi = [0, 0]
        def y_ops(tt):
            for c in range(8):
                gcol = cst[:, C_GAIN + gi * 8 + c:C_GAIN + gi * 8 + c + 1]
                if final:
                    S.op("dve", lambda e, c=c, gcol=gcol: e.scalar_tensor_tensor(
                        out=hT[:, c, tsl(tt)], in0=hT[:, c, tsl(tt)], scalar=gcol, in1=rs_(tt),
                        op0=ALU.mult, op1=ALU.mult), reads=[hB[c][tt], rsB[tt], cstB], writes=[hB[c][tt]])
                else:
                    S.op("dve", lambda e, c=c, gcol=gcol: e.scalar_tensor_tensor(
                        out=yT[:, c, tsl(tt)], in0=hT[:, c, tsl(tt)], scalar=gcol, in1=rs_(tt),
                        op0=ALU.mult, op1=ALU.mult), reads=[hB[c][tt], rsB[tt], cstB, aliasB], writes=[yB[c][tt]])
                if final and c % 4 == 3:
                    c0 = c - 3
                    S.op("sp", lambda e, c0=c0: e.dma_start(out=out_d.rearrange("c p t -> p c t")[:, c0:c0 + 4, tsl(tt)],
                                                            in_=hT[:, c0:c0 + 4, tsl(tt)]),
                         reads=[hB[cc][tt] for cc in range(c0, c0 + 4)], dma_to=outB)

        for tt in range(4):
            for c in range(8):
                al = [aliasB] if (tt == 0 and c in (0, 3)) else []
                if c != 3:
                    sl = sqi[0] % 10
                    sqi[0] += 1
                    S.op("act", lambda e, c=c, tt=tt, sl=sl: e.activation(out=sqb[:, sl, :], in_=hT[:, c, tsl(tt)], func=AF.Square),
                         reads=[hB[c][tt]], writes=[sqB[sl]] + al)
                else:
                    sl = 10 + sqi[1] % 2
                    sqi[1] += 1
                    S.op("dve", lambda e, c=c, tt=tt, sl=sl: e.tensor_tensor(out=sqb[:, sl, :], in0=hT[:, c, tsl(tt)], in1=hT[:, c, tsl(tt)],
                                                                            op=ALU.mult),
                         reads=[hB[c][tt]], writes=[sqB[sl]] + al)
                mm(pbank[tt][:, :], ones[:, :], sqb[:, sl, :], c == 0, c == 7, [sqB[sl], onesB], [pbB[tt]])
            S.op("act", lambda e, tt=tt: e.activation(out=rs_(tt), in_=pbank[tt][:, :], func=AF.Ln,
                                                     scale=1.0 / D, bias=epst[:, :]),
                 reads=[pbB[tt], epsB], writes=[rsB[tt]] + ([aliasB] if tt == 0 else []))
            S.op("act", lambda e, tt=tt: e.activation(out=rs_(tt), in_=rs_(tt), func=AF.Exp, scale=-0.5),
                 reads=[rsB[tt]], writes=[rsB[tt]])
            if tt >= 1:
                y_ops(tt - 1)
        y_ops(3)

    def gmlp_phase():
        sb_off[0] = ARENA
        uT = alloc("uT", [128, 16, 1024], BF16)
        vtm = alloc("vtm", [128, 8, 2048], BF16)
        junk = alloc("junk", [128, 512], BF16)
        t1 = alloc("t1", [128, 2, 512], F32)
        ssq = alloc("ssq", [128, 8, 4], F32)
        ss = alloc("ss", [128, 8], F32)
        rv = alloc("rv", [128, 8], F32)
        uB = [[Buf(f"u{c}_{t}") for t in range(2)] for c in range(16)]
        vB = [[Buf(f"v{c}_{b}") for b in range(4)] for c in range(8)]
        junkB, ssB, rvB = Buf("junk"), Buf("ss"), Buf("rv")
        t1B = [Buf("t1_0"), Buf("t1_1")]
        ssqB = [Buf(f"ssq{c}") for c in range(8)]
        t1i = [0]
        for half in range(2):
            T0 = half * 1024
            for ch in range(8):
                S.op("dve", lambda e, ch=ch: e.memset(ssq[:, ch, :], 0.0), writes=[ssqB[ch]])
            for vb in range(4):
                wt, wB = wnext(f"Wv{vb}")
                wv = wt[:, 0:4096].rearrange("p (k c) -> p k c", k=8)
                for ch in range(8):
                    t0 = T0 + ch * 128
                    bk, bB = bank()
                    for k in range(8):
                        mm(bk[:, :], yT[:, k, t0:t0 + 128], wv[:, k, :], k == 0, k == 7, [yB[k][t0 // 512], wB], [bB])
                    S.op("act", lambda e, bk=bk, ch=ch, vb=vb: e.activation(
                        out=vtm[:, ch, vb * 512:(vb + 1) * 512], in_=bk[:, :], func=AF.Gelu_apprx_tanh),
                        reads=[bB], writes=[vB[ch][vb]])
                    S.op("act", lambda e, ch=ch, vb=vb: e.activation(
                        out=junk[:, :], in_=vtm[:, ch, vb * 512:(vb + 1) * 512], func=AF.Square,
                        accum_out=ssq[:, ch, vb:vb + 1]),
                        reads=[vB[ch][vb]], writes=[junkB, ssqB[ch]])
            S.op("dve", lambda e: e.reduce_sum(out=ss[:, :], in_=ssq[:, :, :], axis=AX.X), reads=ssqB, writes=[ssB])
            S.op("act", lambda e: e.activation(out=rv[:, :], in_=ss[:, :], func=AF.Sqrt, scale=1.0 / 2048, bias=epst[:, :]),
                 reads=[ssB, epsB], writes=[rvB])
            S.op("dve", lambda e: e.reciprocal(out=rv[:, :], in_=rv[:, :]), reads=[rvB], writes=[rvB])
            for ch in range(8):
                S.op("dve", lambda e, ch=ch: e.tensor_scalar(out=vtm[:, ch, :], in0=vtm[:, ch, :], scalar1=rv[:, ch:ch + 1],
                                                            scalar2=None, op0=ALU.mult),
                     reads=vB[ch] + [rvB], writes=vB[ch])
            steps = [(fc, tt) for fc in range(16) for tt in range(2)]
            wu_c = {}

            def U(fc, tt):
                if tt == 0:
                    wt, wB = wnext(f"Wu{fc}")
                    wu_c[fc] = (wt[:, 0:1024].rearrange("p (k c) -> p k c", k=8), wB)
                wu, wB = wu_c[fc]
                gs = slice(T0 + tt * 512, T0 + (tt + 1) * 512)
                ls = slice(tt * 512, (tt + 1) * 512)
                bk, bB = bank()
                for k in range(8):
                    mm(bk[:, :], wu[:, k, :], yT[:, k, gs], k == 0, k == 7, [yB[k][gs.start // 512], wB], [bB])
                S.op("act", lambda e, bk=bk, fc=fc, ls=ls: e.activation(out=uT[:, fc, ls], in_=bk[:, :], func=AF.Gelu_apprx_tanh),
                     reads=[bB] + ([tabdB] if fc >= 8 else []), writes=[uB[fc][tt]] + ([aliasB] if fc < 8 else []))

            def SG(fc, tt):
                g = fc // 2
                ls = slice(tt * 512, (tt + 1) * 512)
                bk2, bB2 = bank()
                for c4 in range(4):
                    ch = tt * 4 + c4
                    mm(bk2[:, c4 * 128:(c4 + 1) * 128], vtm[:, ch, fc * 128:(fc + 1) * 128], wsb[:, g, :], True, True,
                       [vB[ch][fc // 4], wsbB], [bB2])
                i = t1i[0] % 2
                t1i[0] += 1
                S.op("dve", lambda e, bk2=bk2, fc=fc, g=g, i=i: e.scalar_tensor_tensor(
                    out=t1[:, i, :].rearrange("p (a t) -> p a t", a=4),
                    in0=bk2[:, :].rearrange("p (a t) -> p a t", a=4),
                    scalar=cst[:, C_VG + fc:C_VG + fc + 1],
                    in1=cst[:, C_BB + g * 128:C_BB + (g + 1) * 128].unsqueeze(1).to_broadcast([128, 4, 128]),
                    op0=ALU.mult, op1=ALU.add), reads=[bB2, cstB], writes=[t1B[i]])
                S.op("dve", lambda e, fc=fc, ls=ls, i=i: e.tensor_tensor(out=uT[:, fc, ls], in0=t1[:, i, :], in1=uT[:, fc, ls],
                                                                          op=ALU.mult),
                     reads=[t1B[i], uB[fc][tt]], writes=[uB[fc][tt]])

            LA = 5
            for i_ in range(LA):
                U(*steps[i_])
            for i_ in range(32):
                SG(*steps[i_])
                if i_ + LA < 32:
                    U(*steps[i_ + LA])
            for dc in range(8):
                wt, wB = wnext(f"Wo{dc}")
                wo = wt[:, 0:2048].rearrange("p (k c) -> p k c", k=16)
                for tt in range(2):
                    gs = slice(T0 + tt * 512, T0 + (tt + 1) * 512)
                    ls = slice(tt * 512, (tt + 1) * 512)
                    bk, bB = bank()
                    for k in range(16):
                        mm(bk[:, :], wo[:, k, :], uT[:, k, ls], k == 0, k == 15, [uB[k][tt], wB] + ([aliasB] if k < 8 else []), [bB])
                    hb = hB[dc][gs.start // 512]
                    S.op("dve", lambda e, bk=bk, dc=dc, gs=gs: e.tensor_tensor(out=hT[:, dc, gs], in0=bk[:, :], in1=hT[:, dc, gs], op=ALU.add),
                         reads=[bB, hb], writes=[hb])

    def ffn_phase(l):
        sb_off[0] = ARENA
        R = 3
        actT = alloc(f"actT{l}", [128, 11, SEQ], BF16)
        ag = alloc(f"ag{l}", [128, 2 + SEQ], F32)
        au = alloc(f"au{l}", [128, 2 + SEQ], F32)
        tg = alloc(f"tg{l}", [128, R, 512], F32)
        tu = alloc(f"tu{l}", [128, R, 512], F32)
        actB = [[Buf(f"act{j}_{t}") for t in range(4)] for j in range(11)]
        agB = [Buf(f"ag{i}") for i in range(4)]
        auB = [Buf(f"au{i}") for i in range(4)]
        zB = Buf("azero")
        tgB = [Buf(f"tg{i}") for i in range(R)]
        tuB = [Buf(f"tu{i}") for i in range(R)]
        S.op("dve", lambda e: e.memset(ag[:, 0:2], 0.0), writes=[zB])
        S.op("dve", lambda e: e.memset(au[:, 0:2], 0.0), writes=[zB])

        def cw(chunk, tap):
            o = C_CW + (l * 44 + chunk) * 3 + tap
            return cst[:, o:o + 1]

        def cb(chunk):
            o = C_CB + l * 44 + chunk
            return cst[:, o:o + 1]

        def evac(bk, bkB, a, aB, t, tB, i, tt, chunk):
            c0 = 2 + 512 * tt
            S.op("act", lambda e: e.activation(out=a[:, c0:c0 + 512], in_=bk[:, :], func=AF.Copy), reads=[bkB], writes=[aB[tt]])
            S.op("act", lambda e: e.activation(out=t[:, i, :], in_=bk[:, :], func=AF.Identity, scale=cw(chunk, 2), bias=cb(chunk)),
                 reads=[bkB, cstB], writes=[tB[i]])

        def tap(a, aB, t, tB, i, tt, chunk, tp):
            c0 = 2 + 512 * tt
            lo = c0 - (2 - tp)
            rd = [aB[tt], tB[i], cstB] + ([aB[tt - 1]] if tt > 0 else [zB])
            S.op("dve", lambda e: e.scalar_tensor_tensor(out=t[:, i, :], in0=a[:, lo:lo + 512], scalar=cw(chunk, tp), in1=t[:, i, :],
                                                         op0=ALU.mult, op1=ALU.add), reads=rd, writes=[tB[i]])

        cnt = 0
        for G in range(2):
            pend = None
            for j in range(11):
                jj = G * 11 + j
                wt, wB = wnext(f"Wup{l}_{jj}")
                wu = wt[:, 0:2048].rearrange("p (k b c) -> p k b c", k=8, b=2)
                for tt in range(4):
                    i = cnt % R
                    cnt += 1
                    bg, bgB = bank()
                    bu, buB = bank()
                    for k in range(8):
                        mm(bg[:, :], wu[:, k, 0, :], yT[:, k, tsl(tt)], k == 0, k == 7, [yB[k][tt], wB], [bgB])
                    for k in range(8):
                        mm(bu[:, :], wu[:, k, 1, :], yT[:, k, tsl(tt)], k == 0, k == 7, [yB[k][tt], wB], [buB])
                    evac(bg, bgB, ag, agB, tg, tgB, i, tt, jj)
                    evac(bu, buB, au, auB, tu, tuB, i, tt, 22 + jj)
                    if pend is not None:
                        pi_, pj, ptt = pend
                        S.op("act", lambda e, pi_=pi_: e.activation(out=tg[:, pi_, :], in_=tg[:, pi_, :], func=AF.Gelu_apprx_tanh),
                             reads=[tgB[pi_]], writes=[tgB[pi_]])
                    tap(ag, agB, tg, tgB, i, tt, jj, 1)
                    tap(au, auB, tu, tuB, i, tt, 22 + jj, 1)
                    tap(ag, agB, tg, tgB, i, tt, jj, 0)
                    tap(au, auB, tu, tuB, i, tt, 22 + jj, 0)
                    if pend is not None:
                        pi_, pj, ptt = pend
                        S.op("dve", lambda e, pi_=pi_, pj=pj, ptt=ptt: e.tensor_tensor(out=actT[:, pj, tsl(ptt)], in0=tg[:, pi_, :],
                                                                                      in1=tu[:, pi_, :], op=ALU.mult),
                             reads=[tgB[pi_], tuB[pi_]], writes=[actB[pj][ptt]] + ([aliasB] if pj < 4 else []))
                    pend = (i, j, tt)
            pi_, pj, ptt = pend
            S.op("act", lambda e, pi_=pi_: e.activation(out=tg[:, pi_, :], in_=tg[:, pi_, :], func=AF.Gelu_apprx_tanh),
                 reads=[tgB[pi_]], writes=[tgB[pi_]])
            S.op("dve", lambda e, pi_=pi_, pj=pj, ptt=ptt: e.tensor_tensor(out=actT[:, pj, tsl(ptt)], in0=tg[:, pi_, :], in1=tu[:, pi_, :],
                                                                          op=ALU.mult),
                 reads=[tgB[pi_], tuB[pi_]], writes=[actB[pj][ptt]])
            for dc in range(8):
                wt, wB = wnext(f"Wd{l}_{G}_{dc}")
                wd = wt[:, 0:1408].rearrange("p (k c) -> p k c", k=11)
                for tt in range(4):
                    bk, bB = bank()
                    for k in range(11):
                        mm(bk[:, :], wd[:, k, :], actT[:, k, tsl(tt)], k == 0, k == 10, [actB[k][tt], wB] + ([aliasB] if k < 4 else []), [bB])
                    S.op("dve", lambda e, bk=bk, dc=dc, tt=tt: e.tensor_tensor(out=hT[:, dc, tsl(tt)], in0=bk[:, :], in1=hT[:, dc, tsl(tt)],
                                                                              op=ALU.add),
                         reads=[bB, hB[dc][tt]], writes=[hB[dc][tt]])

    def attn_phase():
        sb_off[0] = ARENA
        attnT = alloc("attnT", [128, 4, SEQ], BF16)
        cosT = alloc("cosT", [128, SEQ], F32)
        sinT = alloc("sinT", [128, SEQ], F32)
        qT = [alloc(f"qT{s}", [128, SEQ], BF16) for s in range(2)]
        kT = [alloc(f"kT{s}", [128, SEQ], BF16) for s in range(2)]
        vh = [alloc(f"vh{s}", [128, 16, 128], BF16) for s in range(2)]
        nm = [alloc(f"nm{s}", [128, 16, 8], BF16) for s in range(2)]
        pT = [alloc(f"pT{i}", [128, 2, 512], BF16) for i in range(2)]
        r1 = alloc("r1", [128, 2, 512], F32)
        r2 = alloc("r2", [128, 1, 512], F32)
        r2b = nc.alloc_sbuf_tensor_at("r2b", [128, 512], F32, offset=ARENA - 4096)
        r2s = [r2[:, 0, :], r2b[:, :]]
        qb16 = alloc("qb16", [128, 2, 512], BF16)
        qb16B = [Buf("qb16_0"), Buf("qb16_1")]
        gm = alloc("gm", [128, 16, 8], F32)
        cmpt = alloc("cmpt", [128, 8, 8, 8], BF16)
        rank = alloc("rank", [128, 16, 8], F32)
        elig = alloc("elig", [128, 16, 8], BF16)
        negb = alloc("negb", [128, 16, 8], F32)
        km32 = alloc("km32", [128, 2, 8], F32)
        kmb = alloc("kmb", [128, 2, 8], BF16)
        rec = r2[:, 0, :]
        cm = alloc("cm", [128, 2, 256], BF16)
        negcm = alloc("negcm", [128, 2, 256], BF16)
        negcmB = Buf("negcm")
        ktmp = nc.alloc_sbuf_tensor_at("ktmp", [128, SEQ], I32, offset=(ARENA + 16384 + 16384 + 63) // 64 * 64)
        kf = nc.alloc_sbuf_tensor_at("kf", [128, SEQ], F32, offset=(ARENA + 16384 + 16384 + 8192 + 63) // 64 * 64)
        tpos = nc.alloc_sbuf_tensor_at("tpos", [128, SEQ], F32, offset=(ARENA + 16384 + 16384 + 16384 + 63) // 64 * 64)
        assert ARENA + 16384 * 3 + 8192 <= sb_off[0], "overlay scratch must stay inside arena"

        cosB, sinB, cmB, eligB, negbB = Buf("cos"), Buf("sin"), Buf("cm"), Buf("elig"), Buf("negb")
        attnB = [[Buf(f"at{k}_{q}") for q in range(4)] for k in range(4)]
        qB = [[Buf(f"q{s}_{t}") for t in range(4)] for s in range(2)]
        kB = [[Buf(f"k{s}_{t}") for t in range(4)] for s in range(2)]
        vhB = [[Buf(f"vh{s}_{t}") for t in range(4)] for s in range(2)]
        nmB = [Buf("nm0"), Buf("nm1")]
        pTB = [Buf(f"pT{i}") for i in range(2)]
        r1B = [Buf("r1_0"), Buf("r1_1")]
        r2B = [Buf("r2_0"), Buf("r2_1")]
        gmB, cmpB, rankB = Buf("gm"), Buf("cmp"), Buf("rank")
        km32B = [Buf("km32_0"), Buf("km32_1")]
        kmbB = [Buf("kmb_0"), Buf("kmb_1")]
        recB = r2B[0]
        scr = Buf("scr")

        S.op("sp", lambda e: e.dma_start(out=sinT[:, :], in_=tab_d[0]), reads=[tabdB] + hB[7], writes=[sinB], dma_to=sinB)
        S.op("sp", lambda e: e.dma_start(out=cosT[:, :], in_=tab_d[1]), reads=[tabdB] + hB[7], writes=[cosB], dma_to=cosB)
        S.op("pool", lambda e: e.iota(kf[:, 0:512].rearrange("p (j i) -> p j i", j=2), pattern=[[-128, 2], [1, 256]], base=0,
                                      channel_multiplier=-1, allow_small_or_imprecise_dtypes=True), reads=[scr] + hB[7], writes=[scr])
        S.op("dve", lambda e: e.tensor_single_scalar(out=cm[:, :, :], in_=kf[:, 0:512].rearrange("p (j i) -> p j i", j=2),
                                                     scalar=0.0, op=ALU.is_ge), reads=[scr], writes=[cmB])
        S.op("dve", lambda e: e.tensor_scalar(out=negcm[:, :, :], in0=cm[:, :, :], scalar1=-1.0, scalar2=NEGM, op0=ALU.add, op1=ALU.mult),
             reads=[cmB], writes=[negcmB])
        S.op("pool", lambda e: e.iota(kf[:, 512:640].rearrange("p (a b) -> p a b", a=16), pattern=[[1, 16], [-2, 8]], base=-2,
                                      channel_multiplier=0, allow_small_or_imprecise_dtypes=True), reads=[scr] + hB[7], writes=[scr])
        S.op("dve", lambda e: e.tensor_single_scalar(out=elig[:, :, :], in_=kf[:, 512:640].rearrange("p (a b) -> p a b", a=16),
                                                     scalar=0.0, op=ALU.is_ge), reads=[scr], writes=[eligB])
        S.op("dve", lambda e: e.tensor_scalar(out=negb[:, :, :], in0=elig[:, :, :], scalar1=-1.0, scalar2=1e30, op0=ALU.add, op1=ALU.mult),
             reads=[eligB], writes=[negbB])
        fenceB = Buf("fence")
        S.op("dve", lambda e: e.memset(km32[:, 0, 0:1], 0.0), reads=[scr, cosB, sinB, cmB, negcmB, eligB, negbB], writes=[fenceB, scr])
        yield

        def proj(h):
            s = h % 2
            wt, wB = wnext(f"Wqk{h}")
            wqk = wt[:, 0:2048].rearrange("p (k b c) -> p k b c", k=8, b=2)
            tiles = [(w_, dst, dstB, tt) for (w_, dst, dstB) in ((0, qT[s], qB[s]), (1, kT[s], kB[s])) for tt in range(4)]
            pend = None

            def finish_tile(p):
                n, ba, baB, dst, dstB, tt = p
                i = n % 2
                bs, bsB = bank("proj4")
                mm(bs[:, :], permT[:, :], qb16[:, i, :], True, True, [permB, qb16B[i]], [bsB])
                S.op("dve", lambda e: e.tensor_tensor(out=r1[:, i, :], in0=ba[:, :], in1=cosT[:, tsl(tt)], op=ALU.mult),
                     reads=[baB, cosB, qb16B[i]], writes=[r1B[i]])
                S.op("dve", lambda e: e.tensor_tensor(out=r2s[i], in0=bs[:, :], in1=sinT[:, tsl(tt)], op=ALU.mult),
                     reads=[bsB, sinB], writes=[r2B[i]])
                S.op("pool", lambda e: e.tensor_tensor(out=dst[:, tsl(tt)], in0=r1[:, i, :], in1=r2s[i], op=ALU.add),
                     reads=[r1B[i], r2B[i], fenceB], writes=[dstB[tt]])

            for n, (w_, dst, dstB, tt) in enumerate(tiles):
                i = n % 2
                ba, baB = bank("proj4")
                for k in range(8):
                    mm(ba[:, :], wqk[:, k, w_, :], yT[:, k, tsl(tt)], k == 0, k == 7, [yB[k][tt], wB], [baB])
                S.op("act", lambda e, ba=ba, i=i: e.activation(out=qb16[:, i, :], in_=ba[:, :], func=AF.Copy),
                     reads=[baB], writes=[qb16B[i]])
                if pend is not None:
                    finish_tile(pend)
                pend = (n, ba, baB, dst, dstB, tt)
            finish_tile(pend)
            wt2, wB2 = wnext(f"Wvh{h}")
            wvh = wt2[:, 0:1024].rearrange("p (k c) -> p k c", k=8)
            for tq in range(4):
                bk, bB = bank("proj")
                for c4 in range(4):
                    tch = tq * 4 + c4
                    for k in range(8):
                        mm(bk[:, c4 * 128:(c4 + 1) * 128], yT[:, k, tch * 128:(tch + 1) * 128], wvh[:, k, :], k == 0, k == 7,
                           [yB[k][tq], wB2], [bB])
                S.op("act", lambda e, bk=bk, tq=tq, s=s: e.activation(
                    out=vh[s][:, tq * 4:(tq + 1) * 4, :].rearrange("p a c -> p (a c)"), in_=bk[:, :], func=AF.Copy),
                    reads=[bB, fenceB], writes=[vhB[s][tq]])
            S.op("dve", lambda e, s=s: e.reduce_sum(out=km32[:, s, :], in_=kT[s][:, :].rearrange("p (n t) -> p n t", n=8), axis=AX.X),
                 reads=kB[s], writes=[km32B[s]])
            S.op("dve", lambda e, s=s: e.tensor_scalar(out=kmb[:, s, :], in0=km32[:, s, :], scalar1=1.0 / 256, scalar2=None, op0=ALU.mult),
                 reads=[km32B[s]], writes=[kmbB[s]])
            pending_gates.append(h)

        def gate(h):
            s = h % 2
            b0 = spair()
            bk, bB = pbank[b0], pbB[b0]
            for qt in range(16):
                mm(bk[:, qt * 8:(qt + 1) * 8], qT[s][:, qt * 128:(qt + 1) * 128], kmb[:, s, :], True, True, [qB[s][qt // 4], kmbB[s]], [bB])
            S.op("dve", lambda e, bk=bk: e.tensor_tensor(out=gm[:, :, :], in0=bk[:, 0:128].rearrange("p (a b) -> p a b", a=16),
                                                        in1=negb[:, :, :], op=ALU.add), reads=[bB, negbB], writes=[gmB])
            for hf in range(2):
                hsl = slice(hf * 8, (hf + 1) * 8)
                S.op("dve", lambda e, hsl=hsl: e.tensor_tensor(out=cmpt[:, :, :, :], in0=gm[:, hsl, :].unsqueeze(2).to_broadcast([128, 8, 8, 8]),
                                                               in1=gm[:, hsl, :].unsqueeze(3).to_broadcast([128, 8, 8, 8]), op=ALU.is_gt),
                     reads=[gmB], writes=[cmpB])
                S.op("dve", lambda e, hsl=hsl: e.reduce_sum(out=rank[:, hsl, :], in_=cmpt[:, :, :, :], axis=AX.X), reads=[cmpB], writes=[rankB])
            S.op("dve", lambda e: e.scalar_tensor_tensor(out=rank[:, :, :], in0=rank[:, :, :], scalar=3.0, in1=elig[:, :, :],
                                                         op0=ALU.is_lt, op1=ALU.mult), reads=[rankB, eligB], writes=[rankB])
            S.op("dve", lambda e, s=s: e.tensor_scalar(out=nm[s][:, :, :], in0=rank[:, :, :], scalar1=-1.0, scalar2=NEGM,
                                                      op0=ALU.add, op1=ALU.mult), reads=[rankB], writes=[nmB[s]])

        pti = [0]
        pending_gates = []
        scale = float(128 ** -0.5)

        spi = [0]

        def spair():
            b0 = (0, 2)[spi[0] % 2]
            spi[0] += 1
            return b0

        def attend(h):
            s = h % 2
            hh = h % 4
            items = []
            for t in range(4):
                nblk = 2 * t + 2
                for n in range(nblk):
                    items.append(dict(t=t, n=n, first=(n == 0), last=(n == nblk - 1)))
            banks = {}

            def front(it):
                t, n = it["t"], it["n"]
                c0 = 256 if n == 2 * t + 1 else 0
                W = 512 - c0
                b0 = spair()
                masks = []
                for qh in range(c0 // 128, 4):
                    qb_q = 2 * t + qh // 2
                    if n < qb_q and qb_q >= 4:
                        masks.append(qh)
                own = n >= 2 * t
                hs = slice((n - 2 * t) * 256, (n - 2 * t) * 256 + 256) if own else None
                for j in range(2):
                    kt = 2 * n + j
                    sp_, spB = pbank[b0 + j], pbB[b0 + j]
                    extra = []
                    for qh in masks:
                        qt = 4 * t + qh
                        extra.append((sp_[:, qh * 128:(qh + 1) * 128], nm[s][:, qt, n:n + 1].to_broadcast([128, 128]), ident[:, :],
                                      [nmB[s], identB]))
                    if own:
                        extra.append((sp_[:, hs], ident[:, :], negcm[:, j, :], [identB, negcmB]))
                    mm(sp_[:, c0:512], kT[s][:, kt * 128:(kt + 1) * 128], qT[s][:, t * 512 + c0:(t + 1) * 512], True, len(extra) == 0,
                       [kB[s][kt // 4], qB[s][t]], [spB])
                    for mi, (o_, l_, r_, rd_) in enumerate(extra):
                        mm(o_, l_, r_, False, mi == len(extra) - 1, rd_, [spB])
                pi = pti[0] % 2
                pti[0] += 1
                src = pball[:, b0 * 512:(b0 + 2) * 512].rearrange("p (j c) -> p j c", j=2)
                S.op("act", lambda e: e.activation(out=pT[pi][:, :, c0:512], in_=src[:, :, c0:512], func=AF.Exp, scale=scale),
                     reads=[pbB[b0], pbB[b0 + 1]], writes=[pTB[pi]])
                it["pi"] = pi
                it["c0"] = c0

            def back(it):
                t, n, pi, c0 = it["t"], it["n"], it["pi"], it["c0"]
                if it["first"]:
                    banks[t] = (bank("pv"), bank("dn"))
                (pv, pvB), (dn, dnB) = banks[t]
                for j in range(2):
                    kt = 2 * n + j
                    fst = it["first"] and j == 0
                    lst = it["last"] and j == 1
                    mm(pv[:, c0:512], vh[s][:, kt, :], pT[pi][:, j, c0:512], fst, lst, [vhB[s][kt // 4], pTB[pi]], [pvB], skip=True)
                    mm(dn[:, c0:512], ones[:, :], pT[pi][:, j, c0:512], fst, lst, [onesB, pTB[pi]], [dnB], skip=True)
                if it["last"]:
                    qs = slice(t * 512, (t + 1) * 512)
                    S.op("dve", lambda e: e.reciprocal(out=rec, in_=dn[:, :]), reads=[dnB], writes=[recB])
                    S.op("dve", lambda e: e.tensor_tensor(out=attnT[:, hh, qs], in0=pv[:, :], in1=rec, op=ALU.mult),
                         reads=[pvB, recB], writes=[attnB[hh][t], aliasB])

            front(items[0])
            for i, it in enumerate(items):
                if i + 1 < len(items):
                    front(items[i + 1])
                back(it)
                if it["last"] and it["t"] == 0:
                    for g in [g for g in pending_gates if g <= h]:
                        pending_gates.remove(g)
                        gate(g)
                if it["last"] and it["t"] == 2:
                    while pending_gates:
                        gate(pending_gates.pop(0))

        def wo(hg):
            for dc in range(8):
                wt, wB = wnext(f"Wob{hg}_{dc}")
                wo_ = wt[:, 0:512].rearrange("p (k c) -> p k c", k=4)
                for tt in range(4):
                    bk, bB = bank("proj")
                    for k in range(4):
                        mm(bk[:, :], wo_[:, k, :], attnT[:, k, tsl(tt)], k == 0, k == 3, [attnB[k][tt], wB, aliasB], [bB])
                    S.op("dve", lambda e, bk=bk, dc=dc, tt=tt: e.tensor_tensor(out=hT[:, dc, tsl(tt)], in0=bk[:, :], in1=hT[:, dc, tsl(tt)],
                                                                              op=ALU.add),
                         reads=[bB, hB[dc][tt]], writes=[hB[dc][tt]])

        for it in attn_order():
            k = int(it[1])
            if it[0] == "p":
                proj(k)
            elif it[0] == "a":
                attend(k)
            else:
                wo(k)

    def finish():
        if stop_after != "final":
            for c in range(8):
                S.op("sp", lambda e, c=c: e.dma_start(out=out_d[c], in_=hT[:, c, :]), reads=hB[c], dma_to=outB)
        S.emit(nc, final_waits=[outB])
        return nc

    norm_phase(0, nobar=True)
    gmlp_phase()
    if stop_after == "gmlp":
        return finish()
    norm_phase(2, nobar=True)
    ffn_phase(0)
    if stop_after == "ffn0":
        return finish()
    attn_gen = attn_phase()
    norm_phase(1, pre=lambda: next(attn_gen), nobar=True)
    next(attn_gen, None)
    if stop_after == "attn":
        return finish()
    norm_phase(3, nobar=True)
    ffn_phase(1)
    if stop_after == "ffn1":
        return finish()
    norm_phase(4, final=True, nobar=True)
    return finish()


def _kpc(w, nk):
    C = w.shape[1]
    return np.ascontiguousarray(w.reshape(nk, 128, C).transpose(1, 0, 2))


def pack_weights(a_w_in, a_w_out, b_w_qkv, b_w_o, ffn_w_up, ffn_w_down):
    offs, WTOT = slab_offsets()
    wts = np.empty((128, WTOT), np.float32)

    def put(name, arr):
        o, n = offs[name]
        wts[:, o:o + n] = arr.reshape(128, n)

    w_in = a_w_in[0]
    for vb in range(4):
        put(f"Wv{vb}", _kpc(w_in[:, 2048 + vb * 512:2048 + (vb + 1) * 512], 8))
    for fc in range(16):
        put(f"Wu{fc}", _kpc(w_in[:, fc * 128:(fc + 1) * 128], 8))
    w_out = a_w_out[0]
    for dc in range(8):
        put(f"Wo{dc}", _kpc(w_out[:, dc * 128:(dc + 1) * 128], 16))
    for l in range(2):
        wu = ffn_w_up[l]
        for jj in range(22):
            g = _kpc(wu[:, jj * 128:(jj + 1) * 128], 8)
            u = _kpc(wu[:, DFF + jj * 128:DFF + (jj + 1) * 128], 8)
            put(f"Wup{l}_{jj}", np.stack([g, u], axis=2))
        wd = ffn_w_down[l]
        for G in range(2):
            for dc in range(8):
                put(f"Wd{l}_{G}_{dc}", _kpc(wd[G * 1408:(G + 1) * 1408, dc * 128:(dc + 1) * 128], 11))
    wqkv = b_w_qkv[0]
    perm = (np.arange(128) + 64) % 128
    for h in range(8):
        q = wqkv[:, h * 128:(h + 1) * 128]
        k = wqkv[:, 1024 + h * 128:1024 + (h + 1) * 128]
        v = wqkv[:, 2048 + h * 128:2048 + (h + 1) * 128]
        put(f"Wqk{h}", np.stack([_kpc(q, 8), _kpc(k, 8)], axis=2))
        put(f"Wvh{h}", _kpc(v, 8))
    w_o = b_w_o[0]
    for hg in range(2):
        for dc in range(8):
            put(f"Wob{hg}_{dc}", _kpc(w_o[hg * 512:(hg + 1) * 512, dc * 128:(dc + 1) * 128], 4))
    return wts


def pack_consts(mix_norm, ffn_norm, final_norm, a_v_gain, a_w_s, a_b_s, ffn_conv_w, ffn_conv_b):
    c = np.zeros((128, NCONST), np.float32)
    gains = [mix_norm[0], mix_norm[1], ffn_norm[0], ffn_norm[1], final_norm]
    for i, g in enumerate(gains):
        c[:, C_GAIN + i * 8:C_GAIN + (i + 1) * 8] = g.reshape(8, 128).T
    c[:, C_VG:C_VG + 16] = a_v_gain[0].reshape(16, 128).T
    for l in range(2):
        cw = ffn_conv_w[l].reshape(3, 44, 128)
        c[:, C_CW + l * 132:C_CW + (l + 1) * 132] = cw.transpose(2, 1, 0).reshape(128, 132)
        c[:, C_CB + l * 44:C_CB + (l + 1) * 44] = ffn_conv_b[l].reshape(44, 128).T
    ws = a_w_s[0]
    c[:, C_WS:C_WS + 1024] = ws.transpose(2, 0, 1).reshape(128, 1024)
    c[:, C_BB:C_BB + 1024] = np.broadcast_to(a_b_s[0].reshape(1, 1024), (128, 1024))
    return c


_NC_CACHE = {}


def kernel(x, mix_norm, a_w_in, a_v_gain, a_w_s, a_b_s, a_w_out, b_w_qkv, b_w_o,
           ffn_norm, ffn_w_up, ffn_conv_w, ffn_conv_b, ffn_w_down, final_norm, _stop_after="final", _cores=8):
    f = lambda a: np.asarray(a, dtype=np.float32)
    x = f(x)
    wts = pack_weights(f(a_w_in), f(a_w_out), f(b_w_qkv), f(b_w_o), f(ffn_w_up), f(ffn_w_down))
    consts = pack_consts(f(mix_norm), f(ffn_norm), f(final_norm), f(a_v_gain), f(a_w_s), f(a_b_s), f(ffn_conv_w), f(ffn_conv_b))
    nc = build_nc(_stop_after)
    in_maps = []
    for b in range(_cores):
        xT = np.ascontiguousarray(x[b].T).reshape(8, 128, SEQ)
        in_maps.append({"xT": xT, "consts": consts, "wts": wts})
    res = run_bass_kernel_spmd(nc, in_maps, core_ids=list(range(_cores)))
    outs = []
    for b in range(_cores):
        oT = res.results[b]["outT"].reshape(D, SEQ)
        outs.append(np.ascontiguousarray(oT.T))
    return np.stack(outs, axis=0).astype(np.float32)
```

```python
import os
from contextlib import ExitStack

import numpy as np
import concourse.bass as bass
import concourse.mybir as mybir
from concourse.bass_utils import run_bass_kernel_spmd

F32 = mybir.dt.float32
BF16 = mybir.dt.bfloat16
I32 = mybir.dt.int32
AF = mybir.ActivationFunctionType
ALU = mybir.AluOpType
AX = mybir.AxisListType

D = 1024
SEQ = 2048
NB = 8
DFF = 2816
EPS = 1e-6
ROPE_THETA = 10000.0
NEGM = 30000.0

C_GAIN = 0
C_VG = 40
C_CW = 56
C_CB = C_CW + 264
C_WS = C_CB + 88
C_BB = C_WS + 1024
NCONST = C_BB + 1024


class Buf:
    __slots__ = ("name", "w", "r", "ndma", "sem")

    def __init__(self, name):
        self.name = name
        self.w = None
        self.r = {}
        self.ndma = 0
        self.sem = None


class Op:
    __slots__ = ("eng", "fn", "deps", "dmadeps", "marked", "seq", "dma_buf", "dma_cnt", "pos")

    def __init__(self, eng, fn):
        self.eng = eng
        self.fn = fn
        self.deps = []
        self.dmadeps = {}
        self.marked = False
        self.seq = 0
        self.dma_buf = None
        self.dma_cnt = 0


class Sched:
    ENGS = ("pe", "act", "dve", "pool", "sp")

    def __init__(self):
        self.q = {e: [] for e in self.ENGS}
        self.dma_bufs = []
        self.bar = None
        self.bar_done = set()

    def _dep(self, op, prev):
        if prev is None or prev is op:
            return
        if prev.dma_buf is not None:
            b = prev.dma_buf
            op.dmadeps[b] = max(op.dmadeps.get(b, 0), b.ndma)
            return
        if prev.eng == "pe" and op.eng == "pe":
            return
        op.deps.append(prev)
        prev.marked = True

    def barrier(self):
        self.bar = [self.q[e][-1] for e in self.ENGS if self.q[e]]
        self.bar_done = set()

    def op(self, eng, fn, reads=(), writes=(), dma_to=None, nobarrier=False):
        o = Op(eng, fn)
        if self.bar is not None and not nobarrier and eng not in self.bar_done:
            for p in self.bar:
                self._dep(o, p)
            self.bar_done.add(eng)
        for b in reads:
            self._dep(o, b.w)
        for b in writes:
            self._dep(o, b.w)
            for r in b.r.values():
                self._dep(o, r)
        key = eng if dma_to is None else ("dma", dma_to.name, len(self.q[eng]))
        for b in reads:
            b.r[key] = o
        for b in writes:
            b.w = o
            b.r = {}
        if dma_to is not None:
            dma_to.ndma += 1
            o.dma_buf = dma_to
            o.dma_cnt = dma_to.ndma
            if dma_to not in self.dma_bufs:
                self.dma_bufs.append(dma_to)
        o.pos = len(self.q[eng])
        self.q[eng].append(o)
        return o

    def emit(self, nc, final_waits=()):
        with ExitStack() as es:
            sems = {e: es.enter_context(nc.semaphore("s_" + e)) for e in ("pe", "act", "dve", "pool")}
            for b in self.dma_bufs:
                b.sem = es.enter_context(nc.semaphore("d_" + b.name))
            for e in self.ENGS:
                c = 0
                for o in self.q[e]:
                    if o.marked and o.dma_buf is None:
                        c += 1
                        o.seq = c
            block = es.enter_context(nc.Block())
            sched = self

            def run(ename, eng):
                seen = {}
                for o in sched.q[ename]:
                    need = {}
                    for d in o.deps:
                        if need.get(d.eng, 0) < d.seq:
                            need[d.eng] = d.seq
                    for k, v in need.items():
                        if seen.get(k, 0) < v:
                            eng.wait_ge(sems[k], v)
                            seen[k] = v
                    for b, cnt in o.dmadeps.items():
                        if seen.get(b, 0) < cnt:
                            eng.wait_ge(b.sem, 16 * cnt)
                            seen[b] = cnt
                    ins = o.fn(eng)
                    if o.dma_buf is not None:
                        ins.then_inc(o.dma_buf.sem, 16)
                    elif o.marked:
                        ins.then_inc(sems[ename], 1)
                if ename == "sp":
                    for b in final_waits:
                        eng.wait_ge(b.sem, 16 * b.ndma)

            @block.tensor
            def _(eng):
                run("pe", eng)

            @block.scalar
            def _(eng):
                run("act", eng)

            @block.vector
            def _(eng):
                run("dve", eng)

            @block.gpsimd
            def _(eng):
                run("pool", eng)

            @block.sync
            def _(eng):
                run("sp", eng)


def slab_defs():
    d = {}
    for vb in range(4):
        d[f"Wv{vb}"] = 8 * 512
    for fc in range(16):
        d[f"Wu{fc}"] = 8 * 128
    for dc in range(8):
        d[f"Wo{dc}"] = 16 * 128
    for l in range(2):
        for jj in range(22):
            d[f"Wup{l}_{jj}"] = 8 * 2 * 128
        for G in range(2):
            for dc in range(8):
                d[f"Wd{l}_{G}_{dc}"] = 11 * 128
    for h in range(8):
        d[f"Wqk{h}"] = 8 * 2 * 128
        d[f"Wvh{h}"] = 8 * 128
    for hg in range(2):
        for dc in range(8):
            d[f"Wob{hg}_{dc}"] = 4 * 128
    return d


def slab_offsets():
    offs = {}
    o = 0
    for k, n in slab_defs().items():
        offs[k] = (o, n)
        o += n
    return offs, o


def attn_order():
    return ["p0", "p1", "a0", "p2", "a1", "p3", "a2", "p4", "a3", "w0", "p5", "a4", "p6", "a5", "p7", "a6", "a7", "w1"]


def stream_order(stop_after):
    seq = []
    for half in range(2):
        seq += [f"Wv{vb}" for vb in range(4)]
        seq += [f"Wu{fc}" for fc in range(16)]
        seq += [f"Wo{dc}" for dc in range(8)]
    if stop_after == "gmlp":
        return seq

    def ffn(l):
        s = []
        for G in range(2):
            s += [f"Wup{l}_{G * 11 + j}" for j in range(11)]
            s += [f"Wd{l}_{G}_{dc}" for dc in range(8)]
        return s

    seq += ffn(0)
    if stop_after == "ffn0":
        return seq
    for it in attn_order():
        k = int(it[1])
        if it[0] == "p":
            seq += [f"Wqk{k}", f"Wvh{k}"]
        elif it[0] == "w":
            seq += [f"Wob{k}_{dc}" for dc in range(8)]
    if stop_after == "attn":
        return seq
    seq += ffn(1)
    return seq


def build_nc(stop_after="final"):
    nc = bass.Bass("TRN2", target_bir_lowering=False)
    offs, WTOT = slab_offsets()
    xT_d = nc.dram_tensor("xT", [8, 128, SEQ], F32, kind="ExternalInput").ap()
    consts_d = nc.dram_tensor("consts", [128, NCONST], F32, kind="ExternalInput").ap()
    wts_d = nc.dram_tensor("wts", [128, WTOT], F32, kind="ExternalInput").ap()
    out_d = nc.dram_tensor("outT", [8, 128, SEQ], F32, kind="ExternalOutput").ap()

    S = Sched()
    sb_off = [16512]
    SB_LIMIT = 227328

    def alloc(name, shape, dt, at=None):
        esz = 2 if dt == BF16 else 4
        n = esz
        for s_ in shape[1:]:
            n *= s_
        off = sb_off[0] if at is None else at
        off = (off + 63) // 64 * 64
        assert off + n <= SB_LIMIT, (name, off, n)
        t = nc.alloc_sbuf_tensor_at(name, list(shape), dt, offset=off)
        if at is None:
            sb_off[0] = off + n
        return t

    hT = alloc("hT", [128, 8, SEQ], F32)
    yT = alloc("yT", [128, 8, SEQ], BF16)
    cst = alloc("cst", [128, NCONST], F32)
    ident = alloc("ident", [128, 128], BF16)
    ones = alloc("ones", [128, 128], BF16)
    epst = alloc("epst", [128, 1], F32)
    permT = alloc("permT", [128, 128], BF16)
    wsb = alloc("wsb", [128, 8, 128], BF16)
    wslots = [alloc(f"wslot{i}", [128, 4096], BF16) for i in range(3)]
    ARENA = sb_off[0]

    hB = [[Buf(f"h{c}_{t}") for t in range(4)] for c in range(8)]
    yB = [[Buf(f"y{c}_{t}") for t in range(4)] for c in range(8)]
    cstB, identB, onesB, epsB, wsbB = Buf("cst"), Buf("ident"), Buf("ones"), Buf("eps"), Buf("wsb")
    wslotB = [Buf(f"wslot{i}") for i in range(3)]
    outB = Buf("out")
    xinB = [Buf(f"xin{c}") for c in range(8)]
    aliasB = Buf("alias")

    pball = nc.alloc_psum_tensor("pball", [128, 4096], F32)
    pbank = [pball[:, i * 512:(i + 1) * 512] for i in range(8)]
    pbB = [Buf(f"pb{i}") for i in range(8)]
    rot = {"all": [0, list(range(8))], "proj": [0, [0, 1, 2, 3, 4, 5, 6, 7]], "proj4": [0, [0, 1, 2, 3, 4, 5, 6, 7]], "s": [0, [2, 3, 0]], "pv": [0, [4, 6]], "dn": [0, [5, 7]]}

    def bank(kind="all"):
        r = rot[kind]
        i = r[1][r[0] % len(r[1])]
        r[0] += 1
        return pbank[i], pbB[i]

    order = stream_order(stop_after)
    wstate = {"issued": 0, "taken": 0}

    def w_issue_upto(n):
        while wstate["issued"] < min(n, len(order)):
            i = wstate["issued"]
            name = order[i]
            o, ne = offs[name]
            t, B = wslots[i % 3], wslotB[i % 3]
            S.op("pool", lambda e, t=t, o=o, ne=ne: e.dma_start(out=t[:, 0:ne], in_=wts_d[:, o:o + ne]),
                 writes=[B], dma_to=B, nobarrier=True)
            wstate["issued"] += 1

    def wnext(name):
        i = wstate["taken"]
        assert order[i] == name, (order[i], name)
        w_issue_upto(i + 3)
        wstate["taken"] += 1
        return wslots[i % 3], wslotB[i % 3]

    def mm(out_ap, lhsT, rhs, start, stop, reads, writes, skip=False):
        S.op("pe", lambda e: e.matmul(out_ap, lhsT=lhsT, rhs=rhs, start=start, stop=stop, skip_group_check=skip),
             reads=reads, writes=writes)

    def tsl(tt):
        return slice(tt * 512, (tt + 1) * 512)

    def emit_tables(sinT, cosT, sinB, cosB, ktmp, kf, tpos, pcol, scr):
        S.op("pool", lambda e: e.iota(pcol[:, 0:1], pattern=[[0, 1]], base=0, channel_multiplier=1,
                                      allow_small_or_imprecise_dtypes=True), writes=[scr])
        S.op("dve", lambda e: e.tensor_single_scalar(out=pcol[:, 1:2], in_=pcol[:, 0:1], scalar=64.0, op=ALU.is_ge), reads=[scr], writes=[scr])
        S.op("dve", lambda e: e.scalar_tensor_tensor(out=pcol[:, 2:3], in0=pcol[:, 1:2], scalar=-64.0, in1=pcol[:, 0:1],
                                                     op0=ALU.mult, op1=ALU.add), reads=[scr], writes=[scr])
        S.op("dve", lambda e: e.tensor_scalar(out=pcol[:, 3:4], in0=pcol[:, 1:2], scalar1=2.0, scalar2=-1.0, op0=ALU.mult, op1=ALU.add),
             reads=[scr], writes=[scr])
        S.op("act", lambda e: e.activation(out=pcol[:, 2:3], in_=pcol[:, 2:3], func=AF.Exp, scale=-float(np.log(ROPE_THETA)) / 64.0),
             reads=[scr], writes=[scr])
        S.op("pool", lambda e: e.iota(tpos[:, :], pattern=[[1, SEQ]], base=0, channel_multiplier=0,
                                      allow_small_or_imprecise_dtypes=True), writes=[scr])
        C1 = 6.28125
        C2 = float(2 * np.pi - C1)

        def table(dst, dstB, shift, signed):
            S.op("dve", lambda e: e.tensor_scalar(out=dst[:, :], in0=tpos[:, :], scalar1=pcol[:, 2:3], scalar2=float(shift),
                                                  op0=ALU.mult, op1=ALU.add), reads=[scr], writes=[dstB])
            S.op("dve", lambda e: e.tensor_scalar(out=ktmp[:, :], in0=dst[:, :], scalar1=float(1.0 / (2 * np.pi)), scalar2=None,
                                                  op0=ALU.mult), reads=[dstB], writes=[scr])
            S.op("dve", lambda e: e.tensor_copy(out=kf[:, :], in_=ktmp[:, :]), reads=[scr], writes=[scr])
            S.op("dve", lambda e: e.scalar_tensor_tensor(out=dst[:, :], in0=kf[:, :], scalar=-C1, in1=dst[:, :],
                                                         op0=ALU.mult, op1=ALU.add), reads=[scr, dstB], writes=[dstB])
            S.op("dve", lambda e: e.scalar_tensor_tensor(out=dst[:, :], in0=kf[:, :], scalar=-C2, in1=dst[:, :],
                                                         op0=ALU.mult, op1=ALU.add), reads=[scr, dstB], writes=[dstB])
            S.op("dve", lambda e: e.tensor_scalar(out=dst[:, :], in0=dst[:, :], scalar1=-3.1415925, scalar2=3.1415925,
                                                  op0=ALU.max, op1=ALU.min), reads=[dstB], writes=[dstB])
            S.op("act", lambda e: e.activation(out=dst[:, :], in_=dst[:, :], func=AF.Sin), reads=[dstB], writes=[dstB])
            if signed:
                S.op("dve", lambda e: e.tensor_scalar(out=dst[:, :], in0=dst[:, :], scalar1=pcol[:, 3:4], scalar2=None, op0=ALU.mult),
                     reads=[dstB, scr], writes=[dstB])

        table(sinT, sinB, 0.0, True)
        table(cosT, cosB, float(np.pi / 2), False)

    S.op("sp", lambda e: e.dma_start(out=cst[:, :], in_=consts_d), writes=[cstB], dma_to=cstB)
    for tt in range(4):
        for cg in range(2):
            S.op("sp", lambda e, tt=tt, cg=cg: e.dma_start(
                out=hT[:, cg * 4:(cg + 1) * 4, tsl(tt)],
                in_=xT_d.rearrange("c p t -> p c t")[:, cg * 4:(cg + 1) * 4, tsl(tt)]),
                writes=[hB[c][tt] for c in range(cg * 4, (cg + 1) * 4)], dma_to=xinB[tt * 2 + cg])
    w_issue_upto(3)

    sb_off[0] = ARENA + 61440
    io_f = alloc("io_f", [128, 256], F32)
    ioB = Buf("io")
    S.op("pool", lambda e: e.iota(io_f[:, :], pattern=[[1, 256]], base=0, channel_multiplier=-1,
                                  allow_small_or_imprecise_dtypes=True), writes=[ioB])
    S.op("dve", lambda e: e.tensor_single_scalar(out=ident[:, :], in_=io_f[:, 0:128], scalar=0.0, op=ALU.is_equal),
         reads=[ioB], writes=[identB])
    S.op("dve", lambda e: e.memset(ones[:, :], 1.0), writes=[onesB])
    iosq = alloc("iosq", [128, 128], F32)
    iosqB, permB = Buf("iosq"), Buf("perm")
    S.op("dve", lambda e: e.tensor_tensor(out=iosq[:, :], in0=io_f[:, 0:128], in1=io_f[:, 0:128], op=ALU.mult), reads=[ioB], writes=[iosqB])
    S.op("dve", lambda e: e.tensor_single_scalar(out=permT[:, :], in_=iosq[:, :], scalar=4096.0, op=ALU.is_equal),
         reads=[iosqB], writes=[permB])
    S.op("dve", lambda e: e.memset(epst[:, :], EPS), writes=[epsB])
    msk = alloc("msk", [128, 128], F32)
    mskB = Buf("msk")
    S.op("dve", lambda e: e.tensor_single_scalar(out=msk[:, :], in_=io_f[:, 0:128], scalar=0.0, op=ALU.is_ge),
         reads=[ioB], writes=[mskB])
    S.op("dve", lambda e: e.tensor_tensor(
        out=wsb[:, :, :], in0=cst[:, C_WS:C_WS + 1024].rearrange("p (g t) -> p g t", g=8),
        in1=msk[:, :].unsqueeze(1).to_broadcast([128, 8, 128]), op=ALU.mult),
        reads=[cstB, mskB], writes=[wsbB])

    tab_d = nc.dram_tensor("ropetab", [2, 128, SEQ], F32, kind="Internal").ap()
    tabdB = Buf("tabd")
    if stop_after not in ("gmlp", "ffn0"):
        sb_off[0] = ARENA + 16384
        s_sin = alloc("s_sin", [128, SEQ], F32)
        s_cos = alloc("s_cos", [128, SEQ], F32)
        s_ktmp = alloc("s_ktmp", [128, SEQ], I32)
        s_kf = alloc("s_kf", [128, SEQ], F32)
        s_tpos = alloc("s_tpos", [128, SEQ], F32)
        s_pcol = alloc("s_pcol", [128, 4], F32)
        s_sinB, s_cosB, s_scr = Buf("s_sin"), Buf("s_cos"), Buf("s_scr")
        emit_tables(s_sin, s_cos, s_sinB, s_cosB, s_ktmp, s_kf, s_tpos, s_pcol, s_scr)
        S.op("sp", lambda e: e.dma_start(out=tab_d[0], in_=s_sin[:, :]), reads=[s_sinB], writes=[tabdB], dma_to=tabdB)
        S.op("sp", lambda e: e.dma_start(out=tab_d[1], in_=s_cos[:, :]), reads=[s_cosB], writes=[tabdB], dma_to=tabdB)

    def norm_phase(gi, final=False, pre=None, nobar=False):
        if not nobar:
            S.barrier()
        if pre is not None:
            pre()
        sb_off[0] = ARENA
        rs2 = alloc(f"rs{gi}", [128, 2, 512], F32)
        sqb = alloc(f"sqb{gi}", [128, 12, 512], BF16)
        rsB2 = [Buf(f"rs{gi}_{t}") for t in range(2)]
        rsB = [rsB2[t % 2] for t in range(4)]
        sqB = [Buf(f"sq{gi}_{i}") for i in range(12)]

        def rs_(tt):
            return rs2[:, tt % 2, :]
        sqi = [0, 0]
        def y_ops(tt):
            for c in range(8):
                gcol = cst[:, C_GAIN + gi * 8 + c:C_GAIN + gi * 8 + c + 1]
                if final:
                    S.op("dve", lambda e, c=c, gcol=gcol: e.scalar_tensor_tensor(
                        out=hT[:, c, tsl(tt)], in0=hT[:, c, tsl(tt)], scalar=gcol, in1=rs_(tt),
                        op0=ALU.mult, op1=ALU.mult), reads=[hB[c][tt], rsB[tt], cstB], writes=[hB[c][tt]])
                else:
                    S.op("dve", lambda e, c=c, gcol=gcol: e.scalar_tensor_tensor(
                        out=yT[:, c, tsl(tt)], in0=hT[:, c, tsl(tt)], scalar=gcol, in1=rs_(tt),
                        op0=ALU.mult, op1=ALU.mult), reads=[hB[c][tt], rsB[tt], cstB, aliasB], writes=[yB[c][tt]])
                if final and c % 4 == 3:
                    c0 = c - 3
                    S.op("sp", lambda e, c0=c0: e.dma_start(out=out_d.rearrange("c p t -> p c t")[:, c0:c0 + 4, tsl(tt)],
                                                            in_=hT[:, c0:c0 + 4, tsl(tt)]),
                         reads=[hB[cc][tt] for cc in range(c0, c0 + 4)], dma_to=outB)

        for tt in range(4):
            for c in range(8):
                al = [aliasB] if (tt == 0 and c in (0, 3)) else []
                if c != 3:
                    sl = sqi[0] % 10
                    sqi[0] += 1
                    S.op("act", lambda e, c=c, tt=tt, sl=sl: e.activation(out=sqb[:, sl, :], in_=hT[:, c, tsl(tt)], func=AF.Square),
                         reads=[hB[c][tt]], writes=[sqB[sl]] + al)
                else:
                    sl = 10 + sqi[1] % 2
                    sqi[1] += 1
                    S.op("dve", lambda e, c=c, tt=tt, sl=sl: e.tensor_tensor(out=sqb[:, sl, :], in0=hT[:, c, tsl(tt)], in1=hT[:, c, tsl(tt)],
                                                                            op=ALU.mult),
                         reads=[hB[c][tt]], writes=[sqB[sl]] + al)
                mm(pbank[tt][:, :], ones[:, :], sqb[:, sl, :], c == 0, c == 7, [sqB[sl], onesB], [pbB[tt]])
            S.op("act", lambda e, tt=tt: e.activation(out=rs_(tt), in_=pbank[tt][:, :], func=AF.Ln,
                                                     scale=1.0 / D, bias=epst[:, :]),
                 reads=[pbB[tt], epsB], writes=[rsB[tt]] + ([aliasB] if tt == 0 else []))
            S.op("act", lambda e, tt=tt: e.activation(out=rs_(tt), in_=rs_(tt), func=AF.Exp, scale=-0.5),
                 reads=[rsB[tt]], writes=[rsB[tt]])
            if tt >= 1:
                y_ops(tt - 1)
        y_ops(3)

    def gmlp_phase():
        sb_off[0] = ARENA
        uT = alloc("uT", [128, 16, 1024], BF16)
        vtm = alloc("vtm", [128, 8, 2048], BF16)
        junk = alloc("junk", [128, 512], BF16)
        t1 = alloc("t1", [128, 2, 512], F32)
        ssq = alloc("ssq", [128, 8, 4], F32)
        ss = alloc("ss", [128, 8], F32)
        rv = alloc("rv", [128, 8], F32)
        uB = [[Buf(f"u{c}_{t}") for t in range(2)] for c in range(16)]
        vB = [[Buf(f"v{c}_{b}") for b in range(4)] for c in range(8)]
        junkB, ssB, rvB = Buf("junk"), Buf("ss"), Buf("rv")
        t1B = [Buf("t1_0"), Buf("t1_1")]
        ssqB = [Buf(f"ssq{c}") for c in range(8)]
        t1i = [0]
        for half in range(2):
            T0 = half * 1024
            for ch in range(8):
                S.op("dve", lambda e, ch=ch: e.memset(ssq[:, ch, :], 0.0), writes=[ssqB[ch]])
            for vb in range(4):
                wt, wB = wnext(f"Wv{vb}")
                wv = wt[:, 0:4096].rearrange("p (k c) -> p k c", k=8)
                for ch in range(8):
                    t0 = T0 + ch * 128
                    bk, bB = bank()
                    for k in range(8):
                        mm(bk[:, :], yT[:, k, t0:t0 + 128], wv[:, k, :], k == 0, k == 7, [yB[k][t0 // 512], wB], [bB])
                    S.op("act", lambda e, bk=bk, ch=ch, vb=vb: e.activation(
                        out=vtm[:, ch, vb * 512:(vb + 1) * 512], in_=bk[:, :], func=AF.Gelu_apprx_tanh),
                        reads=[bB], writes=[vB[ch][vb]])
                    S.op("act", lambda e, ch=ch, vb=vb: e.activation(
                        out=junk[:, :], in_=vtm[:, ch, vb * 512:(vb + 1) * 512], func=AF.Square,
                        accum_out=ssq[:, ch, vb:vb + 1]),
                        reads=[vB[ch][vb]], writes=[junkB, ssqB[ch]])
            S.op("dve", lambda e: e.reduce_sum(out=ss[:, :], in_=ssq[:, :, :], axis=AX.X), reads=ssqB, writes=[ssB])
            S.op("act", lambda e: e.activation(out=rv[:, :], in_=ss[:, :], func=AF.Sqrt, scale=1.0 / 2048, bias=epst[:, :]),
                 reads=[ssB, epsB], writes=[rvB])
            S.op("dve", lambda e: e.reciprocal(out=rv[:, :], in_=rv[:, :]), reads=[rvB], writes=[rvB])
            for ch in range(8):
                S.op("dve", lambda e, ch=ch: e.tensor_scalar(out=vtm[:, ch, :], in0=vtm[:, ch, :], scalar1=rv[:, ch:ch + 1],
                                                            scalar2=None, op0=ALU.mult),
                     reads=vB[ch] + [rvB], writes=vB[ch])
            steps = [(fc, tt) for fc in range(16) for tt in range(2)]
            wu_c = {}

            def U(fc, tt):
                if tt == 0:
                    wt, wB = wnext(f"Wu{fc}")
                    wu_c[fc] = (wt[:, 0:1024].rearrange("p (k c) -> p k c", k=8), wB)
                wu, wB = wu_c[fc]
                gs = slice(T0 + tt * 512, T0 + (tt + 1) * 512)
                ls = slice(tt * 512, (tt + 1) * 512)
                bk, bB = bank()
                for k in range(8):
                    mm(bk[:, :], wu[:, k, :], yT[:, k, gs], k == 0, k == 7, [yB[k][gs.start // 512], wB], [bB])
                S.op("act", lambda e, bk=bk, fc=fc, ls=ls: e.activation(out=uT[:, fc, ls], in_=bk[:, :], func=AF.Gelu_apprx_tanh),
                     reads=[bB] + ([tabdB] if fc >= 8 else []), writes=[uB[fc][tt]] + ([aliasB] if fc < 8 else []))

            def SG(fc, tt):
                g = fc // 2
                ls = slice(tt * 512, (tt + 1) * 512)
                bk2, bB2 = bank()
                for c4 in range(4):
                    ch = tt * 4 + c4
                    mm(bk2[:, c4 * 128:(c4 + 1) * 128], vtm[:, ch, fc * 128:(fc + 1) * 128], wsb[:, g, :], True, True,
                       [vB[ch][fc // 4], wsbB], [bB2])
                i = t1i[0] % 2
                t1i[0] += 1
                S.op("dve", lambda e, bk2=bk2, fc=fc, g=g, i=i: e.scalar_tensor_tensor(
                    out=t1[:, i, :].rearrange("p (a t) -> p a t", a=4),
                    in0=bk2[:, :].rearrange("p (a t) -> p a t", a=4),
                    scalar=cst[:, C_VG + fc:C_VG + fc + 1],
                    in1=cst[:, C_BB + g * 128:C_BB + (g + 1) * 128].unsqueeze(1).to_broadcast([128, 4, 128]),
                    op0=ALU.mult, op1=ALU.add), reads=[bB2, cstB], writes=[t1B[i]])
                S.op("dve", lambda e, fc=fc, ls=ls, i=i: e.tensor_tensor(out=uT[:, fc, ls], in0=t1[:, i, :], in1=uT[:, fc, ls],
                                                                          op=ALU.mult),
                     reads=[t1B[i], uB[fc][tt]], writes=[uB[fc][tt]])

            LA = 5
            for i_ in range(LA):
                U(*steps[i_])
            for i_ in range(32):
                SG(*steps[i_])
                if i_ + LA < 32:
                    U(*steps[i_ + LA])
            for dc in range(8):
                wt, wB = wnext(f"Wo{dc}")
                wo = wt[:, 0:2048].rearrange("p (k c) -> p k c", k=16)
                for tt in range(2):
                    gs = slice(T0 + tt * 512, T0 + (tt + 1) * 512)
                    ls = slice(tt * 512, (tt + 1) * 512)
                    bk, bB = bank()
                    for k in range(16):
                        mm(bk[:, :], wo[:, k, :], uT[:, k, ls], k == 0, k == 15, [uB[k][tt], wB] + ([aliasB] if k < 8 else []), [bB])
                    hb = hB[dc][gs.start // 512]
                    S.op("dve", lambda e, bk=bk, dc=dc, gs=gs: e.tensor_tensor(out=hT[:, dc, gs], in0=bk[:, :], in1=hT[:, dc, gs], op=ALU.add),
                         reads=[bB, hb], writes=[hb])

    def ffn_phase(l):
        sb_off[0] = ARENA
        R = 3
        actT = alloc(f"actT{l}", [128, 11, SEQ], BF16)
        ag = alloc(f"ag{l}", [128, 2 + SEQ], F32)
        au = alloc(f"au{l}", [128, 2 + SEQ], F32)
        tg = alloc(f"tg{l}", [128, R, 512], F32)
        tu = alloc(f"tu{l}", [128, R, 512], F32)
        actB = [[Buf(f"act{j}_{t}") for t in range(4)] for j in range(11)]
        agB = [Buf(f"ag{i}") for i in range(4)]
        auB = [Buf(f"au{i}") for i in range(4)]
        zB = Buf("azero")
        tgB = [Buf(f"tg{i}") for i in range(R)]
        tuB = [Buf(f"tu{i}") for i in range(R)]
        S.op("dve", lambda e: e.memset(ag[:, 0:2], 0.0), writes=[zB])
        S.op("dve", lambda e: e.memset(au[:, 0:2], 0.0), writes=[zB])

        def cw(chunk, tap):
            o = C_CW + (l * 44 + chunk) * 3 + tap
            return cst[:, o:o + 1]

        def cb(chunk):
            o = C_CB + l * 44 + chunk
            return cst[:, o:o + 1]

        def evac(bk, bkB, a, aB, t, tB, i, tt, chunk):
            c0 = 2 + 512 * tt
            S.op("act", lambda e: e.activation(out=a[:, c0:c0 + 512], in_=bk[:, :], func=AF.Copy), reads=[bkB], writes=[aB[tt]])
            S.op("act", lambda e: e.activation(out=t[:, i, :], in_=bk[:, :], func=AF.Identity, scale=cw(chunk, 2), bias=cb(chunk)),
                 reads=[bkB, cstB], writes=[tB[i]])

        def tap(a, aB, t, tB, i, tt, chunk, tp):
            c0 = 2 + 512 * tt
            lo = c0 - (2 - tp)
            rd = [aB[tt], tB[i], cstB] + ([aB[tt - 1]] if tt > 0 else [zB])
            S.op("dve", lambda e: e.scalar_tensor_tensor(out=t[:, i, :], in0=a[:, lo:lo + 512], scalar=cw(chunk, tp), in1=t[:, i, :],
                                                         op0=ALU.mult, op1=ALU.add), reads=rd, writes=[tB[i]])

        cnt = 0
        for G in range(2):
            pend = None
            for j in range(11):
                jj = G * 11 + j
                wt, wB = wnext(f"Wup{l}_{jj}")
                wu = wt[:, 0:2048].rearrange("p (k b c) -> p k b c", k=8, b=2)
                for tt in range(4):
                    i = cnt % R
                    cnt += 1
                    bg, bgB = bank()
                    bu, buB = bank()
                    for k in range(8):
                        mm(bg[:, :], wu[:, k, 0, :], yT[:, k, tsl(tt)], k == 0, k == 7, [yB[k][tt], wB], [bgB])
                    for k in range(8):
                        mm(bu[:, :], wu[:, k, 1, :], yT[:, k, tsl(tt)], k == 0, k == 7, [yB[k][tt], wB], [buB])
                    evac(bg, bgB, ag, agB, tg, tgB, i, tt, jj)
                    evac(bu, buB, au, auB, tu, tuB, i, tt, 22 + jj)
                    if pend is not None:
                        pi_, pj, ptt = pend
                        S.op("act", lambda e, pi_=pi_: e.activation(out=tg[:, pi_, :], in_=tg[:, pi_, :], func=AF.Gelu_apprx_tanh),
                             reads=[tgB[pi_]], writes=[tgB[pi_]])
                    tap(ag, agB, tg, tgB, i, tt, jj, 1)
                    tap(au, auB, tu, tuB, i, tt, 22 + jj, 1)
                    tap(ag, agB, tg, tgB, i, tt, jj, 0)
                    tap(au, auB, tu, tuB, i, tt, 22 + jj, 0)
                    if pend is not None:
                        pi_, pj, ptt = pend
                        S.op("dve", lambda e, pi_=pi_, pj=pj, ptt=ptt: e.tensor_tensor(out=actT[:, pj, tsl(ptt)], in0=tg[:, pi_, :],
                                                                                      in1=tu[:, pi_, :], op=ALU.mult),
                             reads=[tgB[pi_], tuB[pi_]], writes=[actB[pj][ptt]] + ([aliasB] if pj < 4 else []))
                    pend = (i, j, tt)
            pi_, pj, ptt = pend
            S.op("act", lambda e, pi_=pi_: e.activation(out=tg[:, pi_, :], in_=tg[:, pi_, :], func=AF.Gelu_apprx_tanh),
                 reads=[tgB[pi_]], writes=[tgB[pi_]])
            S.op("dve", lambda e, pi_=pi_, pj=pj, ptt=ptt: e.tensor_tensor(out=actT[:, pj, tsl(ptt)], in0=tg[:, pi_, :], in1=tu[:, pi_, :],
                                                                          op=ALU.mult),
                 reads=[tgB[pi_], tuB[pi_]], writes=[actB[pj][ptt]])
            for dc in range(8):
                wt, wB = wnext(f"Wd{l}_{G}_{dc}")
                wd = wt[:, 0:1408].rearrange("p (k c) -> p k c", k=11)
                for tt in range(4):
                    bk, bB = bank()
                    for k in range(11):
                        mm(bk[:, :], wd[:, k, :], actT[:, k, tsl(tt)], k == 0, k == 10, [actB[k][tt], wB] + ([aliasB] if k < 4 else []), [bB])
                    S.op("dve", lambda e, bk=bk, dc=dc, tt=tt: e.tensor_tensor(out=hT[:, dc, tsl(tt)], in0=bk[:, :], in1=hT[:, dc, tsl(tt)],
                                                                              op=ALU.add),
                         reads=[bB, hB[dc][tt]], writes=[hB[dc][tt]])

    def attn_phase():
        sb_off[0] = ARENA
        attnT = alloc("attnT", [128, 4, SEQ], BF16)
        cosT = alloc("cosT", [128, SEQ], F32)
        sinT = alloc("sinT", [128, SEQ], F32)
        qT = [alloc(f"qT{s}", [128, SEQ], BF16) for s in range(2)]
        kT = [alloc(f"kT{s}", [128, SEQ], BF16) for s in range(2)]
        vh = [alloc(f"vh{s}", [128, 16, 128], BF16) for s in range(2)]
        nm = [alloc(f"nm{s}", [128, 16, 8], BF16) for s in range(2)]
        pT = [alloc(f"pT{i}", [128, 2, 512], BF16) for i in range(2)]
        r1 = alloc("r1", [128, 2, 512], F32)
        r2 = alloc("r2", [128, 1, 512], F32)
        r2b = nc.alloc_sbuf_tensor_at("r2b", [128, 512], F32, offset=ARENA - 4096)
        r2s = [r2[:, 0, :], r2b[:, :]]
        qb16 = alloc("qb16", [128, 2, 512], BF16)
        qb16x = nc.alloc_sbuf_tensor_at("qb16x", [128, 2, 512], BF16, offset=ARENA - 12288)
        qbs = [qb16[:, 0, :], qb16[:, 1, :], qb16x[:, 0, :], qb16x[:, 1, :]]
        qb16B = [Buf(f"qb16_{i}") for i in range(4)]
        gm = alloc("gm", [128, 16, 8], F32)
        cmpt = alloc("cmpt", [128, 8, 8, 8], BF16)
        rank = alloc("rank", [128, 16, 8], F32)
        elig = alloc("elig", [128, 16, 8], BF16)
        negb = alloc("negb", [128, 16, 8], F32)
        km32 = alloc("km32", [128, 2, 8], F32)
        kmb = alloc("kmb", [128, 2, 8], BF16)
        rec = r2[:, 0, :]
        cm = alloc("cm", [128, 2, 256], BF16)
        negcm = alloc("negcm", [128, 2, 256], BF16)
        negcmB = Buf("negcm")
        ktmp = nc.alloc_sbuf_tensor_at("ktmp", [128, SEQ], I32, offset=(ARENA + 16384 + 16384 + 63) // 64 * 64)
        kf = nc.alloc_sbuf_tensor_at("kf", [128, SEQ], F32, offset=(ARENA + 16384 + 16384 + 8192 + 63) // 64 * 64)
        tpos = nc.alloc_sbuf_tensor_at("tpos", [128, SEQ], F32, offset=(ARENA + 16384 + 16384 + 16384 + 63) // 64 * 64)
        assert ARENA + 16384 * 3 + 8192 <= sb_off[0], "overlay scratch must stay inside arena"

        cosB, sinB, cmB, eligB, negbB = Buf("cos"), Buf("sin"), Buf("cm"), Buf("elig"), Buf("negb")
        attnB = [[Buf(f"at{k}_{q}") for q in range(4)] for k in range(4)]
        qB = [[Buf(f"q{s}_{t}") for t in range(4)] for s in range(2)]
        kB = [[Buf(f"k{s}_{t}") for t in range(4)] for s in range(2)]
        vhB = [[Buf(f"vh{s}_{t}") for t in range(4)] for s in range(2)]
        nmB = [Buf("nm0"), Buf("nm1")]
        pTB = [Buf(f"pT{i}") for i in range(2)]
        r1B = [Buf("r1_0"), Buf("r1_1")]
        r2B = [Buf("r2_0"), Buf("r2_1")]
        gmB, cmpB, rankB = Buf("gm"), Buf("cmp"), Buf("rank")
        km32B = [Buf("km32_0"), Buf("km32_1")]
        kmbB = [Buf("kmb_0"), Buf("kmb_1")]
        recB = r2B[0]
        scr = Buf("scr")

        S.op("sp", lambda e: e.dma_start(out=sinT[:, :], in_=tab_d[0]), reads=[tabdB] + hB[7], writes=[sinB], dma_to=sinB)
        S.op("sp", lambda e: e.dma_start(out=cosT[:, :], in_=tab_d[1]), reads=[tabdB] + hB[7], writes=[cosB], dma_to=cosB)
        S.op("pool", lambda e: e.iota(kf[:, 0:512].rearrange("p (j i) -> p j i", j=2), pattern=[[-128, 2], [1, 256]], base=0,
                                      channel_multiplier=-1, allow_small_or_imprecise_dtypes=True), reads=[scr] + hB[7], writes=[scr])
        S.op("dve", lambda e: e.tensor_single_scalar(out=cm[:, :, :], in_=kf[:, 0:512].rearrange("p (j i) -> p j i", j=2),
                                                     scalar=0.0, op=ALU.is_ge), reads=[scr], writes=[cmB])
        S.op("dve", lambda e: e.tensor_scalar(out=negcm[:, :, :], in0=cm[:, :, :], scalar1=-1.0, scalar2=NEGM, op0=ALU.add, op1=ALU.mult),
             reads=[cmB], writes=[negcmB])
        S.op("pool", lambda e: e.iota(kf[:, 512:640].rearrange("p (a b) -> p a b", a=16), pattern=[[1, 16], [-2, 8]], base=-2,
                                      channel_multiplier=0, allow_small_or_imprecise_dtypes=True), reads=[scr] + hB[7], writes=[scr])
        S.op("dve", lambda e: e.tensor_single_scalar(out=elig[:, :, :], in_=kf[:, 512:640].rearrange("p (a b) -> p a b", a=16),
                                                     scalar=0.0, op=ALU.is_ge), reads=[scr], writes=[eligB])
        S.op("dve", lambda e: e.tensor_scalar(out=negb[:, :, :], in0=elig[:, :, :], scalar1=-1.0, scalar2=1e30, op0=ALU.add, op1=ALU.mult),
             reads=[eligB], writes=[negbB])
        fenceB = Buf("fence")
        S.op("dve", lambda e: e.memset(km32[:, 0, 0:1], 0.0), reads=[scr, cosB, sinB, cmB, negcmB, eligB, negbB], writes=[fenceB, scr])
        yield

        def proj(h):
            s = h % 2
            wt, wB = wnext(f"Wqk{h}")
            wqk = wt[:, 0:2048].rearrange("p (k b c) -> p k b c", k=8, b=2)
            tiles = [(w_, dst, dstB, tt) for (w_, dst, dstB) in ((0, qT[s], qB[s]), (1, kT[s], kB[s])) for tt in range(4)]
            pend = None

            def finish_tile(p):
                n, ba, baB, dst, dstB, tt = p
                i = n % 2
                bs, bsB = bank("proj4")
                qi = n % 4
                mm(bs[:, :], permT[:, :], qbs[qi], True, True, [permB, qb16B[qi]], [bsB])
                S.op("dve", lambda e: e.tensor_tensor(out=r1[:, i, :], in0=ba[:, :], in1=cosT[:, tsl(tt)], op=ALU.mult),
                     reads=[baB, cosB, qb16B[qi]], writes=[r1B[i]])
                S.op("dve", lambda e: e.tensor_tensor(out=r2s[i], in0=bs[:, :], in1=sinT[:, tsl(tt)], op=ALU.mult),
                     reads=[bsB, sinB], writes=[r2B[i]])
                S.op("pool", lambda e: e.tensor_tensor(out=dst[:, tsl(tt)], in0=r1[:, i, :], in1=r2s[i], op=ALU.add),
                     reads=[r1B[i], r2B[i], fenceB], writes=[dstB[tt]])

            for n, (w_, dst, dstB, tt) in enumerate(tiles):
                i = n % 2
                ba, baB = bank("proj4")
                for k in range(8):
                    mm(ba[:, :], wqk[:, k, w_, :], yT[:, k, tsl(tt)], k == 0, k == 7, [yB[k][tt], wB], [baB])
                S.op("act", lambda e, ba=ba, n=n: e.activation(out=qbs[n % 4], in_=ba[:, :], func=AF.Copy),
                     reads=[baB], writes=[qb16B[n % 4]])
                if pend is not None:
                    finish_tile(pend)
                pend = (n, ba, baB, dst, dstB, tt)
            finish_tile(pend)
            wt2, wB2 = wnext(f"Wvh{h}")
            wvh = wt2[:, 0:1024].rearrange("p (k c) -> p k c", k=8)
            for tq in range(4):
                bk, bB = bank("proj")
                for c4 in range(4):
                    tch = tq * 4 + c4
                    for k in range(8):
                        mm(bk[:, c4 * 128:(c4 + 1) * 128], yT[:, k, tch * 128:(tch + 1) * 128], wvh[:, k, :], k == 0, k == 7,
                           [yB[k][tq], wB2], [bB])
                S.op("act", lambda e, bk=bk, tq=tq, s=s: e.activation(
                    out=vh[s][:, tq * 4:(tq + 1) * 4, :].rearrange("p a c -> p (a c)"), in_=bk[:, :], func=AF.Copy),
                    reads=[bB, fenceB], writes=[vhB[s][tq]])
            S.op("dve", lambda e, s=s: e.reduce_sum(out=km32[:, s, :], in_=kT[s][:, :].rearrange("p (n t) -> p n t", n=8), axis=AX.X),
                 reads=kB[s], writes=[km32B[s]])
            S.op("dve", lambda e, s=s: e.tensor_scalar(out=kmb[:, s, :], in0=km32[:, s, :], scalar1=1.0 / 256, scalar2=None, op0=ALU.mult),
                 reads=[km32B[s]], writes=[kmbB[s]])
            pending_gates.append(h)

        def gate(h):
            s = h % 2
            b0 = spair()
            bk, bB = pbank[b0], pbB[b0]
            for qt in range(16):
                mm(bk[:, qt * 8:(qt + 1) * 8], qT[s][:, qt * 128:(qt + 1) * 128], kmb[:, s, :], True, True, [qB[s][qt // 4], kmbB[s]], [bB])
            S.op("dve", lambda e, bk=bk: e.tensor_tensor(out=gm[:, :, :], in0=bk[:, 0:128].rearrange("p (a b) -> p a b", a=16),
                                                        in1=negb[:, :, :], op=ALU.add), reads=[bB, negbB], writes=[gmB])
            for hf in range(2):
                hsl = slice(hf * 8, (hf + 1) * 8)
                S.op("dve", lambda e, hsl=hsl: e.tensor_tensor(out=cmpt[:, :, :, :], in0=gm[:, hsl, :].unsqueeze(2).to_broadcast([128, 8, 8, 8]),
                                                               in1=gm[:, hsl, :].unsqueeze(3).to_broadcast([128, 8, 8, 8]), op=ALU.is_gt),
                     reads=[gmB], writes=[cmpB])
                S.op("dve", lambda e, hsl=hsl: e.reduce_sum(out=rank[:, hsl, :], in_=cmpt[:, :, :, :], axis=AX.X), reads=[cmpB], writes=[rankB])
            S.op("dve", lambda e: e.scalar_tensor_tensor(out=rank[:, :, :], in0=rank[:, :, :], scalar=3.0, in1=elig[:, :, :],
                                                         op0=ALU.is_lt, op1=ALU.mult), reads=[rankB, eligB], writes=[rankB])
            S.op("dve", lambda e, s=s: e.tensor_scalar(out=nm[s][:, :, :], in0=rank[:, :, :], scalar1=-1.0, scalar2=NEGM,
                                                      op0=ALU.add, op1=ALU.mult), reads=[rankB], writes=[nmB[s]])

        pti = [0]
        pending_gates = []
        scale = float(128 ** -0.5)

        spi = [0]

        def spair():
            b0 = (0, 2)[spi[0] % 2]
            spi[0] += 1
            return b0

        def attend(h):
            s = h % 2
            hh = h % 4
            items = []
            for t in range(4):
                nblk = 2 * t + 2
                for n in range(nblk):
                    items.append(dict(t=t, n=n, first=(n == 0), last=(n == nblk - 1)))
            banks = {}

            def front(it):
                t, n = it["t"], it["n"]
                c0 = 256 if n == 2 * t + 1 else 0
                W = 512 - c0
                b0 = spair()
                masks = []
                for qh in range(c0 // 128, 4):
                    qb_q = 2 * t + qh // 2
                    if n < qb_q and qb_q >= 4:
                        masks.append(qh)
                own = n >= 2 * t
                hs = slice((n - 2 * t) * 256, (n - 2 * t) * 256 + 256) if own else None
                for j in range(2):
                    kt = 2 * n + j
                    sp_, spB = pbank[b0 + j], pbB[b0 + j]
                    extra = []
                    for qh in masks:
                        qt = 4 * t + qh
                        extra.append((sp_[:, qh * 128:(qh + 1) * 128], nm[s][:, qt, n:n + 1].to_broadcast([128, 128]), ident[:, :],
                                      [nmB[s], identB]))
                    if own:
                        extra.append((sp_[:, hs], ident[:, :], negcm[:, j, :], [identB, negcmB]))
                    mm(sp_[:, c0:512], kT[s][:, kt * 128:(kt + 1) * 128], qT[s][:, t * 512 + c0:(t + 1) * 512], True, len(extra) == 0,
                       [kB[s][kt // 4], qB[s][t]], [spB])
                    for mi, (o_, l_, r_, rd_) in enumerate(extra):
                        mm(o_, l_, r_, False, mi == len(extra) - 1, rd_, [spB])
                pi = pti[0] % 2
                pti[0] += 1
                src = pball[:, b0 * 512:(b0 + 2) * 512].rearrange("p (j c) -> p j c", j=2)
                S.op("act", lambda e: e.activation(out=pT[pi][:, :, c0:512], in_=src[:, :, c0:512], func=AF.Exp, scale=scale),
                     reads=[pbB[b0], pbB[b0 + 1]], writes=[pTB[pi]])
                it["pi"] = pi
                it["c0"] = c0

            def back(it):
                t, n, pi, c0 = it["t"], it["n"], it["pi"], it["c0"]
                if it["first"]:
                    banks[t] = (bank("pv"), bank("dn"))
                (pv, pvB), (dn, dnB) = banks[t]
                for j in range(2):
                    kt = 2 * n + j
                    fst = it["first"] and j == 0
                    lst = it["last"] and j == 1
                    mm(pv[:, c0:512], vh[s][:, kt, :], pT[pi][:, j, c0:512], fst, lst, [vhB[s][kt // 4], pTB[pi]], [pvB], skip=True)
                    mm(dn[:, c0:512], ones[:, :], pT[pi][:, j, c0:512], fst, lst, [onesB, pTB[pi]], [dnB], skip=True)
                if it["last"]:
                    qs = slice(t * 512, (t + 1) * 512)
                    S.op("dve", lambda e: e.reciprocal(out=rec, in_=dn[:, :]), reads=[dnB], writes=[recB])
                    S.op("dve", lambda e: e.tensor_tensor(out=attnT[:, hh, qs], in0=pv[:, :], in1=rec, op=ALU.mult),
                         reads=[pvB, recB], writes=[attnB[hh][t], aliasB])

            front(items[0])
            for i, it in enumerate(items):
                if i + 1 < len(items):
                    front(items[i + 1])
                back(it)
                if it["last"] and it["t"] == 0:
                    for g in [g for g in pending_gates if g <= h]:
                        pending_gates.remove(g)
                        gate(g)
                if it["last"] and it["t"] == 2:
                    while pending_gates:
                        gate(pending_gates.pop(0))

        def wo(hg):
            for dc in range(8):
                wt, wB = wnext(f"Wob{hg}_{dc}")
                wo_ = wt[:, 0:512].rearrange("p (k c) -> p k c", k=4)
                for tt in range(4):
                    bk, bB = bank("proj")
                    for k in range(4):
                        mm(bk[:, :], wo_[:, k, :], attnT[:, k, tsl(tt)], k == 0, k == 3, [attnB[k][tt], wB, aliasB], [bB])
                    S.op("dve", lambda e, bk=bk, dc=dc, tt=tt: e.tensor_tensor(out=hT[:, dc, tsl(tt)], in0=bk[:, :], in1=hT[:, dc, tsl(tt)],
                                                                              op=ALU.add),
                         reads=[bB, hB[dc][tt]], writes=[hB[dc][tt]])

        for it in attn_order():
            k = int(it[1])
            if it[0] == "p":
                proj(k)
            elif it[0] == "a":
                attend(k)
            else:
                wo(k)

    def finish():
        if stop_after != "final":
            for c in range(8):
                S.op("sp", lambda e, c=c: e.dma_start(out=out_d[c], in_=hT[:, c, :]), reads=hB[c], dma_to=outB)
        S.emit(nc, final_waits=[outB])
        return nc

    norm_phase(0, nobar=True)
    gmlp_phase()
    if stop_after == "gmlp":
        return finish()
    norm_phase(2, nobar=True)
    ffn_phase(0)
    if stop_after == "ffn0":
        return finish()
    attn_gen = attn_phase()
    norm_phase(1, pre=lambda: next(attn_gen), nobar=True)
    next(attn_gen, None)
    if stop_after == "attn":
        return finish()
    norm_phase(3, nobar=True)
    ffn_phase(1)
    if stop_after == "ffn1":
        return finish()
    norm_phase(4, final=True, nobar=True)
    return finish()


def _kpc(w, nk):
    C = w.shape[1]
    return np.ascontiguousarray(w.reshape(nk, 128, C).transpose(1, 0, 2))


def pack_weights(a_w_in, a_w_out, b_w_qkv, b_w_o, ffn_w_up, ffn_w_down):
    offs, WTOT = slab_offsets()
    wts = np.empty((128, WTOT), np.float32)

    def put(name, arr):
        o, n = offs[name]
        wts[:, o:o + n] = arr.reshape(128, n)

    w_in = a_w_in[0]
    for vb in range(4):
        put(f"Wv{vb}", _kpc(w_in[:, 2048 + vb * 512:2048 + (vb + 1) * 512], 8))
    for fc in range(16):
        put(f"Wu{fc}", _kpc(w_in[:, fc * 128:(fc + 1) * 128], 8))
    w_out = a_w_out[0]
    for dc in range(8):
        put(f"Wo{dc}", _kpc(w_out[:, dc * 128:(dc + 1) * 128], 16))
    for l in range(2):
        wu = ffn_w_up[l]
        for jj in range(22):
            g = _kpc(wu[:, jj * 128:(jj + 1) * 128], 8)
            u = _kpc(wu[:, DFF + jj * 128:DFF + (jj + 1) * 128], 8)
            put(f"Wup{l}_{jj}", np.stack([g, u], axis=2))
        wd = ffn_w_down[l]
        for G in range(2):
            for dc in range(8):
                put(f"Wd{l}_{G}_{dc}", _kpc(wd[G * 1408:(G + 1) * 1408, dc * 128:(dc + 1) * 128], 11))
    wqkv = b_w_qkv[0]
    perm = (np.arange(128) + 64) % 128
    for h in range(8):
        q = wqkv[:, h * 128:(h + 1) * 128]
        k = wqkv[:, 1024 + h * 128:1024 + (h + 1) * 128]
        v = wqkv[:, 2048 + h * 128:2048 + (h + 1) * 128]
        put(f"Wqk{h}", np.stack([_kpc(q, 8), _kpc(k, 8)], axis=2))
        put(f"Wvh{h}", _kpc(v, 8))
    w_o = b_w_o[0]
    for hg in range(2):
        for dc in range(8):
            put(f"Wob{hg}_{dc}", _kpc(w_o[hg * 512:(hg + 1) * 512, dc * 128:(dc + 1) * 128], 4))
    return wts


def pack_consts(mix_norm, ffn_norm, final_norm, a_v_gain, a_w_s, a_b_s, ffn_conv_w, ffn_conv_b):
    c = np.zeros((128, NCONST), np.float32)
    gains = [mix_norm[0], mix_norm[1], ffn_norm[0], ffn_norm[1], final_norm]
    for i, g in enumerate(gains):
        c[:, C_GAIN + i * 8:C_GAIN + (i + 1) * 8] = g.reshape(8, 128).T
    c[:, C_VG:C_VG + 16] = a_v_gain[0].reshape(16, 128).T
    for l in range(2):
        cw = ffn_conv_w[l].reshape(3, 44, 128)
        c[:, C_CW + l * 132:C_CW + (l + 1) * 132] = cw.transpose(2, 1, 0).reshape(128, 132)
        c[:, C_CB + l * 44:C_CB + (l + 1) * 44] = ffn_conv_b[l].reshape(44, 128).T
    ws = a_w_s[0]
    c[:, C_WS:C_WS + 1024] = ws.transpose(2, 0, 1).reshape(128, 1024)
    c[:, C_BB:C_BB + 1024] = np.broadcast_to(a_b_s[0].reshape(1, 1024), (128, 1024))
    return c


_NC_CACHE = {}


def kernel(x, mix_norm, a_w_in, a_v_gain, a_w_s, a_b_s, a_w_out, b_w_qkv, b_w_o,
           ffn_norm, ffn_w_up, ffn_conv_w, ffn_conv_b, ffn_w_down, final_norm, _stop_after="final", _cores=8):
    f = lambda a: np.asarray(a, dtype=np.float32)
    x = f(x)
    wts = pack_weights(f(a_w_in), f(a_w_out), f(b_w_qkv), f(b_w_o), f(ffn_w_up), f(ffn_w_down))
    consts = pack_consts(f(mix_norm), f(ffn_norm), f(final_norm), f(a_v_gain), f(a_w_s), f(a_b_s), f(ffn_conv_w), f(ffn_conv_b))
    nc = build_nc(_stop_after)
    in_maps = []
    for b in range(_cores):
        xT = np.ascontiguousarray(x[b].T).reshape(8, 128, SEQ)
        in_maps.append({"xT": xT, "consts": consts, "wts": wts})
    res = run_bass_kernel_spmd(nc, in_maps, core_ids=list(range(_cores)))
    outs = []
    for b in range(_cores):
        oT = res.results[b]["outT"].reshape(D, SEQ)
        outs.append(np.ascontiguousarray(oT.T))
    return np.stack(outs, axis=0).astype(np.float32)
```
